# Optimizing a Trainium2 kernel written in Bass

```python
import math
import jax, jax.numpy as jnp
from jax import lax
import numpy as np

D_MODEL = 2048
BATCH = 4
SEQ = 2048
DEPTH = 1
DEC_BATCH = 128
DEC_SEQ = 1
PAST_LEN = 16384
PAGE_SIZE = 128

D_MIX = 2 * D_MODEL
D_CONV_GRP = D_MIX // 4
D_SSM = D_MIX - D_CONV_GRP
HEAD_DIM = 64
N_SSM_HEADS = D_SSM // HEAD_DIM
N_SSM_GROUPS = 8
HEADS_PER_GROUP = N_SSM_HEADS // N_SSM_GROUPS
D_STATE = 128
SSM_CONV_W = 4
CONV_MOD_W = 31
CHUNK = 128
D_XBC = D_SSM + 2 * N_SSM_GROUPS * D_STATE
D_IN_PROJ = 2 * D_CONV_GRP + D_SSM + D_XBC + N_SSM_HEADS
D_FF = 5632
PLE_DIM = 256
EPS = 1e-6

kernel_name = "hymba_conformer_ssd_macaron_step"


def rmsnorm(x, g):
    xf = x.astype(jnp.float32)
    y = xf * lax.rsqrt(jnp.mean(xf * xf, axis=-1, keepdims=True) + EPS)
    return (y * g.astype(jnp.float32)).astype(x.dtype)


def layernorm(x, g, b):
    xf = x.astype(jnp.float32)
    mu = jnp.mean(xf, axis=-1, keepdims=True)
    xc = xf - mu
    y = xc * lax.rsqrt(jnp.mean(xc * xc, axis=-1, keepdims=True) + EPS)
    return (y * g.astype(jnp.float32) + b.astype(jnp.float32)).astype(x.dtype)


def swiglu(x, wg, wu, wd):
    return (jax.nn.silu(x @ wg) * (x @ wu)) @ wd


def causal_dwconv(x, buf, w, b):
    k = w.shape[0]
    xp = jnp.concatenate([buf.astype(x.dtype), x], axis=1)
    out = lax.conv_general_dilated(
        xp, w.astype(x.dtype)[:, None, :], window_strides=(1,), padding='VALID',
        dimension_numbers=('NWC', 'WIO', 'NWC'), feature_group_count=x.shape[-1])
    new_buf = xp[:, xp.shape[1] - (k - 1):]
    return out + b.astype(x.dtype), new_buf


def ssd(x, dt, a, bm, cm, d_skip, h0):
    b_, l = x.shape[0], x.shape[1]
    q = min(CHUNK, l)
    pad = (-l) % q
    if pad:
        padw = lambda t: jnp.pad(t, [(0, 0), (0, pad)] + [(0, 0)] * (t.ndim - 2))
        x, dt, bm, cm = padw(x), padw(dt), padw(bm), padw(cm)
    c = (l + pad) // q
    xc = x.reshape(b_, c, q, N_SSM_GROUPS, HEADS_PER_GROUP, HEAD_DIM)
    dtc = dt.reshape(b_, c, q, N_SSM_GROUPS, HEADS_PER_GROUP)
    bc = bm.reshape(b_, c, q, N_SSM_GROUPS, D_STATE)
    cc = cm.reshape(b_, c, q, N_SSM_GROUPS, D_STATE)
    a_cs = jnp.cumsum(dtc * a, axis=2)
    xdt = xc * dtc[..., None]
    idx = jnp.arange(q)
    mask = (idx[:, None] >= idx[None, :])[None, None, :, :, None, None]
    seg = a_cs[:, :, :, None] - a_cs[:, :, None, :]
    decay_ls = jnp.where(mask, jnp.exp(jnp.where(mask, seg, 0.0)), 0.0)
    cb = jnp.einsum('bclgn,bcsgn->bclsg', cc, bc)
    y_diag = jnp.einsum('bclsg,bclsgr,bcsgrp->bclgrp', cb, decay_ls, xdt)
    decay_to_end = jnp.exp(a_cs[:, :, -1:] - a_cs)
    chunk_states = jnp.einsum('bcsgn,bcsgr,bcsgrp->bcgrpn', bc, decay_to_end, xdt)
    chunk_decay = jnp.exp(a_cs[:, :, -1])

    def step(h, inp):
        dec, s = inp
        return h * dec[..., None, None] + s, h

    h_final, h_prev = lax.scan(step, h0, (jnp.moveaxis(chunk_decay, 1, 0),
                                          jnp.moveaxis(chunk_states, 1, 0)))
    h_prev = jnp.moveaxis(h_prev, 0, 1)
    y_off = jnp.einsum('bclgn,bcgrpn,bclgr->bclgrp', cc, h_prev, jnp.exp(a_cs))
    y = y_diag + y_off + d_skip[..., None] * xc
    y = y.reshape(b_, c * q, N_SSM_GROUPS, HEADS_PER_GROUP, HEAD_DIM)[:, :l]
    return y, h_final


def token_mixer(u, conv_buf, xbc_buf, h0, w_in, conv_mod_w, conv_mod_b, conv_mod_ln_g,
                conv_mod_ln_b, ssm_conv_w, ssm_conv_b, dt_bias, a_log, d_skip, ssm_norm_g, w_out):
    f32 = jnp.float32
    b_, l, _ = u.shape
    proj = u @ w_in
    s1 = D_CONV_GRP
    s2 = 2 * D_CONV_GRP
    s3 = s2 + D_SSM
    s4 = s3 + D_XBC
    glu_a, glu_b, z, xbc, dt_raw = jnp.split(proj, [s1, s2, s3, s4], axis=-1)
    v = glu_a * jax.nn.sigmoid(glu_b)
    v, new_conv_buf = causal_dwconv(v, conv_buf, conv_mod_w, conv_mod_b)
    v = jax.nn.silu(layernorm(v, conv_mod_ln_g, conv_mod_ln_b))
    xbc, new_xbc_buf = causal_dwconv(xbc, xbc_buf, ssm_conv_w, ssm_conv_b)
    xbc = jax.nn.silu(xbc)
    xs, bm, cm = jnp.split(xbc, [D_SSM, D_SSM + N_SSM_GROUPS * D_STATE], axis=-1)
    xs = xs.astype(f32).reshape(b_, l, N_SSM_GROUPS, HEADS_PER_GROUP, HEAD_DIM)
    bm = bm.astype(f32).reshape(b_, l, N_SSM_GROUPS, D_STATE)
    cm = cm.astype(f32).reshape(b_, l, N_SSM_GROUPS, D_STATE)
    dt = jax.nn.softplus(dt_raw.astype(f32) + dt_bias.astype(f32)).reshape(
        b_, l, N_SSM_GROUPS, HEADS_PER_GROUP)
    a = -jnp.exp(a_log.astype(f32)).reshape(N_SSM_GROUPS, HEADS_PER_GROUP)
    h0r = h0.astype(f32).reshape(b_, N_SSM_GROUPS, HEADS_PER_GROUP, HEAD_DIM, D_STATE)
    y, h_new = ssd(xs, dt, a, bm, cm,
                   d_skip.astype(f32).reshape(N_SSM_GROUPS, HEADS_PER_GROUP), h0r)
    yg = y.reshape(b_, l, D_SSM) * jax.nn.silu(z.astype(f32))
    yg = yg.reshape(b_, l, N_SSM_GROUPS, D_SSM // N_SSM_GROUPS)
    yg = yg * lax.rsqrt(jnp.mean(yg * yg, axis=-1, keepdims=True) + EPS)
    yg = (yg.reshape(b_, l, D_SSM) * ssm_norm_g.astype(f32)).astype(u.dtype)
    out = jnp.concatenate([v, yg], axis=-1) @ w_out
    h_new = h_new.reshape(b_, N_SSM_HEADS, HEAD_DIM, D_STATE)
    return out, new_conv_buf, new_xbc_buf, h_new


def decoder_layer(h, p, conv_buf, xbc_buf, h0,
                  norm_ffn1_pre, w_ffn1_gate, w_ffn1_up, w_ffn1_down, norm_ffn1_post,
                  norm_mix_pre, w_in, conv_mod_w, conv_mod_b, conv_mod_ln_g, conv_mod_ln_b,
                  ssm_conv_w, ssm_conv_b, dt_bias, a_log, d_skip, ssm_norm_g, w_out, norm_mix_post,
                  norm_ffn2_pre, w_ffn2_gate, w_ffn2_up, w_ffn2_down, norm_ffn2_post,
                  norm_ple_pre, w_ple_gate, w_ple_proj, norm_ple_post):
    h = h + 0.5 * rmsnorm(swiglu(rmsnorm(h, norm_ffn1_pre), w_ffn1_gate, w_ffn1_up, w_ffn1_down),
                          norm_ffn1_post)
    mix, cb, xb, hs = token_mixer(rmsnorm(h, norm_mix_pre), conv_buf, xbc_buf, h0, w_in,
                                  conv_mod_w, conv_mod_b, conv_mod_ln_g, conv_mod_ln_b,
                                  ssm_conv_w, ssm_conv_b, dt_bias, a_log, d_skip, ssm_norm_g, w_out)
    h = h + rmsnorm(mix, norm_mix_post)
    h = h + 0.5 * rmsnorm(swiglu(rmsnorm(h, norm_ffn2_pre), w_ffn2_gate, w_ffn2_up, w_ffn2_down),
                          norm_ffn2_post)
    gate = jax.nn.sigmoid(rmsnorm(h, norm_ple_pre) @ w_ple_gate)
    h = h + rmsnorm(gate * (p.astype(h.dtype) @ w_ple_proj), norm_ple_post)
    return h, cb, xb, hs


def setup_inputs(seed: int = 0) -> dict:
    key = jax.random.key(seed)
    ks = iter(jax.random.split(key, 48))
    f32 = jnp.float32

    def nrm(shape, scale):
        return scale * jax.random.normal(next(ks), shape, f32)

    def gain(shape):
        return 1.0 + 0.05 * jax.random.normal(next(ks), shape, f32)

    d = {}
    d['x_prompt'] = nrm((BATCH, SEQ, D_MODEL), 1.0)
    d['x_sample'] = nrm((DEC_BATCH, DEC_SEQ, D_MODEL), 1.0)
    d['state_conv_mod'] = nrm((DEPTH, DEC_BATCH, CONV_MOD_W - 1, D_CONV_GRP), 0.5)
    d['state_ssm_conv'] = nrm((DEPTH, DEC_BATCH, SSM_CONV_W - 1, D_XBC), 1.0)
    d['state_ssm'] = nrm((DEPTH, DEC_BATCH, N_SSM_HEADS, HEAD_DIM, D_STATE), 0.1)
    d['p_prompt'] = nrm((DEPTH, BATCH, SEQ, PLE_DIM), 1.0)
    d['p_sample'] = nrm((DEPTH, DEC_BATCH, DEC_SEQ, PLE_DIM), 1.0)
    d['norm_ffn1_pre'] = gain((DEPTH, D_MODEL))
    d['w_ffn1_gate'] = nrm((DEPTH, D_MODEL, D_FF), D_MODEL ** -0.5)
    d['w_ffn1_up'] = nrm((DEPTH, D_MODEL, D_FF), D_MODEL ** -0.5)
    d['w_ffn1_down'] = nrm((DEPTH, D_FF, D_MODEL), D_FF ** -0.5)
    d['norm_ffn1_post'] = gain((DEPTH, D_MODEL))
    d['norm_mix_pre'] = gain((DEPTH, D_MODEL))
    d['w_in'] = nrm((DEPTH, D_MODEL, D_IN_PROJ), D_MODEL ** -0.5)
    d['conv_mod_w'] = nrm((DEPTH, CONV_MOD_W, D_CONV_GRP), CONV_MOD_W ** -0.5)
    d['conv_mod_b'] = nrm((DEPTH, D_CONV_GRP), 0.02)
    d['conv_mod_ln_g'] = gain((DEPTH, D_CONV_GRP))
    d['conv_mod_ln_b'] = nrm((DEPTH, D_CONV_GRP), 0.02)
    d['ssm_conv_w'] = nrm((DEPTH, SSM_CONV_W, D_XBC), SSM_CONV_W ** -0.5)
    d['ssm_conv_b'] = nrm((DEPTH, D_XBC), 0.02)
    dt0 = jnp.exp(jax.random.uniform(next(ks), (DEPTH, N_SSM_HEADS), f32,
                                     minval=math.log(1e-3), maxval=math.log(1e-1)))
    d['dt_bias'] = dt0 + jnp.log(-jnp.expm1(-dt0))
    d['a_log'] = jnp.log(jax.random.uniform(next(ks), (DEPTH, N_SSM_HEADS), f32,
                                            minval=1.0, maxval=16.0))
    d['d_skip'] = gain((DEPTH, N_SSM_HEADS))
    d['ssm_norm_g'] = gain((DEPTH, D_SSM))
    d['w_out'] = nrm((DEPTH, D_MIX, D_MODEL), D_MIX ** -0.5)
    d['norm_mix_post'] = gain((DEPTH, D_MODEL))
    d['norm_ffn2_pre'] = gain((DEPTH, D_MODEL))
    d['w_ffn2_gate'] = nrm((DEPTH, D_MODEL, D_FF), D_MODEL ** -0.5)
    d['w_ffn2_up'] = nrm((DEPTH, D_MODEL, D_FF), D_MODEL ** -0.5)
    d['w_ffn2_down'] = nrm((DEPTH, D_FF, D_MODEL), D_FF ** -0.5)
    d['norm_ffn2_post'] = gain((DEPTH, D_MODEL))
    d['norm_ple_pre'] = gain((DEPTH, D_MODEL))
    d['w_ple_gate'] = nrm((DEPTH, D_MODEL, D_MODEL), D_MODEL ** -0.5)
    d['w_ple_proj'] = nrm((DEPTH, PLE_DIM, D_MODEL), PLE_DIM ** -0.5)
    d['norm_ple_post'] = gain((DEPTH, D_MODEL))
    return d


def reference(x_prompt, x_sample, state_conv_mod, state_ssm_conv, state_ssm, p_prompt, p_sample,
              norm_ffn1_pre, w_ffn1_gate, w_ffn1_up, w_ffn1_down, norm_ffn1_post,
              norm_mix_pre, w_in, conv_mod_w, conv_mod_b, conv_mod_ln_g, conv_mod_ln_b,
              ssm_conv_w, ssm_conv_b, dt_bias, a_log, d_skip, ssm_norm_g, w_out, norm_mix_post,
              norm_ffn2_pre, w_ffn2_gate, w_ffn2_up, w_ffn2_down, norm_ffn2_post,
              norm_ple_pre, w_ple_gate, w_ple_proj, norm_ple_post):
    hp, hs = x_prompt, x_sample
    bp = x_prompt.shape[0]
    cp_l, xp_l, sp_l, cs_l, xs_l, ss_l = [], [], [], [], [], []
    for i in range(DEPTH):
        lw = (norm_ffn1_pre[i], w_ffn1_gate[i], w_ffn1_up[i], w_ffn1_down[i], norm_ffn1_post[i],
              norm_mix_pre[i], w_in[i], conv_mod_w[i], conv_mod_b[i], conv_mod_ln_g[i],
              conv_mod_ln_b[i], ssm_conv_w[i], ssm_conv_b[i], dt_bias[i], a_log[i], d_skip[i],
              ssm_norm_g[i], w_out[i], norm_mix_post[i],
              norm_ffn2_pre[i], w_ffn2_gate[i], w_ffn2_up[i], w_ffn2_down[i], norm_ffn2_post[i],
              norm_ple_pre[i], w_ple_gate[i], w_ple_proj[i], norm_ple_post[i])
        zc = jnp.zeros((bp, CONV_MOD_W - 1, D_CONV_GRP), hp.dtype)
        zx = jnp.zeros((bp, SSM_CONV_W - 1, D_XBC), hp.dtype)
        zs = jnp.zeros((bp, N_SSM_HEADS, HEAD_DIM, D_STATE), jnp.float32)
        hp, cp, xp, sp = decoder_layer(hp, p_prompt[i], zc, zx, zs, *lw)
        hs, cs, xs, ss = decoder_layer(hs, p_sample[i], state_conv_mod[i], state_ssm_conv[i],
                                       state_ssm[i], *lw)
        cp_l.append(cp); xp_l.append(xp); sp_l.append(sp.astype(hp.dtype))
        cs_l.append(cs); xs_l.append(xs); ss_l.append(ss.astype(hs.dtype))
    new_conv_mod_prompt = jnp.stack(cp_l, axis=0)
    new_ssm_conv_prompt = jnp.stack(xp_l, axis=0)
    new_ssm_prompt = jnp.stack(sp_l, axis=0)
    new_conv_mod_sample = jnp.stack(cs_l, axis=0)
    new_ssm_conv_sample = jnp.stack(xs_l, axis=0)
    new_ssm_sample = jnp.stack(ss_l, axis=0)
    return (hp, hs, new_conv_mod_prompt, new_ssm_conv_prompt, new_ssm_prompt,
            new_conv_mod_sample, new_ssm_conv_sample, new_ssm_sample)
```

```python
import numpy as np
import concourse.bass as bass
import concourse.mybir as mybir
from concourse.bass_utils import run_bass_kernel_spmd
from contextlib import ExitStack

F32 = mybir.dt.float32
BF = mybir.dt.bfloat16
ALU = mybir.AluOpType
AF = mybir.ActivationFunctionType
AX = mybir.AxisListType

D = 2048
KC = 16
DFF = 5632
NHALF = 22
DCV = 1024
DSSM = 3072
DXBC = 5120
NH = 48
NG = 8
HPG = 6
DST = 128
EPS = 1e-6
S_Z = 2048
S_X = 5120
S_B = 8192
S_C = 9216
S_DT = 10240
TP = 1024
NCH = 8
NSLOT = 3

CV_G = 0
CV_CW = 128
CV_CB = CV_CW + 248
CV_LG = CV_CB + 8
CV_LB = CV_LG + 8
CV_SW = CV_LB + 8
CV_SB = CV_SW + 160
CV_NG = CV_SB + 40
CV_BC = CV_NG + 24
CV_DF = CV_BC + 144
CV_N = CV_DF + 24


class Res:
    __slots__ = ("name", "w", "r", "excl")

    def __init__(self, name, excl=False):
        self.name = name
        self.w = None
        self.r = []
        self.excl = excl


class Eng:
    def __init__(self, raw, sem, name):
        self.raw = raw
        self.sem = sem
        self.name = name
        self.n = 0
        self.waited = {}


class Stream:
    def __init__(self, sem, name):
        self.sem = sem
        self.name = name
        self.n = 0
        self.inc = 16


class Builder:
    def __init__(self, nc, es):
        self.nc = nc
        self.es = es
        self.nsem = 0
        self.pe = Eng(nc.tensor, self.sem("pe"), "pe")
        self.act = Eng(nc.scalar, self.sem("act"), "act")
        self.dve = Eng(nc.vector, self.sem("dve"), "dve")
        self.pool = Eng(nc.gpsimd, self.sem("pool"), "pool")
        self.sp = Eng(nc.sync, self.sem("sp"), "sp")
        self.engs = [self.pe, self.act, self.dve, self.pool, self.sp]
        self.streams = []
        self.pe_pending = []
        self.pe_out_pending = []

    def sem(self, name):
        self.nsem += 1
        return self.es.enter_context(self.nc.semaphore(name))

    def stream(self, name):
        s = Stream(self.sem("d_" + name), name)
        self.streams.append(s)
        return s

    def wait(self, eng, tok):
        if tok is None:
            return
        if tok[0] == "e":
            src, val = tok[1], tok[2]
            if src is eng and eng is self.pe:
                return
            sem = src.sem
            key = src.name
        else:
            st = tok[1]
            sem = st.sem
            val = st.inc * st.n
            key = "d_" + st.name
        if eng.waited.get(key, 0) >= val:
            return
        eng.raw.wait_ge(sem, val)
        eng.waited[key] = val

    def _pre(self, eng, reads, writes):
        for r in reads:
            self.wait(eng, r.w)
            if r.excl:
                for t in r.r:
                    if t[0] != "e" or t[1] is not eng:
                        self.wait(eng, t)
        for w in writes:
            assert w not in self.pe_pending or eng is self.pe, "PE pending read on " + w.name
            self.wait(eng, w.w)
            for t in w.r:
                self.wait(eng, t)

    def _post(self, tok, reads, writes):
        for r in reads:
            r.r.append(tok)
            if len(r.r) > 24:
                r.r = r.r[-24:]
        for w in writes:
            w.w = tok
            w.r = []

    def op(self, eng, fn, reads=(), writes=()):
        self._pre(eng, reads, writes)
        ins = fn()
        ins.then_inc(eng.sem, 1)
        eng.n += 1
        tok = ("e", eng, eng.n)
        self._post(tok, reads, writes)
        return tok

    def A(self, fn, reads=(), writes=()):
        return self.op(self.act, fn, reads, writes)

    def V(self, fn, reads=(), writes=()):
        return self.op(self.dve, fn, reads, writes)

    def dma(self, q, st, out, in_, reads=(), writes=()):
        self._pre(q, reads, writes)
        ins = q.raw.dma_start(out=out, in_=in_)
        ins.then_inc(st.sem, 16)
        st.n += 1
        tok = ("d", st, st.n)
        self._post(tok, reads, writes)
        return tok

    def mm(self, out, lhsT, rhs, start, stop, reads, out_res, sig, transpose=False):
        pe = self.pe
        for r in reads:
            self.wait(pe, r.w)
            if r not in self.pe_pending:
                self.pe_pending.append(r)
        if start:
            self.wait(pe, out_res.w)
            for t in out_res.r:
                self.wait(pe, t)
        if transpose:
            ins = pe.raw.transpose(out=out, in_=lhsT, identity=rhs)
        else:
            ins = pe.raw.matmul(out, lhsT, rhs, start=start, stop=stop)
        if stop and out_res not in self.pe_out_pending:
            self.pe_out_pending.append(out_res)
        if sig:
            ins.then_inc(pe.sem, 1)
            pe.n += 1
            tok = ("e", pe, pe.n)
            for r in self.pe_pending:
                r.r.append(tok)
            self.pe_pending = []
            for o in self.pe_out_pending:
                o.w = tok
                o.r = []
            self.pe_out_pending = []
        return ins

    def barrier(self, engs=None):
        assert not self.pe_pending and not self.pe_out_pending
        engs = engs or [self.pe, self.act, self.dve, self.sp]
        for e in engs:
            for f in [self.pe, self.act, self.dve]:
                if f is not e and f.n > 0:
                    self.wait(e, ("e", f, f.n))
            for st in self.streams:
                if st.n > 0 and not st.name.startswith("slot"):
                    self.wait(e, ("d", st, st.n))


def tt_tiles(T):
    a = (T + 2) // 3
    b = (T - a + 1) // 2
    return [(0, a), (a, a + b), (a + b, T)]


def build_program(NP, TS, stage=99):
    T = TP + TS
    TT = tt_tiles(T)
    nc = bass.Bass("TRN2", target_bir_lowering=False)

    def din(name, shape):
        return nc.dram_tensor(name, list(shape), F32, kind="ExternalInput").ap()

    def dout(name, shape):
        return nc.dram_tensor(name, list(shape), F32, kind="ExternalOutput").ap()

    xT = din("xT", [NP, KC, 128, T])
    pT = din("pT", [NP, 2, 128, T])
    st_conv = din("st_conv", [NP, TS, 30, DCV])
    st_xbc = din("st_xbc", [NP, TS * 3, DXBC])
    st_ssm = din("st_ssm", [NP, TS, DSSM, DST])
    cvec_d = din("cvec", [128, CV_N])
    ident_d = din("ident", [128, 128])
    tri_d = din("tri", [128, 128])
    flag_d = din("flag", [128, 1])
    negm_d = din("negm", [128, 128])
    W = {}
    for nm, shp in [("w_ffn1_gate", (D, DFF)), ("w_ffn1_up", (D, DFF)), ("w_ffn1_down", (DFF, D)),
                    ("w_in", (D, 10288)), ("w_out", (4096, D)),
                    ("w_ffn2_gate", (D, DFF)), ("w_ffn2_up", (D, DFF)), ("w_ffn2_down", (DFF, D)),
                    ("w_ple_gate", (D, D)), ("w_ple_proj", (256, D))]:
        W[nm] = din(nm, shp)
    yT = dout("yT", [NP, KC, 128, T])
    o_cms = dout("o_cms", [NP, TS, 30, DCV])
    o_xbs = dout("o_xbs", [NP, TS, 3, DXBC])
    o_sss = dout("o_sss", [NP, TS, DSSM, DST])
    o_cmp = dout("o_cmp", [NP, 30, DCV])
    o_xbp = dout("o_xbp", [NP, 3, DXBC])
    o_ssp = dout("o_ssp", [NP, DSSM, DST])
    hsp = nc.dram_tensor("hsp", [KC, 128, T], F32, kind="Internal").ap()

    with ExitStack() as es:
        K = Builder(nc, es)
        PE, ACT, DVE, POOL, SP = K.pe, K.act, K.dve, K.pool, K.sp

        ARENA_B = 188000
        arena = es.enter_context(nc.sbuf_tensor("arena", [128, ARENA_B // 4], F32))
        slots = [es.enter_context(nc.sbuf_tensor("slot%d" % i, [128, 4096], BF)) for i in range(NSLOT)]
        slot_res = [Res("slot%d" % i) for i in range(NSLOT)]
        slot_st = [K.stream("slot%d" % i) for i in range(NSLOT)]
        banks = [es.enter_context(nc.psum_tensor("bank%d" % i, [128, 512], F32)) for i in range(8)]
        bank_res = [Res("bank%d" % i, excl=True) for i in range(8)]
        free_banks = list(range(8))

        def balloc():
            assert free_banks, "out of PSUM banks"
            return free_banks.pop(0)

        def bfree(b):
            free_banks.append(b)

        def view(off, shape, dt):
            esz = 4 if dt == F32 else 2
            n = int(np.prod(shape))
            nb = (n * esz + 3) // 4 * 4
            assert off % 4 == 0
            assert off + nb <= ARENA_B, (off, nb)
            ap = arena[:, off // 4: off // 4 + nb // 4]
            if dt != F32:
                ap = ap.bitcast(dt)[:, 0:n]
            if len(shape) == 2:
                return ap.rearrange("p (a b) -> p a b", a=shape[0])
            if len(shape) == 3:
                return ap.rearrange("p (a b c) -> p a b c", a=shape[0], b=shape[1])
            return ap

        OFF_U = 0
        OFF_H = OFF_U + 2 * KC * T
        OFF_X = OFF_H + 4 * KC * T
        OFF_C = OFF_X + 4 * KC * T
        U = view(OFF_U, [KC, T], BF)
        H = view(OFF_H, [KC, T], F32)
        AH = view(OFF_X, [NHALF, T], BF)
        INT = view(OFF_X, [32, T], BF)
        U_res = Res("U")
        H_res = [Res("H%d" % i) for i in range(KC)]
        AH_res = [Res("AH%d" % i) for i in range(NHALF)]
        INT_res = [Res("INT%d" % i) for i in range(32)]
        hsp_res = [Res("hsp%d" % i) for i in range(KC)]

        coff = [OFF_C]

        def calloc(shape, dt):
            esz = 4 if dt == F32 else 2
            n = int(np.prod(shape)) * esz
            n = (n + 3) // 4 * 4
            o = coff[0]
            coff[0] += n
            return view(o, shape, dt)

        CV = calloc([CV_N], F32)
        IDN = calloc([128], F32)
        TRI = calloc([128], F32)
        ONE = calloc([128], F32)
        IDB = calloc([128], BF)
        ONB = calloc([128], BF)
        EPSC = calloc([1], F32)
        GH = calloc([2 * KC], F32)
        ANEG = calloc([NH], F32)
        RS = calloc([T], F32)
        SQ = [calloc([T], BF) for _ in range(2)]
        SQ_res = [Res("SQ0"), Res("SQ1")]
        RS_res = Res("RS")
        const_res = Res("const")
        OFF_XS = OFF_X + 2 * NHALF * T
        SS = [view(OFF_XS + i * 4 * 352, [352], F32) for i in range(2)]
        SS_res = [Res("SS0"), Res("SS1")]
        HST = [view(OFF_XS + 2 * 4 * 352 + i * 4 * T, [T], F32) for i in range(2)]
        HST_res = [Res("HST0"), Res("HST1")]
        assert OFF_XS + 2 * 4 * 352 + 2 * 4 * T <= OFF_C

        ld_st = K.stream("ld")
        sp_st = K.stream("spill")
        hst_st = [K.stream("hst0"), K.stream("hst1")]
        out_st = K.stream("out")

        K.dma(SP, ld_st, CV, cvec_d, writes=[const_res])
        K.dma(SP, ld_st, IDN, ident_d, writes=[const_res])
        K.dma(SP, ld_st, TRI, tri_d, writes=[const_res])
        K.V(lambda: nc.vector.memset(ONE, 1.0), writes=[const_res])
        K.V(lambda: nc.vector.memset(ONB, 1.0), writes=[const_res])
        K.V(lambda: nc.vector.memset(EPSC, EPS), writes=[const_res])
        K.V(lambda: nc.vector.tensor_copy(out=IDB, in_=IDN), reads=[const_res], writes=[const_res])
        K.V(lambda: nc.vector.tensor_scalar(out=GH[:, 0:KC], in0=CV[:, CV_G + 16:CV_G + 32], scalar1=0.5,
                                            scalar2=None, op0=ALU.mult), reads=[const_res], writes=[const_res])
        K.V(lambda: nc.vector.tensor_scalar(out=GH[:, KC:2 * KC], in0=CV[:, CV_G + 80:CV_G + 96], scalar1=0.5,
                                            scalar2=None, op0=ALU.mult), reads=[const_res], writes=[const_res])
        K.A(lambda: nc.scalar.activation(out=ANEG, in_=CV[:, CV_BC + 48:CV_BC + 96], func=AF.Exp),
            reads=[const_res], writes=[const_res])
        K.V(lambda: nc.vector.tensor_scalar(out=ANEG, in0=ANEG, scalar1=-1.0, scalar2=None, op0=ALU.mult),
            reads=[const_res], writes=[const_res])

        class WS:
            def __init__(self):
                self.plan = []
                self.issued = 0
                self.next_i = 0
                self.done_upto = 0

            def pump(self):
                while self.issued < len(self.plan) and self.issued < self.done_upto + NSLOT:
                    i = self.issued
                    s = i % NSLOT
                    off = 0
                    for ent in self.plan[i][1]:
                        w, k0, kc, f0 = ent[:4]
                        fw = ent[4] if len(ent) > 4 else 128
                        dst = slots[s][:, off:off + kc * fw].rearrange("p (k f) -> p k f", k=kc)
                        src = w[k0 * 128:(k0 + kc) * 128, f0:f0 + fw].rearrange("(k p) f -> p k f", p=128)
                        K.dma(POOL, slot_st[s], dst, src, writes=[slot_res[s]])
                        off += kc * fw
                    self.issued += 1

            def get(self, tag):
                i = self.next_i
                assert self.plan[i][0] == tag, (self.plan[i][0], tag)
                self.pump()
                assert i < self.issued
                self.next_i += 1
                s = i % NSLOT
                views = []
                off = 0
                for ent in self.plan[i][1]:
                    w, k0, kc, f0 = ent[:4]
                    fw = ent[4] if len(ent) > 4 else 128
                    views.append(slots[s][:, off:off + kc * fw].rearrange("p (k f) -> p k f", k=kc))
                    off += kc * fw
                return i, slot_res[s], views

            def done(self, i):
                assert i == self.done_upto
                self.done_upto += 1
                self.pump()

        ws = WS()

        def plan_ffn(pre):
            wg, wu, wd = W[pre + "_gate"], W[pre + "_up"], W[pre + "_down"]
            for hf in range(2):
                for f in range(NHALF):
                    fc = hf * NHALF + f
                    ws.plan.append((pre + "gu", [(wg, 0, KC, fc * 128), (wu, 0, KC, fc * 128)]))
                for d in range(KC):
                    ws.plan.append((pre + "dn", [(wd, hf * NHALF, NHALF, d * 128)]))

        def plan_mixer():
            wi = W["w_in"]
            for cc in range(8):
                ws.plan.append(("glu", [(wi, 0, KC, cc * 128), (wi, 0, KC, 1024 + cc * 128)]))
            ws.plan.append(("dt", [(wi, 0, KC, S_DT, NH)]))
            for g in range(NG):
                for i in range(3):
                    ws.plan.append(("xs", [(wi, 0, KC, S_X + g * 384 + i * 128)]))
                ws.plan.append(("bc", [(wi, 0, KC, S_B + g * 128), (wi, 0, KC, S_C + g * 128)]))
                for i in range(3):
                    ws.plan.append(("z", [(wi, 0, KC, S_Z + g * 384 + i * 128)]))
            for d in range(KC):
                ws.plan.append(("out", [(W["w_out"], 0, 32, d * 128)]))

        def plan_ple():
            for d in range(KC):
                ws.plan.append(("ple", [(W["w_ple_gate"], 0, KC, d * 128), (W["w_ple_proj"], 0, 2, d * 128)]))

        for p in range(NP):
            plan_ffn("w_ffn1")
            if stage >= 2:
                plan_mixer()
            if stage >= 3:
                plan_ffn("w_ffn2")
            if stage >= 4:
                plan_ple()

        def stats_to_RS(src_fn, n_k, src_res_fn, scale):
            bks = [balloc() for _ in TT]
            for k in range(n_k):
                sq = SQ[k % 2]
                sr = SQ_res[k % 2]
                K.A(lambda k=k, sq=sq: nc.scalar.activation(out=sq, in_=src_fn(k), func=AF.Square),
                    reads=[src_res_fn(k)], writes=[sr])
                for ti, (a, b) in enumerate(TT):
                    K.mm(banks[bks[ti]][:, 0:b - a], ONB, sq[:, a:b], start=(k == 0), stop=(k == n_k - 1),
                         reads=[sr, const_res], out_res=bank_res[bks[ti]], sig=(ti == len(TT) - 1))
            for ti, (a, b) in enumerate(TT):
                K.A(lambda ti=ti, a=a, b=b: nc.scalar.activation(out=RS[:, a:b], in_=banks[bks[ti]][:, 0:b - a],
                                                                func=AF.Ln, bias=EPSC[:, 0:1], scale=scale),
                    reads=[bank_res[bks[ti]], const_res], writes=[RS_res])
            K.A(lambda: nc.scalar.activation(out=RS, in_=RS, func=AF.Exp, scale=-0.5), reads=[RS_res], writes=[RS_res])
            for b in bks:
                bfree(b)

        def prenorm(gi):
            stats_to_RS(lambda k: H[:, k, :], KC, lambda k: H_res[k], 1.0 / D)
            for k in range(KC):
                K.V(lambda k=k: nc.vector.scalar_tensor_tensor(out=U[:, k, :], in0=H[:, k, :],
                                                               scalar=CV[:, CV_G + gi * 16 + k:CV_G + gi * 16 + k + 1],
                                                               in1=RS, op0=ALU.mult, op1=ALU.mult),
                    reads=[H_res[k], RS_res, const_res], writes=[U_res])

        def spill_H():
            for k in range(KC):
                K.dma(SP, sp_st, hsp[k], H[:, k, :], reads=[H_res[k]], writes=[hsp_res[k]])

        def post_residual(gcol_fn, after_k=None):
            stats_to_RS(lambda k: H[:, k, :], KC, lambda k: H_res[k], 1.0 / D)
            for k in range(KC):
                K.dma(SP, hst_st[k % 2], HST[k % 2], hsp[k], reads=[hsp_res[k]], writes=[HST_res[k % 2]])
                K.V(lambda k=k: nc.vector.scalar_tensor_tensor(out=H[:, k, :], in0=H[:, k, :], scalar=gcol_fn(k),
                                                               in1=RS, op0=ALU.mult, op1=ALU.mult),
                    reads=[RS_res, const_res], writes=[H_res[k]])
                K.V(lambda k=k: nc.vector.tensor_tensor(out=H[:, k, :], in0=H[:, k, :], in1=HST[k % 2], op=ALU.add),
                    reads=[HST_res[k % 2]], writes=[H_res[k]])
                if after_k is not None:
                    after_k(k)

        def ffn(pre, g_pre, gh_idx):
            prenorm(g_pre)
            spill_H()
            ssi = [0]
            for hf in range(2):
                for f in range(NHALF):
                    pi, pres, (pg, pu) = ws.get(pre + "gu")
                    for ti, (a, b) in enumerate(TT):
                        n = b - a
                        bg, bu = balloc(), balloc()
                        for k in range(KC):
                            K.mm(banks[bg][:, 0:n], pg[:, k, :], U[:, k, a:b], start=(k == 0), stop=(k == KC - 1),
                                 reads=[pres, U_res], out_res=bank_res[bg], sig=False)
                        for k in range(KC):
                            K.mm(banks[bu][:, 0:n], pu[:, k, :], U[:, k, a:b], start=(k == 0), stop=(k == KC - 1),
                                 reads=[pres, U_res], out_res=bank_res[bu], sig=(k == KC - 1))
                        s = ssi[0] % 2
                        ssi[0] += 1
                        K.A(lambda bg=bg, n=n, s=s: nc.scalar.activation(out=SS[s][:, 0:n], in_=banks[bg][:, 0:n],
                                                                         func=AF.Silu),
                            reads=[bank_res[bg]], writes=[SS_res[s]])
                        K.V(lambda bu=bu, n=n, s=s, f=f, a=a, b=b: nc.vector.tensor_tensor(
                            out=AH[:, f, a:b], in0=SS[s][:, 0:n], in1=banks[bu][:, 0:n], op=ALU.mult),
                            reads=[SS_res[s], bank_res[bu]], writes=[AH_res[f]])
                        bfree(bg)
                        bfree(bu)
                    ws.done(pi)
                for d in range(KC):
                    pi, pres, (pd,) = ws.get(pre + "dn")
                    for ti, (a, b) in enumerate(TT):
                        n = b - a
                        bk = balloc()
                        for k in range(NHALF):
                            K.mm(banks[bk][:, 0:n], pd[:, k, :], AH[:, k, a:b], start=(k == 0), stop=(k == NHALF - 1),
                                 reads=[pres, AH_res[k]], out_res=bank_res[bk], sig=(k == NHALF - 1))
                        if hf == 0:
                            K.A(lambda bk=bk, n=n, d=d, a=a, b=b: nc.scalar.copy(out=H[:, d, a:b], in_=banks[bk][:, 0:n]),
                                reads=[bank_res[bk]], writes=[H_res[d]])
                        else:
                            K.V(lambda bk=bk, n=n, d=d, a=a, b=b: nc.vector.tensor_tensor(
                                out=H[:, d, a:b], in0=H[:, d, a:b], in1=banks[bk][:, 0:n], op=ALU.add),
                                reads=[bank_res[bk]], writes=[H_res[d]])
                        bfree(bk)
                    ws.done(pi)
            post_residual(lambda k: GH[:, gh_idx * KC + k:gh_idx * KC + k + 1])

        XDT = [calloc([6, 64], BF) for _ in range(2)]
        XDD = [calloc([6, 64], BF) for _ in range(2)]
        E32 = [calloc([3, 128], BF) for _ in range(2)]
        XDT_res = [Res("XDT0"), Res("XDT1")]
        XDD_res = [Res("XDD0"), Res("XDD1")]
        E32_res = [Res("E320"), Res("E321")]
        SL4 = calloc([3, 128], F32)
        FLAG = calloc([1], F32)
        K.dma(SP, ld_st, FLAG, flag_d, writes=[const_res])
        NEGM = calloc([128], F32)
        K.dma(SP, ld_st, NEGM, negm_d, writes=[const_res])
        assert coff[0] <= ARENA_B, coff[0]
        misc_st = K.stream("misc")
        slab_st = [K.stream("slab%d" % i) for i in range(4)]
        sout_st = K.stream("sout")
        yout_st = K.stream("yout")
        pt_st = K.stream("ptld")
        xs_st = K.stream("xsrc")
        xd_st = K.stream("xdst")
        cc_st = K.stream("cc")
        cc_st.inc = 1
        PAIRS = [[0, 1], [2, 3], [4, 5], [6, 7]]
        xch_i = [0]

        def exchange(src_ap, src_res, width, dst_ap, dst_res):
            i = xch_i[0]
            xch_i[0] += 1
            xsrc = nc.dram_tensor("xsrc%d" % i, [128, width], F32, kind="Internal").ap()
            xdst = nc.dram_tensor("xdst%d" % i, [256, width], F32, kind="Internal").ap()
            r1, r2 = Res("xsrc%d" % i), Res("xdst%d" % i)
            xs_v, xd_v = xsrc, xdst[0:128, :]
            if len(src_ap.shape) == 3:
                xs_v = xsrc.rearrange("p (a b) -> p a b", a=src_ap.shape[1])
                xd_v = xdst[0:128, :].rearrange("p (a b) -> p a b", a=src_ap.shape[1])
            K.dma(SP, xs_st, xs_v, src_ap, reads=[src_res], writes=[r1])
            K._pre(POOL, [r1], [r2])
            ins = nc.gpsimd.collective_compute("AllGather", ALU.bypass, replica_groups=PAIRS, ins=[xsrc],
                                               outs=[xdst])
            ins.then_inc(cc_st.sem, 1)
            cc_st.n += 1
            tok = ("d", cc_st, cc_st.n)
            K._post(tok, [r1], [r2])
            K.dma(SP, xd_st, dst_ap, xd_v, reads=[r2], writes=[dst_res])

        def linear_fm(tag, in_fn, in_res_fn, nk, evac):
            pi, pres, (pw,) = ws.get(tag)
            for ti, (a, b) in enumerate(TT):
                bk = balloc()
                for k in range(nk):
                    K.mm(banks[bk][:, 0:b - a], pw[:, k, :], in_fn(k, a, b), start=(k == 0), stop=(k == nk - 1),
                         reads=[pres, in_res_fn(k)], out_res=bank_res[bk], sig=(k == nk - 1))
                evac(ti, a, b, bk)
                bfree(bk)
            ws.done(pi)

        def mixer(p):
            prenorm(2)
            spill_H()
            K.barrier()
            hoff = [OFF_H]

            def halloc(shape, dt):
                esz = 4 if dt == F32 else 2
                n = (int(np.prod(shape)) * esz + 3) // 4 * 4
                o = hoff[0]
                hoff[0] += n
                assert hoff[0] <= OFF_X, ("H scratch overflow", hoff[0] - OFF_X)
                return view(o, shape, dt)

            LA = TT[-1][0]
            assert LA <= TP - 30
            DT = halloc([9, NH], F32)
            DTA = halloc([9, NH], F32)
            NACS = halloc([8, NH], F32)
            EXA = halloc([8, NH], F32)
            CD = halloc([8, NH], F32)
            W2 = halloc([8, NH], F32)
            ETOT = halloc([8, NH], F32)
            CDTOT = halloc([NH], F32)
            DECS = halloc([NH], F32)
            dt_res = Res("dt")
            mix_base = hoff[0]

            Vb = halloc([8, 30 + T], BF)
            MSS = [halloc([352], F32) for _ in range(2)]
            VT32 = halloc([8, 30 + TS], F32)
            HV32 = halloc([8, 30], F32)
            ACC = [halloc([T], F32) for _ in range(2)]
            CBF = [halloc([T], BF)] * 2
            CSQ = [halloc([T], BF)] * 2
            MEAN = halloc([T], F32)
            RSTD = halloc([T], F32)
            DGC = view(hoff[0] - 8 * T, [31, 128], BF)
            assert 31 * 128 * 2 <= 8 * T
            STC = [halloc([TS, 30], F32) for _ in range(2)]
            TMPS = halloc([TS, 30], F32)
            NB4 = TS // 4
            STG = [view(OFF_X + 8 * 2 * T + 4 * 8 * T + 4096 + i * 4 * NB4 * 128, [NB4, 128], F32) for i in range(2)]
            assert OFF_X + 8 * 2 * T + 4 * 8 * T + 4096 + 2 * 4 * NB4 * 128 <= OFF_C
            CO = view(OFF_X + 8 * 2 * T, [8, T], F32)
            TROW = view(OFF_X + 8 * 2 * T + 4 * 8 * T, [1024], F32)
            assert OFF_X + 8 * 2 * T + 4 * 8 * T + 4096 <= OFF_C
            Vb_res = [Res("Vb%d" % i) for i in range(8)]
            MSS_res = [Res("MSS0"), Res("MSS1")]
            vt_res = Res("vt32")
            hv_res = Res("hv32")
            ACC_res = [Res("ACC0"), Res("ACC1")]
            CBF_res = [Res("CBF0")] * 2
            CSQ_res = [Res("CSQ0")] * 2
            STG_res = [Res("STG0"), Res("STG1")]
            STC_res = [Res("stc0"), Res("stc1")]
            CO_res = [Res("CO%d" % i) for i in range(8)]
            ln_res = Res("ln")
            trow_res = Res("trow")
            tmps_res = Res("tmps")

            def cwc(cc, j):
                return CV[:, CV_CW + cc * 31 + j:CV_CW + cc * 31 + j + 1]

            mi = [0]
            for cc in range(8):
                pi, pres, (pa, pb) = ws.get("glu")
                for ti, (a, b) in enumerate(TT):
                    n = b - a
                    ba, bb = balloc(), balloc()
                    for k in range(KC):
                        K.mm(banks[ba][:, 0:n], pa[:, k, :], U[:, k, a:b], start=(k == 0), stop=(k == KC - 1),
                             reads=[pres, U_res], out_res=bank_res[ba], sig=False)
                    for k in range(KC):
                        K.mm(banks[bb][:, 0:n], pb[:, k, :], U[:, k, a:b], start=(k == 0), stop=(k == KC - 1),
                             reads=[pres, U_res], out_res=bank_res[bb], sig=(k == KC - 1))
                    s = mi[0] % 2
                    mi[0] += 1
                    K.A(lambda bb=bb, n=n, s=s: nc.scalar.activation(out=MSS[s][:, 0:n], in_=banks[bb][:, 0:n],
                                                                     func=AF.Sigmoid),
                        reads=[bank_res[bb]], writes=[MSS_res[s]])
                    K.V(lambda ba=ba, n=n, s=s, cc=cc, a=a, b=b: nc.vector.tensor_tensor(
                        out=Vb[:, cc, 30 + a:30 + b], in0=banks[ba][:, 0:n], in1=MSS[s][:, 0:n], op=ALU.mult),
                        reads=[bank_res[ba], MSS_res[s]], writes=[Vb_res[cc]])
                    if ti == len(TT) - 1:
                        K.V(lambda ba=ba, s=s, cc=cc, a=a: nc.vector.tensor_tensor(
                            out=VT32[:, cc, :], in0=banks[ba][:, TP - 30 - a:T - a], in1=MSS[s][:, TP - 30 - a:T - a],
                            op=ALU.mult), reads=[bank_res[ba], MSS_res[s]], writes=[vt_res])
                    bfree(ba)
                    bfree(bb)
                ws.done(pi)
            exchange(VT32[:, :, 0:30], vt_res, 240, HV32, hv_res)

            K.V(lambda: nc.vector.memset(DT, 0.0), writes=[dt_res])
            K.V(lambda: nc.vector.memset(DTA, 0.0), writes=[dt_res])
            pi, pres, (pdt,) = ws.get("dt")
            for c in range(9):
                M = 128 if c < 8 else TS
                bk = balloc()
                for k in range(KC):
                    K.mm(banks[bk][0:M, 0:NH], U[:, k, c * 128:c * 128 + M], pdt[:, k, :], start=(k == 0),
                         stop=(k == KC - 1), reads=[pres, U_res], out_res=bank_res[bk], sig=(k == KC - 1))
                K.V(lambda c=c, M=M, bk=bk: nc.vector.tensor_tensor(out=DT[0:M, c, :], in0=banks[bk][0:M, 0:NH],
                                                                     in1=CV[0:M, CV_BC:CV_BC + NH], op=ALU.add),
                    reads=[bank_res[bk], const_res], writes=[dt_res])
                bfree(bk)
            ws.done(pi)
            K.A(lambda: nc.scalar.activation(out=DT, in_=DT, func=AF.Exp), reads=[dt_res], writes=[dt_res])
            K.A(lambda: nc.scalar.activation(out=DT, in_=DT, func=AF.Ln, bias=1.0, scale=1.0), reads=[dt_res],
                writes=[dt_res])
            K.V(lambda: nc.vector.tensor_tensor(out=DTA, in0=DT, in1=ANEG.unsqueeze(1).to_broadcast([128, 9, NH]),
                                                op=ALU.mult), reads=[dt_res, const_res], writes=[dt_res])
            b1, b2 = balloc(), balloc()
            for c in range(8):
                K.mm(banks[b1][:, c * NH:(c + 1) * NH], TRI, DTA[:, c, :], start=True, stop=True,
                     reads=[dt_res, const_res], out_res=bank_res[b1], sig=False)
            for c in range(8):
                K.mm(banks[b2][:, c * NH:(c + 1) * NH], ONE, DTA[:, c, :], start=True, stop=True,
                     reads=[dt_res, const_res], out_res=bank_res[b2], sig=(c == 7))
            K.A(lambda: nc.scalar.activation(out=EXA, in_=banks[b1][:, 0:8 * NH], func=AF.Exp),
                reads=[bank_res[b1]], writes=[dt_res])
            K.V(lambda: nc.vector.tensor_scalar(out=NACS, in0=banks[b1][:, 0:8 * NH], scalar1=-1.0, scalar2=None,
                                                op0=ALU.mult), reads=[bank_res[b1]], writes=[dt_res])
            K.A(lambda: nc.scalar.activation(out=CD, in_=banks[b2][:, 0:8 * NH], func=AF.Exp),
                reads=[bank_res[b2]], writes=[dt_res])
            K.V(lambda: nc.vector.tensor_tensor(out=W2, in0=banks[b2][:, 0:8 * NH], in1=NACS, op=ALU.add),
                reads=[bank_res[b2], dt_res], writes=[dt_res])
            K.A(lambda: nc.scalar.activation(out=W2, in_=W2, func=AF.Exp), reads=[dt_res], writes=[dt_res])
            K.V(lambda: nc.vector.tensor_tensor(out=W2, in0=W2, in1=DT[:, 0:8, :], op=ALU.mult), reads=[dt_res],
                writes=[dt_res])
            K.A(lambda: nc.scalar.activation(out=DECS[0:TS, :], in_=DTA[0:TS, 8, :], func=AF.Exp), reads=[dt_res],
                writes=[dt_res])
            bfree(b1)
            bfree(b2)
            K.V(lambda: nc.vector.tensor_copy(out=ETOT[:, 0, :], in_=EXA[:, 0, :]), reads=[dt_res], writes=[dt_res])
            K.V(lambda: nc.vector.tensor_copy(out=CDTOT, in_=CD[:, 0, :]), reads=[dt_res], writes=[dt_res])
            for c in range(1, 8):
                K.V(lambda c=c: nc.vector.tensor_tensor(out=ETOT[:, c, :], in0=EXA[:, c, :], in1=CDTOT, op=ALU.mult),
                    reads=[dt_res], writes=[dt_res])
                K.V(lambda c=c: nc.vector.tensor_tensor(out=CDTOT, in0=CDTOT, in1=CD[:, c, :], op=ALU.mult),
                    reads=[dt_res], writes=[dt_res])

            K.dma(SP, out_st, o_xbs[p][:, 0:2, :], st_xbc[p].rearrange("(b j) c -> b j c", j=3)[:, 1:3, :])
            K.dma(SP, out_st, o_cms[p][:, 0:29, :], st_conv[p][:, 1:30, :])

            K.V(lambda: nc.vector.tensor_scalar(out=Vb[:, :, 0:30], in0=HV32, scalar1=FLAG[:, 0:1], scalar2=None,
                                                op0=ALU.mult), reads=[hv_res, const_res], writes=Vb_res)
            for cc in range(8):
                stc, stcr = STC[cc % 2], STC_res[cc % 2]
                sg, sr = STG[cc % 2], STG_res[cc % 2]
                for q4 in range(NB4):
                    K.dma(SP, misc_st, sg[0:120, q4, :],
                          st_conv[p][q4 * 4:(q4 + 1) * 4, :, cc * 128:(cc + 1) * 128].rearrange("b j c -> (b j) c"),
                          writes=[sr])
                bk = balloc()
                for q4 in range(NB4):
                    K.mm(banks[bk][:, q4 * 120:(q4 + 1) * 120], sg[0:120, q4, :], IDN[0:120, 0:120], start=True,
                         stop=True, reads=[sr, const_res], out_res=bank_res[bk], sig=(q4 == NB4 - 1), transpose=True)
                K.A(lambda bk=bk, stc=stc: nc.scalar.copy(out=stc.rearrange("p b j -> p (b j)"),
                                                          in_=banks[bk][:, 0:TS * 30]),
                    reads=[bank_res[bk]], writes=[stcr])
                bfree(bk)
                acc = ACC[cc % 2]
                ar = ACC_res[cc % 2]
                K.V(lambda cc=cc: nc.vector.tensor_tensor(
                    out=DGC, in0=IDB.unsqueeze(1).to_broadcast([128, 31, 128]),
                    in1=CV[:, CV_CW + cc * 31:CV_CW + cc * 31 + 31].unsqueeze(2).to_broadcast([128, 31, 128]),
                    op=ALU.mult), reads=[const_res], writes=[ln_res])
                for ti, (a, b) in enumerate(TT):
                    n = b - a
                    bk = balloc()
                    for j in range(31):
                        K.mm(banks[bk][:, 0:n], DGC[:, j, :], Vb[:, cc, j + a:j + b], start=(j == 0), stop=(j == 30),
                             reads=[ln_res, Vb_res[cc]], out_res=bank_res[bk], sig=(j == 30))
                    pe_ = min(b, TP)
                    K.A(lambda cc=cc, bk=bk, a=a, pe_=pe_: nc.scalar.activation(
                        out=CO[:, cc, a:pe_], in_=banks[bk][:, 0:pe_ - a], func=AF.Identity,
                        bias=CV[:, CV_CB + cc:CV_CB + cc + 1], scale=1.0),
                        reads=[bank_res[bk], const_res, STG_res[0], STG_res[1]], writes=[CO_res[cc]])
                    bfree(bk)
                K.V(lambda cc=cc, stc=stc: nc.vector.tensor_tensor(
                    out=TMPS, in0=stc,
                    in1=CV[:, CV_CW + cc * 31:CV_CW + cc * 31 + 30].unsqueeze(1).to_broadcast([128, TS, 30]),
                    op=ALU.mult), reads=[stcr, const_res], writes=[tmps_res])
                K.V(lambda acc=acc: nc.vector.tensor_reduce(out=acc[:, TP:T], in_=TMPS, axis=AX.X, op=ALU.add),
                    reads=[tmps_res], writes=[ar])
                K.V(lambda cc=cc, acc=acc: nc.vector.scalar_tensor_tensor(
                    out=acc[:, TP:T], in0=Vb[:, cc, 30 + TP:30 + T], scalar=cwc(cc, 30), in1=acc[:, TP:T],
                    op0=ALU.mult, op1=ALU.add), reads=[Vb_res[cc], const_res], writes=[ar])
                K.V(lambda cc=cc, acc=acc: nc.vector.tensor_scalar(
                    out=acc[:, TP:T], in0=acc[:, TP:T], scalar1=CV[:, CV_CB + cc:CV_CB + cc + 1], scalar2=None,
                    op0=ALU.add), reads=[const_res], writes=[ar])
                K.A(lambda cc=cc, acc=acc: nc.scalar.copy(out=CO[:, cc, TP:T], in_=acc[:, TP:T]),
                    reads=[ar, STG_res[0], STG_res[1]], writes=[CO_res[cc]])
            bm = [balloc() for _ in TT]
            bq = [balloc() for _ in TT]
            for cc in range(8):
                x = cc % 2
                K.A(lambda cc=cc, x=x: nc.scalar.copy(out=CBF[x], in_=CO[:, cc, :]), reads=[CO_res[cc]],
                    writes=[CBF_res[x]])
                K.A(lambda cc=cc, x=x: nc.scalar.activation(out=CSQ[x], in_=CO[:, cc, :], func=AF.Square),
                    reads=[CO_res[cc]], writes=[CSQ_res[x]])
                for ti, (a, b) in enumerate(TT):
                    K.mm(banks[bm[ti]][:, 0:b - a], ONB, CBF[x][:, a:b], start=(cc == 0), stop=(cc == 7),
                         reads=[CBF_res[x], const_res], out_res=bank_res[bm[ti]], sig=False)
                for ti, (a, b) in enumerate(TT):
                    K.mm(banks[bq[ti]][:, 0:b - a], ONB, CSQ[x][:, a:b], start=(cc == 0), stop=(cc == 7),
                         reads=[CSQ_res[x], const_res], out_res=bank_res[bq[ti]], sig=(ti == len(TT) - 1))
            for ti, (a, b) in enumerate(TT):
                K.A(lambda ti=ti, a=a, b=b: nc.scalar.mul(out=MEAN[:, a:b], in_=banks[bm[ti]][:, 0:b - a],
                                                          mul=1.0 / DCV), reads=[bank_res[bm[ti]]], writes=[ln_res])
                K.V(lambda ti=ti, a=a, b=b: nc.vector.tensor_scalar(out=RSTD[:, a:b], in0=banks[bq[ti]][:, 0:b - a],
                                                                    scalar1=1.0 / DCV, scalar2=None, op0=ALU.mult),
                    reads=[bank_res[bq[ti]]], writes=[ln_res])
            for b_ in bm + bq:
                bfree(b_)
            K.V(lambda: nc.vector.tensor_tensor(out=ACC[0], in0=MEAN, in1=MEAN, op=ALU.mult), reads=[ln_res],
                writes=[ACC_res[0]])
            K.V(lambda: nc.vector.tensor_tensor(out=RSTD, in0=RSTD, in1=ACC[0], op=ALU.subtract),
                reads=[ACC_res[0], ln_res], writes=[ln_res])
            K.A(lambda: nc.scalar.activation(out=RSTD, in_=RSTD, func=AF.Ln, bias=EPSC[:, 0:1], scale=1.0),
                reads=[ln_res, const_res], writes=[ln_res])
            K.A(lambda: nc.scalar.activation(out=RSTD, in_=RSTD, func=AF.Exp, scale=-0.5), reads=[ln_res],
                writes=[ln_res])
            for cc in range(8):
                acc = ACC[cc % 2]
                ar = ACC_res[cc % 2]
                K.V(lambda cc=cc, acc=acc: nc.vector.tensor_tensor(out=acc, in0=CO[:, cc, :], in1=MEAN,
                                                                   op=ALU.subtract),
                    reads=[CO_res[cc], ln_res], writes=[ar])
                K.V(lambda acc=acc: nc.vector.tensor_tensor(out=acc, in0=acc, in1=RSTD, op=ALU.mult),
                    reads=[ln_res], writes=[ar])
                K.A(lambda cc=cc, acc=acc: nc.scalar.activation(out=INT[:, cc, :], in_=acc, func=AF.Silu,
                                                                bias=CV[:, CV_LB + cc:CV_LB + cc + 1],
                                                                scale=CV[:, CV_LG + cc:CV_LG + cc + 1]),
                    reads=[ar, const_res], writes=[INT_res[cc]])
            bk0, bk1 = balloc(), balloc()
            for cc in range(8):
                bk = bk0 if cc < 4 else bk1
                K.mm(banks[bk][0:30 + TS, (cc % 4) * 128:(cc % 4 + 1) * 128], VT32[:, cc, :], IDN, start=True,
                     stop=True, reads=[vt_res, const_res], out_res=bank_res[bk], sig=(cc % 4 == 3), transpose=True)
            K.A(lambda: nc.scalar.copy(out=TROW[0:30 + TS, 0:512], in_=banks[bk0][0:30 + TS, :]),
                reads=[bank_res[bk0]], writes=[trow_res])
            K.A(lambda: nc.scalar.copy(out=TROW[0:30 + TS, 512:1024], in_=banks[bk1][0:30 + TS, :]),
                reads=[bank_res[bk1]], writes=[trow_res])
            bfree(bk0)
            bfree(bk1)
            K.dma(SP, out_st, o_cmp[p], TROW[0:30, :], reads=[trow_res])
            K.dma(SP, out_st, o_cms[p][:, 29, :], TROW[30:30 + TS, :], reads=[trow_res])
            K.barrier()

            hoff[0] = mix_base
            XP = [halloc([3 + T], BF) for _ in range(2)]
            XT32 = halloc([5, 3 + TS], F32)
            XF = halloc([5, T], BF)
            YT = halloc([3, T], F32)
            ACX = [YT[:, 0, :], YT[:, 1, :]]
            ZT = halloc([3, T], BF)
            XTM = [halloc([512], BF) for _ in range(2)]
            HT = halloc([384], F32)
            HTB = halloc([384], BF)
            HIN = halloc([384], F32)
            HINB = halloc([384], BF)
            MTALL = halloc([4 * 384], BF)
            MT = [MTALL[:, i * 384:(i + 1) * 384].rearrange("p (a b) -> p a b", a=3) for i in range(4)]
            Y1ALL = halloc([768], F32)
            Y1 = [Y1ALL[:, 0:384], Y1ALL[:, 384:768]]
            XROW = Y1ALL[:, 0:640]
            SSTG = Y1ALL[:, 0:640]
            SCTG = halloc([5, TS * 3], F32)
            TM3 = halloc([TS, 3], F32)
            EXPD = MTALL.bitcast(F32).rearrange("p (a b) -> p a b", a=6)
            DTS = halloc([6, TS], F32)
            DX = halloc([3, TS], F32)
            YS = halloc([3, TS], F32)
            SL = [halloc([3, 128], F32) for _ in range(3)] + [SL4]
            XS32 = halloc([2, TS], F32)
            T2 = [halloc([3, 128], F32)] * 2
            SROW = T2[0]
            GSQ = [XP[0][:, 0:T], XP[1][:, 0:T]]
            ACC3 = halloc([5, 3], F32)
            HX = halloc([5, 3], F32)
            HP = halloc([5, 6], F32)
            XP_res = [Res("XP0"), Res("XP1")]
            xt_res = Res("xt32")
            XF_res = [Res("XF%d" % i) for i in range(5)]
            ZT_res = [Res("ZT%d" % i) for i in range(3)]
            YT_res = [Res("YT%d" % i) for i in range(3)]
            ACX_res = [YT_res[0], YT_res[1]]
            XTM_res = [Res("XTM0"), Res("XTM1")]
            ht_res = Res("HT")
            htb_res = Res("HTB")
            hin_res = Res("HIN")
            hinb_res = Res("HINB")
            MT_res = [Res("MT%d" % i) for i in range(4)]
            Y1_res = [Res("Y10"), Res("Y11")]
            sct_res = Res("sctg")
            tm3_res = Res("tm3")
            smp_res = Res("smp")
            ys_res = Res("ys")
            SL_res = [Res("SL0"), Res("SL1"), Res("SL2"), Res("SL3")]
            T2_res = [Res("T20")] * 2
            srow_res = T2_res[0]
            GSQ_res = XP_res
            a3_res = Res("acc3")
            hx_res = Res("hx")
            hp_res = Res("hp")
            xi = [0]
            ci = [0]
            print("H scratch free bytes (SSD):", OFF_X - hoff[0], "const free:", ARENA_B - coff[0])
            K.V(lambda: nc.vector.memset(HP, 0.0), writes=[hp_res])

            for g in range(NG):
                h0 = g * HPG
                chs = [3 * g, 3 * g + 1, 3 * g + 2, 24 + g, 32 + g]
                for (c0, w_, s0) in [(g * 384, 384, 0), (DSSM + g * 128, 128, 384), (DSSM + 1024 + g * 128, 128, 512)]:
                    K.dma(SP, misc_st, SSTG[0:TS * 3, s0:s0 + w_], st_xbc[p][:, c0:c0 + w_], writes=Y1_res)
                bk = balloc()
                for i in range(5):
                    K.mm(banks[bk][:, i * TS * 3:(i + 1) * TS * 3], SSTG[0:TS * 3, i * 128:(i + 1) * 128],
                         IDN[0:TS * 3, 0:TS * 3], start=True, stop=True, reads=Y1_res + [const_res],
                         out_res=bank_res[bk], sig=(i == 4), transpose=True)
                K.A(lambda bk=bk: nc.scalar.copy(out=SCTG.rearrange("p i x -> p (i x)"), in_=banks[bk][:, 0:5 * TS * 3]),
                    reads=[bank_res[bk]], writes=[sct_res])
                bfree(bk)

                def xbc_chunk(i, ch, pw, pres):
                    x = xi[0] % 2
                    xi[0] += 1
                    xp, xr = XP[x], XP_res[x]
                    for ti, (a, b) in enumerate(TT):
                        bk = balloc()
                        for k in range(KC):
                            K.mm(banks[bk][:, 0:b - a], pw[:, k, :], U[:, k, a:b], start=(k == 0), stop=(k == KC - 1),
                                 reads=[pres, U_res], out_res=bank_res[bk], sig=(k == KC - 1))
                        K.A(lambda bk=bk, a=a, b=b, xp=xp: nc.scalar.copy(out=xp[:, 3 + a:3 + b],
                                                                          in_=banks[bk][:, 0:b - a]),
                            reads=[bank_res[bk]], writes=[xr])
                        if ti == len(TT) - 1:
                            K.V(lambda bk=bk, a=a, i=i: nc.vector.tensor_copy(out=XT32[:, i, :],
                                                                              in_=banks[bk][:, TP - 3 - a:T - a]),
                                reads=[bank_res[bk]], writes=[xt_res])
                        bfree(bk)
                    K.V(lambda xp=xp: nc.vector.memset(xp[:, 0:3], 0.0), writes=[xr])
                    acc, ar = ACX[x], ACX_res[x]

                    def swc(j):
                        return CV[:, CV_SW + ch * 4 + j:CV_SW + ch * 4 + j + 1]
                    sbc = CV[:, CV_SB + ch:CV_SB + ch + 1]
                    K.V(lambda: nc.vector.tensor_scalar(out=acc[:, 0:TP], in0=xp[:, 0:TP], scalar1=swc(0), scalar2=sbc,
                                                        op0=ALU.mult, op1=ALU.add), reads=[xr, const_res], writes=[ar])
                    for j in range(1, 4):
                        K.V(lambda j=j: nc.vector.scalar_tensor_tensor(out=acc[:, 0:TP], in0=xp[:, j:j + TP],
                                                                       scalar=swc(j), in1=acc[:, 0:TP], op0=ALU.mult,
                                                                       op1=ALU.add), reads=[xr, const_res], writes=[ar])
                    K.V(lambda: nc.vector.tensor_tensor(
                        out=TM3, in0=SCTG[:, i, :].rearrange("p (b j) -> p b j", j=3),
                        in1=CV[:, CV_SW + ch * 4:CV_SW + ch * 4 + 3].unsqueeze(1).to_broadcast([128, TS, 3]),
                        op=ALU.mult), reads=[sct_res, const_res], writes=[tm3_res])
                    K.V(lambda: nc.vector.tensor_reduce(out=acc[:, TP:T], in_=TM3, axis=AX.X, op=ALU.add),
                        reads=[tm3_res], writes=[ar])
                    K.V(lambda: nc.vector.scalar_tensor_tensor(out=acc[:, TP:T], in0=xp[:, 3 + TP:3 + T], scalar=swc(3),
                                                               in1=acc[:, TP:T], op0=ALU.mult, op1=ALU.add),
                        reads=[xr, const_res], writes=[ar])
                    K.V(lambda: nc.vector.tensor_scalar(out=acc[:, TP:T], in0=acc[:, TP:T], scalar1=sbc, scalar2=None,
                                                        op0=ALU.add), reads=[const_res], writes=[ar])
                    K.V(lambda: nc.vector.tensor_copy(out=ACC3[:, i, :], in_=acc[:, 0:3]), reads=[ar], writes=[a3_res])
                    K.A(lambda: nc.scalar.activation(out=XF[:, i, :], in_=acc, func=AF.Silu), reads=[ar],
                        writes=[XF_res[i]])

                for i in range(3):
                    pi, pres, (pw,) = ws.get("xs")
                    xbc_chunk(i, chs[i], pw, pres)
                    ws.done(pi)
                pi, pres, (pwb, pwc) = ws.get("bc")
                xbc_chunk(3, chs[3], pwb, pres)
                xbc_chunk(4, chs[4], pwc, pres)
                ws.done(pi)
                exchange(XT32[:, :, 0:3], xt_res, 15, HX, hx_res)
                bk, bk2 = balloc(), balloc()
                for i in range(5):
                    bb_ = bk if i < 4 else bk2
                    K.mm(banks[bb_][0:3 + TS, (i % 4) * 128:(i % 4 + 1) * 128], XT32[:, i, :], IDN, start=True,
                         stop=True, reads=[xt_res, const_res], out_res=bank_res[bb_], sig=(i >= 3), transpose=True)
                K.A(lambda bk=bk: nc.scalar.copy(out=XROW[0:3 + TS, 0:512], in_=banks[bk][0:3 + TS, 0:512]),
                    reads=[bank_res[bk]], writes=Y1_res)
                K.A(lambda bk2=bk2: nc.scalar.copy(out=XROW[0:3 + TS, 512:640], in_=banks[bk2][0:3 + TS, 0:128]),
                    reads=[bank_res[bk2]], writes=Y1_res)
                bfree(bk)
                bfree(bk2)
                for (c0, w_, s0) in [(g * 384, 384, 0), (DSSM + g * 128, 128, 384), (DSSM + 1024 + g * 128, 128, 512)]:
                    K.dma(SP, out_st, o_xbp[p][:, c0:c0 + w_], XROW[0:3, s0:s0 + w_], reads=Y1_res)
                    K.dma(SP, out_st, o_xbs[p][:, 2, c0:c0 + w_], XROW[3:3 + TS, s0:s0 + w_], reads=Y1_res)
                for i in range(3):
                    def zevac(ti, a, b, bk, i=i):
                        K.A(lambda: nc.scalar.activation(out=ZT[:, i, a:b], in_=banks[bk][:, 0:b - a], func=AF.Silu),
                            reads=[bank_res[bk]], writes=[ZT_res[i]])
                    linear_fm("z", lambda k, a, b: U[:, k, a:b], lambda k: U_res, KC, zevac)
                K.V(lambda: nc.vector.tensor_scalar(out=HP[:, :, 0:3], in0=HX, scalar1=FLAG[:, 0:1], scalar2=None,
                                                    op0=ALU.mult), reads=[hx_res, const_res], writes=[hp_res])
                for i in range(5):
                    ch = chs[i]
                    for j in range(3):
                        K.V(lambda i=i, j=j, ch=ch: nc.vector.scalar_tensor_tensor(
                            out=ACC3[:, i, :], in0=HP[:, i, j:j + 3],
                            scalar=CV[:, CV_SW + ch * 4 + j:CV_SW + ch * 4 + j + 1], in1=ACC3[:, i, :], op0=ALU.mult,
                            op1=ALU.add), reads=[hp_res, const_res], writes=[a3_res])
                    K.A(lambda i=i: nc.scalar.activation(out=XF[:, i, 0:3], in_=ACC3[:, i, :], func=AF.Silu),
                        reads=[a3_res], writes=[XF_res[i]])
                K.V(lambda: nc.vector.memset(HT, 0.0), writes=[ht_res])
                K.V(lambda: nc.vector.memset(HTB, 0.0), writes=[htb_res])
                dsk = CV[:, CV_BC + 96 + h0:CV_BC + 96 + h0 + HPG].unsqueeze(2).to_broadcast([128, HPG, 64])
                def stage_a(c):
                    cs = slice(c * 128, (c + 1) * 128)
                    x = c % 2
                    bk = balloc()
                    bkv = banks[bk][:, 0:256].bitcast(BF)
                    for i in range(4):
                        K.mm(bkv[:, i * 128:(i + 1) * 128], XF[:, i, cs], IDB, start=True, stop=True,
                             reads=[XF_res[i], const_res], out_res=bank_res[bk], sig=(i == 3), transpose=True)
                    K.A(lambda bkv=bkv, x=x: nc.scalar.copy(out=XTM[x], in_=bkv), reads=[bank_res[bk]],
                        writes=[XTM_res[x]])
                    bfree(bk)
                    xs3 = XTM[x][:, 0:384].rearrange("p (r q) -> p r q", r=HPG)
                    bkc = balloc()
                    K.mm(banks[bkc][:, 0:128], XF[:, 3, cs], XF[:, 4, cs], start=True, stop=True,
                         reads=[XF_res[3], XF_res[4]], out_res=bank_res[bkc], sig=True)
                    K.V(lambda x=x, c=c, xs3=xs3: nc.vector.tensor_tensor(
                        out=XDT[x], in0=xs3, in1=DT[:, c, h0:h0 + HPG].unsqueeze(2).to_broadcast([128, HPG, 64]),
                        op=ALU.mult), reads=[XTM_res[x], dt_res], writes=[XDT_res[x]])
                    K.V(lambda x=x, c=c, xs3=xs3: nc.vector.tensor_tensor(
                        out=XDD[x], in0=xs3, in1=W2[:, c, h0:h0 + HPG].unsqueeze(2).to_broadcast([128, HPG, 64]),
                        op=ALU.mult), reads=[XTM_res[x], dt_res], writes=[XDD_res[x]])
                    bkd = [balloc(), balloc()]
                    for hh in range(2):
                        for j in range(3):
                            h = h0 + hh * 3 + j
                            K.mm(banks[bkd[hh]][:, j * 128:(j + 1) * 128],
                                 DTA[:, c, h:h + 1].to_broadcast([128, 128]), TRI, start=True, stop=False,
                                 reads=[dt_res, const_res], out_res=bank_res[bkd[hh]], sig=False)
                            K.mm(banks[bkd[hh]][:, j * 128:(j + 1) * 128], IDN, NEGM, start=False, stop=True,
                                 reads=[const_res], out_res=bank_res[bkd[hh]], sig=(j == 2))
                    for hh in range(2):
                        bk = bkd[hh]
                        e = E32[hh]
                        for j in range(3):
                            h = h0 + hh * 3 + j
                            K.A(lambda bk=bk, j=j, h=h, e=e, c=c: nc.scalar.activation(
                                out=e[:, j, :], in_=banks[bk][:, j * 128:(j + 1) * 128], func=AF.Exp,
                                bias=NACS[:, c, h:h + 1], scale=1.0),
                                reads=[bank_res[bk], dt_res], writes=[E32_res[hh]])
                        bfree(bk)
                        m = MT[2 * x + hh]
                        K.V(lambda e=e, m=m, bkc=bkc: nc.vector.tensor_tensor(
                            out=m, in0=e, in1=banks[bkc][:, 0:128].unsqueeze(1).to_broadcast([128, 3, 128]),
                            op=ALU.mult), reads=[E32_res[hh], bank_res[bkc]], writes=[MT_res[2 * x + hh]])
                    bfree(bkc)

                def stage_b(c):
                    cs = slice(c * 128, (c + 1) * 128)
                    x = c % 2
                    xs3 = XTM[x][:, 0:384].rearrange("p (r q) -> p r q", r=HPG)
                    bks = balloc()
                    K.mm(banks[bks][:, 0:384], XTM[x][:, 384:512], XDD[x].rearrange("p r q -> p (r q)"), start=True,
                         stop=True, reads=[XTM_res[x], XDD_res[x]], out_res=bank_res[bks], sig=True)
                    bky = balloc()
                    for r in range(HPG):
                        m = MT[2 * x + r // 3]
                        K.mm(banks[bky][:, r * 64:(r + 1) * 64], m[:, r % 3, :], XDT[x][:, r, :], start=True, stop=True,
                             reads=[MT_res[2 * x + r // 3], XDT_res[x]], out_res=bank_res[bky], sig=(r == HPG - 1))
                    bko = None
                    if c > 0:
                        bko = balloc()
                        K.mm(banks[bko][:, 0:384], XF[:, 4, cs], HTB, start=True, stop=True, reads=[XF_res[4], htb_res],
                             out_res=bank_res[bko], sig=True)
                    if c > 0:
                        K.V(lambda c=c: nc.vector.tensor_tensor(
                            out=HT.rearrange("p (r q) -> p r q", r=HPG), in0=HT.rearrange("p (r q) -> p r q", r=HPG),
                            in1=CD[:, c, h0:h0 + HPG].unsqueeze(2).to_broadcast([128, HPG, 64]), op=ALU.mult),
                            reads=[dt_res], writes=[ht_res])
                    K.V(lambda bks=bks: nc.vector.tensor_tensor(out=HT, in0=HT, in1=banks[bks][:, 0:384], op=ALU.add),
                        reads=[bank_res[bks]], writes=[ht_res])
                    bfree(bks)
                    if c < NCH - 1:
                        K.A(lambda: nc.scalar.copy(out=HTB, in_=HT), reads=[ht_res], writes=[htb_res])
                    if c > 0:
                        K.V(lambda bko=bko, x=x, c=c: nc.vector.tensor_tensor(
                            out=Y1[x].rearrange("p (r q) -> p r q", r=HPG),
                            in0=banks[bko][:, 0:384].rearrange("p (r q) -> p r q", r=HPG),
                            in1=EXA[:, c, h0:h0 + HPG].unsqueeze(2).to_broadcast([128, HPG, 64]), op=ALU.mult),
                            reads=[bank_res[bko], dt_res], writes=[Y1_res[x]])
                        bfree(bko)
                        K.V(lambda bky=bky, x=x: nc.vector.tensor_tensor(out=Y1[x], in0=Y1[x],
                                                                         in1=banks[bky][:, 0:384], op=ALU.add),
                            reads=[bank_res[bky]], writes=[Y1_res[x]])
                    else:
                        K.A(lambda bky=bky, x=x: nc.scalar.copy(out=Y1[x], in_=banks[bky][:, 0:384]),
                            reads=[bank_res[bky]], writes=[Y1_res[x]])
                    bfree(bky)

                def stage_b_tail(c):
                    cs = slice(c * 128, (c + 1) * 128)
                    x = c % 2
                    bk = balloc()
                    for i in range(3):
                        K.mm(banks[bk][:, i * 128:(i + 1) * 128], Y1[x][:, i * 128:(i + 1) * 128], IDN, start=True,
                             stop=True, reads=[Y1_res[x], const_res], out_res=bank_res[bk], sig=(i == 2),
                             transpose=True)
                    K.A(lambda bk=bk, cs=cs: nc.scalar.copy(out=YT[:, :, cs],
                                                            in_=banks[bk][:, 0:384].rearrange("p (i q) -> p i q", i=3)),
                        reads=[bank_res[bk]], writes=YT_res)
                    bfree(bk)

                stage_a(0)
                for c in range(NCH):
                    if c + 1 < NCH:
                        stage_a(c + 1)
                    if c > 0:
                        stage_b_tail(c - 1)
                    stage_b(c)
                stage_b_tail(NCH - 1)
                exchange(HT, ht_res, 384, HIN, hin_res)
                for i in range(3):
                    K.V(lambda i=i: nc.vector.scalar_tensor_tensor(
                        out=YT[:, i, 0:TP], in0=XF[:, i, 0:TP], scalar=CV[:, CV_DF + 3 * g + i:CV_DF + 3 * g + i + 1],
                        in1=YT[:, i, 0:TP], op0=ALU.mult, op1=ALU.add),
                        reads=[XF_res[i], const_res], writes=[YT_res[i]])
                K.V(lambda: nc.vector.tensor_copy(
                    out=EXPD[0:TS, 0:3, :].rearrange("p a (h q) -> p (a h) q", h=2),
                    in_=DT[0:TS, 8, h0:h0 + HPG].unsqueeze(2).to_broadcast([TS, HPG, 64])),
                    reads=[dt_res], writes=[smp_res] + MT_res)
                K.V(lambda: nc.vector.tensor_copy(
                    out=EXPD[0:TS, 3:6, :].rearrange("p a (h q) -> p (a h) q", h=2),
                    in_=DECS[0:TS, h0:h0 + HPG].unsqueeze(2).to_broadcast([TS, HPG, 64])),
                    reads=[dt_res], writes=[smp_res] + MT_res)
                bk = balloc()
                for j in range(6):
                    K.mm(banks[bk][:, j * TS:(j + 1) * TS], EXPD[0:TS, j, :], IDN[0:TS, 0:TS], start=True, stop=True,
                         reads=[smp_res, const_res] + MT_res, out_res=bank_res[bk], sig=(j == 5), transpose=True)
                K.A(lambda bk=bk: nc.scalar.copy(out=DTS, in_=banks[bk][:, 0:6 * TS].rearrange("p (j b) -> p j b", j=6)),
                    reads=[bank_res[bk]], writes=[smp_res])
                bfree(bk)
                K.V(lambda: nc.vector.tensor_tensor(out=DX, in0=DTS[:, 0:3, :], in1=XF[:, 0:3, TP:T], op=ALU.mult),
                    reads=[smp_res, XF_res[0], XF_res[1], XF_res[2]], writes=[smp_res])
                K.V(lambda: nc.vector.tensor_copy(out=XS32, in_=XF[:, 3:5, TP:T]), reads=[XF_res[3], XF_res[4]],
                    writes=[smp_res])
                def slab_load(bb):
                    K.dma(SP, slab_st[bb % 4], SL[bb % 4],
                          st_ssm[p][bb, g * 384:(g + 1) * 384, :].rearrange("(i q) n -> q i n", q=128),
                          writes=[SL_res[bb % 4]])
                for bb in range(3):
                    slab_load(bb)

                def bc_prep(bb):
                    bk_ = balloc()
                    for w_ in range(2):
                        K.mm(banks[bk_][:, w_ * 128:(w_ + 1) * 128],
                             XS32[:, w_, bb:bb + 1].to_broadcast([128, 128]), IDN, start=True, stop=True,
                             reads=[smp_res, const_res], out_res=bank_res[bk_], sig=(w_ == 1))
                    return bk_
                nxt_bk = bc_prep(0)
                for b_ in range(TS):
                    x = b_ % 2
                    x3 = b_ % 4
                    sl, slr = SL[x3], SL_res[x3]
                    t2, t2r = T2[0], T2_res[0]
                    bk = nxt_bk
                    if b_ + 1 < TS:
                        nxt_bk = bc_prep(b_ + 1)
                    for blk in range(3):
                        K.A(lambda bk=bk, b_=b_, blk=blk: nc.scalar.activation(
                            out=t2[:, blk, :], in_=banks[bk][:, 0:128], func=AF.Copy, scale=DX[:, blk, b_:b_ + 1]),
                            reads=[bank_res[bk], smp_res], writes=[t2r])
                    for blk in range(3):
                        K.V(lambda b_=b_, sl=sl, blk=blk: nc.vector.scalar_tensor_tensor(
                            out=sl[:, blk, :], in0=sl[:, blk, :], scalar=DTS[:, 3 + blk, b_:b_ + 1], in1=t2[:, blk, :],
                            op0=ALU.mult, op1=ALU.add), reads=[smp_res, t2r], writes=[slr])
                    K.dma(SP, sout_st, o_sss[p][b_, g * 384:(g + 1) * 384, :].rearrange("(i q) n -> q i n", q=128), sl,
                          reads=[slr])
                    for blk in range(3):
                        K.V(lambda bk=bk, b_=b_, sl=sl, blk=blk: nc.vector.scalar_tensor_tensor(
                            out=Y1[x][:, blk * 128:(blk + 1) * 128], in0=sl[:, blk, :], scalar=1.0,
                            in1=banks[bk][:, 128:256], op0=ALU.mult, op1=ALU.mult, accum_out=YS[:, blk, b_:b_ + 1]),
                            reads=[slr, bank_res[bk]], writes=[Y1_res[x], ys_res])
                    bfree(bk)
                    if b_ + 3 < TS:
                        slab_load(b_ + 3)
                for i in range(3):
                    K.V(lambda i=i: nc.vector.scalar_tensor_tensor(
                        out=YT[:, i, TP:T], in0=XF[:, i, TP:T], scalar=CV[:, CV_DF + 3 * g + i:CV_DF + 3 * g + i + 1],
                        in1=YS[:, i, :], op0=ALU.mult, op1=ALU.add),
                        reads=[XF_res[i], ys_res, const_res], writes=[YT_res[i]])
                K.V(lambda: nc.vector.tensor_scalar(out=HIN, in0=HIN, scalar1=FLAG[:, 0:1], scalar2=None, op0=ALU.mult),
                    reads=[const_res], writes=[hin_res])
                K.A(lambda: nc.scalar.copy(out=HINB, in_=HIN), reads=[hin_res], writes=[hinb_res])
                for c in range(NCH):
                    cs = slice(c * 128, (c + 1) * 128)
                    x = c % 2
                    bko = balloc()
                    K.mm(banks[bko][:, 0:384], XF[:, 4, cs], HINB, start=True, stop=True, reads=[XF_res[4], hinb_res],
                         out_res=bank_res[bko], sig=True)
                    K.V(lambda bko=bko, x=x, c=c: nc.vector.tensor_tensor(
                        out=Y1[x].rearrange("p (r q) -> p r q", r=HPG),
                        in0=banks[bko][:, 0:384].rearrange("p (r q) -> p r q", r=HPG),
                        in1=ETOT[:, c, h0:h0 + HPG].unsqueeze(2).to_broadcast([128, HPG, 64]), op=ALU.mult),
                        reads=[bank_res[bko], dt_res], writes=[Y1_res[x]])
                    bfree(bko)
                    bk = balloc()
                    for i in range(3):
                        K.mm(banks[bk][:, i * 128:(i + 1) * 128], Y1[x][:, i * 128:(i + 1) * 128], IDN, start=True,
                             stop=True, reads=[Y1_res[x], const_res], out_res=bank_res[bk], sig=(i == 2),
                             transpose=True)
                    K.V(lambda bk=bk, cs=cs: nc.vector.tensor_tensor(
                        out=YT[:, :, cs], in0=YT[:, :, cs], in1=banks[bk][:, 0:384].rearrange("p (i q) -> p i q", i=3),
                        op=ALU.add), reads=[bank_res[bk]], writes=YT_res)
                    bfree(bk)
                K.V(lambda: nc.vector.tensor_tensor(
                    out=HIN.rearrange("p (r q) -> p r q", r=HPG), in0=HIN.rearrange("p (r q) -> p r q", r=HPG),
                    in1=CDTOT[:, h0:h0 + HPG].unsqueeze(2).to_broadcast([128, HPG, 64]), op=ALU.mult),
                    reads=[dt_res, hinb_res], writes=[hin_res])
                K.V(lambda: nc.vector.tensor_tensor(out=HT, in0=HT, in1=HIN, op=ALU.add), reads=[hin_res],
                    writes=[ht_res])
                bk = balloc()
                for i in range(3):
                    K.mm(banks[bk][:, i * 128:(i + 1) * 128], HT[:, i * 128:(i + 1) * 128], IDN, start=True, stop=True,
                         reads=[ht_res, const_res], out_res=bank_res[bk], sig=(i == 2), transpose=True)
                K.A(lambda bk=bk: nc.scalar.copy(out=SROW, in_=banks[bk][:, 0:384].rearrange("p (i q) -> p i q", i=3)),
                    reads=[bank_res[bk]], writes=[srow_res])
                bfree(bk)
                K.dma(SP, out_st, o_ssp[p][g * 384:(g + 1) * 384, :].rearrange("(i q) n -> q i n", q=128), SROW,
                      reads=[srow_res])
                for i in range(3):
                    K.V(lambda i=i: nc.vector.tensor_tensor(out=YT[:, i, :], in0=YT[:, i, :], in1=ZT[:, i, :],
                                                            op=ALU.mult), reads=[ZT_res[i]], writes=[YT_res[i]])
                bks = [balloc() for _ in TT]
                for i in range(3):
                    x = i % 2
                    K.A(lambda i=i, x=x: nc.scalar.activation(out=GSQ[x], in_=YT[:, i, :], func=AF.Square),
                        reads=[YT_res[i]], writes=[GSQ_res[x]])
                    for ti, (a, b) in enumerate(TT):
                        K.mm(banks[bks[ti]][:, 0:b - a], ONB, GSQ[x][:, a:b], start=(i == 0), stop=(i == 2),
                             reads=[GSQ_res[x], const_res], out_res=bank_res[bks[ti]], sig=(ti == len(TT) - 1))
                for ti, (a, b) in enumerate(TT):
                    K.A(lambda ti=ti, a=a, b=b: nc.scalar.activation(out=RS[:, a:b], in_=banks[bks[ti]][:, 0:b - a],
                                                                    func=AF.Ln, bias=EPSC[:, 0:1], scale=1.0 / 384),
                        reads=[bank_res[bks[ti]], const_res], writes=[RS_res])
                K.A(lambda: nc.scalar.activation(out=RS, in_=RS, func=AF.Exp, scale=-0.5), reads=[RS_res],
                    writes=[RS_res])
                for b_ in bks:
                    bfree(b_)
                for i in range(3):
                    K.V(lambda i=i: nc.vector.scalar_tensor_tensor(
                        out=INT[:, 8 + 3 * g + i, :], in0=YT[:, i, :],
                        scalar=CV[:, CV_NG + 3 * g + i:CV_NG + 3 * g + i + 1], in1=RS, op0=ALU.mult, op1=ALU.mult),
                        reads=[YT_res[i], RS_res, const_res], writes=[INT_res[8 + 3 * g + i]])
            K.barrier()
            for d in range(KC):
                def oevac(ti, a, b, bk, d=d):
                    K.A(lambda: nc.scalar.copy(out=H[:, d, a:b], in_=banks[bk][:, 0:b - a]), reads=[bank_res[bk]],
                        writes=[H_res[d]])
                linear_fm("out", lambda k, a, b: INT[:, k, a:b], lambda k: INT_res[k], 32, oevac)
            K.barrier()
            post_residual(lambda k: CV[:, CV_G + 3 * 16 + k:CV_G + 3 * 16 + k + 1])

        def ple(p):
            prenorm(6)
            spill_H()
            K.barrier()
            PT = view(OFF_X, [2, T], BF)
            pt_res = Res("PT")
            K.dma(POOL, pt_st, PT, pT[p].rearrange("k q t -> q k t"), writes=[pt_res])
            si = [0]
            for d in range(KC):
                pi, pres, (pg, pp) = ws.get("ple")
                for ti, (a, b) in enumerate(TT):
                    n = b - a
                    bg, bp = balloc(), balloc()
                    for k in range(KC):
                        K.mm(banks[bg][:, 0:n], pg[:, k, :], U[:, k, a:b], start=(k == 0), stop=(k == KC - 1),
                             reads=[pres, U_res], out_res=bank_res[bg], sig=False)
                    for k in range(2):
                        K.mm(banks[bp][:, 0:n], pp[:, k, :], PT[:, k, a:b], start=(k == 0), stop=(k == 1),
                             reads=[pres, pt_res], out_res=bank_res[bp], sig=(k == 1))
                    s = si[0] % 2
                    si[0] += 1
                    K.A(lambda bg=bg, n=n, s=s: nc.scalar.activation(out=SS[s][:, 0:n], in_=banks[bg][:, 0:n],
                                                                     func=AF.Sigmoid),
                        reads=[bank_res[bg]], writes=[SS_res[s]])
                    K.V(lambda bp=bp, n=n, s=s, d=d, a=a, b=b: nc.vector.tensor_tensor(
                        out=H[:, d, a:b], in0=SS[s][:, 0:n], in1=banks[bp][:, 0:n], op=ALU.mult),
                        reads=[SS_res[s], bank_res[bp]], writes=[H_res[d]])
                    bfree(bg)
                    bfree(bp)
                ws.done(pi)
            K.barrier()
            post_residual(lambda k: CV[:, CV_G + 7 * 16 + k:CV_G + 7 * 16 + k + 1],
                          after_k=lambda k: K.dma(ACT, yout_st, yT[p, k], H[:, k, :], reads=[H_res[k]]))

        for p in range(NP):
            for k in range(KC):
                K.dma(SP, ld_st, H[:, k, :], xT[p, k], writes=[H_res[k]])
            ffn("w_ffn1", 0, 0)
            if stage >= 2:
                mixer(p)
            if stage >= 3:
                ffn("w_ffn2", 4, 1)
            if stage >= 4:
                ple(p)
            K.barrier()
            if stage < 4:
                for k in range(KC):
                    K.dma(SP, out_st, yT[p, k], H[:, k, :], reads=[H_res[k]])
            K.barrier()
        SP.raw.wait_ge(out_st.sem, 16 * out_st.n)
    return nc


def _cvec(inp):
    cv = np.zeros((128, CV_N), np.float32)
    names = ["norm_ffn1_pre", "norm_ffn1_post", "norm_mix_pre", "norm_mix_post",
             "norm_ffn2_pre", "norm_ffn2_post", "norm_ple_pre", "norm_ple_post"]
    for i, n in enumerate(names):
        cv[:, CV_G + i * 16:CV_G + (i + 1) * 16] = inp[n].reshape(16, 128).T
    cw = inp["conv_mod_w"].reshape(31, 8, 128)
    cv[:, CV_CW:CV_CW + 248] = cw.transpose(2, 1, 0).reshape(128, 248)
    cv[:, CV_CB:CV_CB + 8] = inp["conv_mod_b"].reshape(8, 128).T
    cv[:, CV_LG:CV_LG + 8] = inp["conv_mod_ln_g"].reshape(8, 128).T
    cv[:, CV_LB:CV_LB + 8] = inp["conv_mod_ln_b"].reshape(8, 128).T
    sw = inp["ssm_conv_w"].reshape(4, 40, 128)
    cv[:, CV_SW:CV_SW + 160] = sw.transpose(2, 1, 0).reshape(128, 160)
    cv[:, CV_SB:CV_SB + 40] = inp["ssm_conv_b"].reshape(40, 128).T
    cv[:, CV_NG:CV_NG + 24] = inp["ssm_norm_g"].reshape(24, 128).T
    cv[:, CV_BC:CV_BC + 48] = inp["dt_bias"].reshape(1, 48)
    cv[:, CV_BC + 48:CV_BC + 96] = inp["a_log"].reshape(1, 48)
    cv[:, CV_BC + 96:CV_BC + 144] = inp["d_skip"].reshape(1, 48)
    cv[:, CV_DF:CV_DF + 24] = np.repeat(inp["d_skip"].reshape(48), 64).reshape(24, 128).T
    return cv


def run(inputs, stage=99, NP=1, TS=16):
    inp = {k: np.ascontiguousarray(np.asarray(v, dtype=np.float32)) for k, v in inputs.items()}
    T = TP + TS
    xp = inp["x_prompt"]
    xs = inp["x_sample"].reshape(128, D)
    pp = inp["p_prompt"].reshape(4, 2048, 256)
    psm = inp["p_sample"].reshape(128, 256)
    shared = {"cvec": _cvec(inp), "ident": np.eye(128, dtype=np.float32),
              "tri": np.triu(np.ones((128, 128), np.float32))}
    for nm in ["w_ffn1_gate", "w_ffn1_up", "w_ffn1_down", "w_in", "w_out", "w_ffn2_gate", "w_ffn2_up",
               "w_ffn2_down", "w_ple_gate", "w_ple_proj"]:
        shared[nm] = inp[nm][0]
    in_maps = []
    for c in range(8):
        s = c // 2
        hf = c % 2
        m = dict(shared)
        m["flag"] = np.full((128, 1), float(hf), np.float32)
        m["negm"] = ((np.triu(np.ones((128, 128), np.float32)) - 1.0) * 30000.0).astype(np.float32)
        xTc = np.zeros((NP, KC, 128, T), np.float32)
        pTc = np.zeros((NP, 2, 128, T), np.float32)
        stc = np.zeros((NP, TS, 30, DCV), np.float32)
        stx = np.zeros((NP, TS * 3, DXBC), np.float32)
        sts = np.zeros((NP, TS, DSSM, DST), np.float32)
        for p in range(NP):
            b0 = (c * NP + p) * TS
            xt = np.concatenate([xp[s, hf * TP:(hf + 1) * TP], xs[b0:b0 + TS]], 0)
            xTc[p] = xt.T.reshape(KC, 128, T)
            pt = np.concatenate([pp[s, hf * TP:(hf + 1) * TP], psm[b0:b0 + TS]], 0)
            pTc[p] = pt.T.reshape(2, 128, T)
            stc[p] = inp["state_conv_mod"][0, b0:b0 + TS]
            stx[p] = inp["state_ssm_conv"][0, b0:b0 + TS].reshape(TS * 3, DXBC)
            sts[p] = inp["state_ssm"][0, b0:b0 + TS].reshape(TS, DSSM, DST)
        m.update({"xT": xTc, "pT": pTc, "st_conv": stc, "st_xbc": stx, "st_ssm": sts})
        in_maps.append(m)
    nc = build_program(NP, TS, stage)
    res = run_bass_kernel_spmd(nc, in_maps, core_ids=list(range(8)))
    R = res.results
    y_prompt = np.zeros((4, 2048, D), np.float32)
    y_sample = np.zeros((128, 1, D), np.float32)
    cmp_ = np.zeros((1, 4, 30, DCV), np.float32)
    xbp = np.zeros((1, 4, 3, DXBC), np.float32)
    ssp = np.zeros((1, 4, NH, 64, DST), np.float32)
    cms = np.zeros((1, 128, 30, DCV), np.float32)
    xbs = np.zeros((1, 128, 3, DXBC), np.float32)
    sss = np.zeros((1, 128, NH, 64, DST), np.float32)
    for c in range(8):
        r = R[c]
        s = c // 2
        for p in range(NP):
            b0 = (c * NP + p) * TS
            yt = r["yT"][p].reshape(D, T).T
            y_prompt[s, (c % 2) * TP:(c % 2 + 1) * TP] = yt[:TP]
            y_sample[b0:b0 + TS, 0] = yt[TP:]
            cms[0, b0:b0 + TS] = r["o_cms"][p]
            xbs[0, b0:b0 + TS] = r["o_xbs"][p]
            sss[0, b0:b0 + TS] = r["o_sss"][p].reshape(TS, NH, 64, DST)
        if c % 2 == 1:
            cmp_[0, s] = r["o_cmp"][NP - 1]
            xbp[0, s] = r["o_xbp"][NP - 1]
            ssp[0, s] = r["o_ssp"][NP - 1].reshape(NH, 64, DST)
    return (y_prompt, y_sample, cmp_, xbp, ssp, cms, xbs, sss)


def kernel(**inputs):
    return run(inputs)
```

```python
import numpy as np
import concourse.bass as bass
import concourse.mybir as mybir
from concourse.bass_utils import run_bass_kernel_spmd
from contextlib import ExitStack

F32 = mybir.dt.float32
BF = mybir.dt.bfloat16
ALU = mybir.AluOpType
AF = mybir.ActivationFunctionType
AX = mybir.AxisListType

D = 2048
KC = 16
DFF = 5632
NHALF = 22
DCV = 1024
DSSM = 3072
DXBC = 5120
NH = 48
NG = 8
HPG = 6
DST = 128
EPS = 1e-6
S_Z = 2048
S_X = 5120
S_B = 8192
S_C = 9216
S_DT = 10240
TP = 1024
NCH = 8
NSLOT = 3

CV_G = 0
CV_CW = 128
CV_CB = CV_CW + 248
CV_LG = CV_CB + 8
CV_LB = CV_LG + 8
CV_SW = CV_LB + 8
CV_SB = CV_SW + 160
CV_NG = CV_SB + 40
CV_BC = CV_NG + 24
CV_DF = CV_BC + 144
CV_N = CV_DF + 24


class Res:
    __slots__ = ("name", "w", "r", "excl")

    def __init__(self, name, excl=False):
        self.name = name
        self.w = None
        self.r = []
        self.excl = excl


class Eng:
    def __init__(self, raw, sem, name):
        self.raw = raw
        self.sem = sem
        self.name = name
        self.n = 0
        self.waited = {}


class Stream:
    def __init__(self, sem, name):
        self.sem = sem
        self.name = name
        self.n = 0
        self.inc = 16


class Builder:
    def __init__(self, nc, es):
        self.nc = nc
        self.es = es
        self.nsem = 0
        self.pe = Eng(nc.tensor, self.sem("pe"), "pe")
        self.act = Eng(nc.scalar, self.sem("act"), "act")
        self.dve = Eng(nc.vector, self.sem("dve"), "dve")
        self.pool = Eng(nc.gpsimd, self.sem("pool"), "pool")
        self.sp = Eng(nc.sync, self.sem("sp"), "sp")
        self.engs = [self.pe, self.act, self.dve, self.pool, self.sp]
        self.streams = []
        self.pe_pending = []
        self.pe_out_pending = []

    def sem(self, name):
        self.nsem += 1
        return self.es.enter_context(self.nc.semaphore(name))

    def stream(self, name):
        s = Stream(self.sem("d_" + name), name)
        self.streams.append(s)
        return s

    def wait(self, eng, tok):
        if tok is None:
            return
        if tok[0] == "e":
            src, val = tok[1], tok[2]
            if src is eng and eng is self.pe:
                return
            sem = src.sem
            key = src.name
        else:
            st = tok[1]
            sem = st.sem
            val = st.inc * st.n
            key = "d_" + st.name
        if eng.waited.get(key, 0) >= val:
            return
        eng.raw.wait_ge(sem, val)
        eng.waited[key] = val

    def _pre(self, eng, reads, writes):
        for r in reads:
            self.wait(eng, r.w)
            if r.excl:
                for t in r.r:
                    if t[0] != "e" or t[1] is not eng:
                        self.wait(eng, t)
        for w in writes:
            assert w not in self.pe_pending or eng is self.pe, "PE pending read on " + w.name
            self.wait(eng, w.w)
            for t in w.r:
                self.wait(eng, t)

    def _post(self, tok, reads, writes):
        for r in reads:
            r.r.append(tok)
            if len(r.r) > 24:
                r.r = r.r[-24:]
        for w in writes:
            w.w = tok
            w.r = []

    def op(self, eng, fn, reads=(), writes=()):
        self._pre(eng, reads, writes)
        ins = fn()
        ins.then_inc(eng.sem, 1)
        eng.n += 1
        tok = ("e", eng, eng.n)
        self._post(tok, reads, writes)
        return tok

    def A(self, fn, reads=(), writes=()):
        return self.op(self.act, fn, reads, writes)

    def V(self, fn, reads=(), writes=()):
        return self.op(self.dve, fn, reads, writes)

    def dma(self, q, st, out, in_, reads=(), writes=()):
        self._pre(q, reads, writes)
        ins = q.raw.dma_start(out=out, in_=in_)
        ins.then_inc(st.sem, 16)
        st.n += 1
        tok = ("d", st, st.n)
        self._post(tok, reads, writes)
        return tok

    def mm(self, out, lhsT, rhs, start, stop, reads, out_res, sig, transpose=False):
        pe = self.pe
        for r in reads:
            self.wait(pe, r.w)
            if r not in self.pe_pending:
                self.pe_pending.append(r)
        if start:
            self.wait(pe, out_res.w)
            for t in out_res.r:
                self.wait(pe, t)
        if transpose:
            ins = pe.raw.transpose(out=out, in_=lhsT, identity=rhs)
        else:
            ins = pe.raw.matmul(out, lhsT, rhs, start=start, stop=stop)
        if stop and out_res not in self.pe_out_pending:
            self.pe_out_pending.append(out_res)
        if sig:
            ins.then_inc(pe.sem, 1)
            pe.n += 1
            tok = ("e", pe, pe.n)
            for r in self.pe_pending:
                r.r.append(tok)
            self.pe_pending = []
            for o in self.pe_out_pending:
                o.w = tok
                o.r = []
            self.pe_out_pending = []
        return ins

    def barrier(self, engs=None):
        assert not self.pe_pending and not self.pe_out_pending
        engs = engs or [self.pe, self.act, self.dve, self.sp]
        for e in engs:
            for f in [self.pe, self.act, self.dve]:
                if f is not e and f.n > 0:
                    self.wait(e, ("e", f, f.n))
            for st in self.streams:
                if st.n > 0 and not st.name.startswith("slot"):
                    self.wait(e, ("d", st, st.n))


def tt_tiles(T):
    a = (T + 2) // 3
    b = (T - a + 1) // 2
    return [(0, a), (a, a + b), (a + b, T)]


def build_program(NP, TS, stage=99):
    T = TP + TS
    TT = tt_tiles(T)
    nc = bass.Bass("TRN2", target_bir_lowering=False)

    def din(name, shape):
        return nc.dram_tensor(name, list(shape), F32, kind="ExternalInput").ap()

    def dout(name, shape):
        return nc.dram_tensor(name, list(shape), F32, kind="ExternalOutput").ap()

    xT = din("xT", [NP, KC, 128, T])
    pT = din("pT", [NP, 2, 128, T])
    st_conv = din("st_conv", [NP, TS, 30, DCV])
    st_xbc = din("st_xbc", [NP, TS * 3, DXBC])
    st_ssm = din("st_ssm", [NP, TS, DSSM, DST])
    cvec_d = din("cvec", [128, CV_N])
    ident_d = din("ident", [128, 128])
    tri_d = din("tri", [128, 128])
    flag_d = din("flag", [128, 1])
    negm_d = din("negm", [128, 128])
    W = {}
    for nm, shp in [("w_ffn1_gate", (D, DFF)), ("w_ffn1_up", (D, DFF)), ("w_ffn1_down", (DFF, D)),
                    ("w_in", (D, 10288)), ("w_out", (4096, D)),
                    ("w_ffn2_gate", (D, DFF)), ("w_ffn2_up", (D, DFF)), ("w_ffn2_down", (DFF, D)),
                    ("w_ple_gate", (D, D)), ("w_ple_proj", (256, D))]:
        W[nm] = din(nm, shp)
    yT = dout("yT", [NP, KC, 128, T])
    o_cms = dout("o_cms", [NP, TS, 30, DCV])
    o_xbs = dout("o_xbs", [NP, TS, 3, DXBC])
    o_sss = dout("o_sss", [NP, TS, DSSM, DST])
    o_cmp = dout("o_cmp", [NP, 30, DCV])
    o_xbp = dout("o_xbp", [NP, 3, DXBC])
    o_ssp = dout("o_ssp", [NP, DSSM, DST])
    hsp = nc.dram_tensor("hsp", [KC, 128, T], F32, kind="Internal").ap()

    with ExitStack() as es:
        K = Builder(nc, es)
        PE, ACT, DVE, POOL, SP = K.pe, K.act, K.dve, K.pool, K.sp

        ARENA_B = 188000
        arena = es.enter_context(nc.sbuf_tensor("arena", [128, ARENA_B // 4], F32))
        slots = [es.enter_context(nc.sbuf_tensor("slot%d" % i, [128, 4096], BF)) for i in range(NSLOT)]
        slot_res = [Res("slot%d" % i) for i in range(NSLOT)]
        slot_st = [K.stream("slot%d" % i) for i in range(NSLOT)]
        banks = [es.enter_context(nc.psum_tensor("bank%d" % i, [128, 512], F32)) for i in range(8)]
        bank_res = [Res("bank%d" % i, excl=True) for i in range(8)]
        free_banks = list(range(8))

        def balloc():
            assert free_banks, "out of PSUM banks"
            return free_banks.pop(0)

        def bfree(b):
            free_banks.append(b)

        def view(off, shape, dt):
            esz = 4 if dt == F32 else 2
            n = int(np.prod(shape))
            nb = (n * esz + 3) // 4 * 4
            assert off % 4 == 0
            assert off + nb <= ARENA_B, (off, nb)
            ap = arena[:, off // 4: off // 4 + nb // 4]
            if dt != F32:
                ap = ap.bitcast(dt)[:, 0:n]
            if len(shape) == 2:
                return ap.rearrange("p (a b) -> p a b", a=shape[0])
            if len(shape) == 3:
                return ap.rearrange("p (a b c) -> p a b c", a=shape[0], b=shape[1])
            return ap

        OFF_U = 0
        OFF_H = OFF_U + 2 * KC * T
        OFF_X = OFF_H + 4 * KC * T
        OFF_C = OFF_X + 4 * KC * T
        U = view(OFF_U, [KC, T], BF)
        H = view(OFF_H, [KC, T], F32)
        AH = view(OFF_X, [NHALF, T], BF)
        INT = view(OFF_X, [32, T], BF)
        U_res = Res("U")
        H_res = [Res("H%d" % i) for i in range(KC)]
        AH_res = [Res("AH%d" % i) for i in range(NHALF)]
        INT_res = [Res("INT%d" % i) for i in range(32)]
        hsp_res = [Res("hsp%d" % i) for i in range(KC)]

        coff = [OFF_C]

        def calloc(shape, dt):
            esz = 4 if dt == F32 else 2
            n = int(np.prod(shape)) * esz
            n = (n + 3) // 4 * 4
            o = coff[0]
            coff[0] += n
            return view(o, shape, dt)

        CV = calloc([CV_N], F32)
        IDN = calloc([128], F32)
        TRI = calloc([128], F32)
        ONE = calloc([128], F32)
        IDB = calloc([128], BF)
        ONB = calloc([128], BF)
        EPSC = calloc([1], F32)
        GH = calloc([2 * KC], F32)
        ANEG = calloc([NH], F32)
        RS = calloc([T], F32)
        SQ = [calloc([T], BF) for _ in range(2)]
        SQ_res = [Res("SQ0"), Res("SQ1")]
        RS_res = Res("RS")
        const_res = Res("const")
        OFF_XS = OFF_X + 2 * NHALF * T
        SS = [view(OFF_XS + i * 4 * 352, [352], F32) for i in range(2)]
        SS_res = [Res("SS0"), Res("SS1")]
        HST = [view(OFF_XS + 2 * 4 * 352 + i * 4 * T, [T], F32) for i in range(2)]
        HST_res = [Res("HST0"), Res("HST1")]
        assert OFF_XS + 2 * 4 * 352 + 2 * 4 * T <= OFF_C

        ld_st = K.stream("ld")
        sp_st = K.stream("spill")
        hst_st = [K.stream("hst0"), K.stream("hst1")]
        out_st = K.stream("out")

        K.dma(SP, ld_st, CV, cvec_d, writes=[const_res])
        K.dma(SP, ld_st, IDN, ident_d, writes=[const_res])
        K.dma(SP, ld_st, TRI, tri_d, writes=[const_res])
        K.V(lambda: nc.vector.memset(ONE, 1.0), writes=[const_res])
        K.V(lambda: nc.vector.memset(ONB, 1.0), writes=[const_res])
        K.V(lambda: nc.vector.memset(EPSC, EPS), writes=[const_res])
        K.V(lambda: nc.vector.tensor_copy(out=IDB, in_=IDN), reads=[const_res], writes=[const_res])
        K.V(lambda: nc.vector.tensor_scalar(out=GH[:, 0:KC], in0=CV[:, CV_G + 16:CV_G + 32], scalar1=0.5,
                                            scalar2=None, op0=ALU.mult), reads=[const_res], writes=[const_res])
        K.V(lambda: nc.vector.tensor_scalar(out=GH[:, KC:2 * KC], in0=CV[:, CV_G + 80:CV_G + 96], scalar1=0.5,
                                            scalar2=None, op0=ALU.mult), reads=[const_res], writes=[const_res])
        K.A(lambda: nc.scalar.activation(out=ANEG, in_=CV[:, CV_BC + 48:CV_BC + 96], func=AF.Exp),
            reads=[const_res], writes=[const_res])
        K.V(lambda: nc.vector.tensor_scalar(out=ANEG, in0=ANEG, scalar1=-1.0, scalar2=None, op0=ALU.mult),
            reads=[const_res], writes=[const_res])

        class WS:
            def __init__(self):
                self.plan = []
                self.issued = 0
                self.next_i = 0
                self.done_upto = 0

            def pump(self):
                while self.issued < len(self.plan) and self.issued < self.done_upto + NSLOT:
                    i = self.issued
                    s = i % NSLOT
                    off = 0
                    for ent in self.plan[i][1]:
                        w, k0, kc, f0 = ent[:4]
                        fw = ent[4] if len(ent) > 4 else 128
                        dst = slots[s][:, off:off + kc * fw].rearrange("p (k f) -> p k f", k=kc)
                        src = w[k0 * 128:(k0 + kc) * 128, f0:f0 + fw].rearrange("(k p) f -> p k f", p=128)
                        K.dma(POOL, slot_st[s], dst, src, writes=[slot_res[s]])
                        off += kc * fw
                    self.issued += 1

            def get(self, tag):
                i = self.next_i
                assert self.plan[i][0] == tag, (self.plan[i][0], tag)
                self.pump()
                assert i < self.issued
                self.next_i += 1
                s = i % NSLOT
                views = []
                off = 0
                for ent in self.plan[i][1]:
                    w, k0, kc, f0 = ent[:4]
                    fw = ent[4] if len(ent) > 4 else 128
                    views.append(slots[s][:, off:off + kc * fw].rearrange("p (k f) -> p k f", k=kc))
                    off += kc * fw
                return i, slot_res[s], views

            def done(self, i):
                assert i == self.done_upto
                self.done_upto += 1
                self.pump()

        ws = WS()

        def plan_ffn(pre):
            wg, wu, wd = W[pre + "_gate"], W[pre + "_up"], W[pre + "_down"]
            for hf in range(2):
                for f in range(NHALF):
                    fc = hf * NHALF + f
                    ws.plan.append((pre + "gu", [(wg, 0, KC, fc * 128), (wu, 0, KC, fc * 128)]))
                for d in range(KC):
                    ws.plan.append((pre + "dn", [(wd, hf * NHALF, NHALF, d * 128)]))

        def plan_mixer():
            wi = W["w_in"]
            for cc in range(8):
                ws.plan.append(("glu", [(wi, 0, KC, cc * 128), (wi, 0, KC, 1024 + cc * 128)]))
            ws.plan.append(("dt", [(wi, 0, KC, S_DT, NH)]))
            for g in range(NG):
                for i in range(3):
                    ws.plan.append(("xs", [(wi, 0, KC, S_X + g * 384 + i * 128)]))
                ws.plan.append(("bc", [(wi, 0, KC, S_B + g * 128), (wi, 0, KC, S_C + g * 128)]))
                for i in range(3):
                    ws.plan.append(("z", [(wi, 0, KC, S_Z + g * 384 + i * 128)]))
            for d in range(KC):
                ws.plan.append(("out", [(W["w_out"], 0, 32, d * 128)]))

        def plan_ple():
            for d in range(KC):
                ws.plan.append(("ple", [(W["w_ple_gate"], 0, KC, d * 128), (W["w_ple_proj"], 0, 2, d * 128)]))

        for p in range(NP):
            plan_ffn("w_ffn1")
            if stage >= 2:
                plan_mixer()
            if stage >= 3:
                plan_ffn("w_ffn2")
            if stage >= 4:
                plan_ple()

        def stats_to_RS(src_fn, n_k, src_res_fn, scale):
            bks = [balloc() for _ in TT]
            for k in range(n_k):
                sq = SQ[k % 2]
                sr = SQ_res[k % 2]
                K.A(lambda k=k, sq=sq: nc.scalar.activation(out=sq, in_=src_fn(k), func=AF.Square),
                    reads=[src_res_fn(k)], writes=[sr])
                for ti, (a, b) in enumerate(TT):
                    K.mm(banks[bks[ti]][:, 0:b - a], ONB, sq[:, a:b], start=(k == 0), stop=(k == n_k - 1),
                         reads=[sr, const_res], out_res=bank_res[bks[ti]], sig=(ti == len(TT) - 1))
            for ti, (a, b) in enumerate(TT):
                K.A(lambda ti=ti, a=a, b=b: nc.scalar.activation(out=RS[:, a:b], in_=banks[bks[ti]][:, 0:b - a],
                                                                func=AF.Ln, bias=EPSC[:, 0:1], scale=scale),
                    reads=[bank_res[bks[ti]], const_res], writes=[RS_res])
            K.A(lambda: nc.scalar.activation(out=RS, in_=RS, func=AF.Exp, scale=-0.5), reads=[RS_res], writes=[RS_res])
            for b in bks:
                bfree(b)

        def prenorm(gi):
            stats_to_RS(lambda k: H[:, k, :], KC, lambda k: H_res[k], 1.0 / D)
            for k in range(KC):
                K.V(lambda k=k: nc.vector.scalar_tensor_tensor(out=U[:, k, :], in0=H[:, k, :],
                                                               scalar=CV[:, CV_G + gi * 16 + k:CV_G + gi * 16 + k + 1],
                                                               in1=RS, op0=ALU.mult, op1=ALU.mult),
                    reads=[H_res[k], RS_res, const_res], writes=[U_res])

        def spill_H():
            for k in range(KC):
                K.dma(SP, sp_st, hsp[k], H[:, k, :], reads=[H_res[k]], writes=[hsp_res[k]])

        def post_residual(gcol_fn, after_k=None):
            stats_to_RS(lambda k: H[:, k, :], KC, lambda k: H_res[k], 1.0 / D)
            for k in range(KC):
                K.dma(SP, hst_st[k % 2], HST[k % 2], hsp[k], reads=[hsp_res[k]], writes=[HST_res[k % 2]])
                K.V(lambda k=k: nc.vector.scalar_tensor_tensor(out=H[:, k, :], in0=H[:, k, :], scalar=gcol_fn(k),
                                                               in1=RS, op0=ALU.mult, op1=ALU.mult),
                    reads=[RS_res, const_res], writes=[H_res[k]])
                K.V(lambda k=k: nc.vector.tensor_tensor(out=H[:, k, :], in0=H[:, k, :], in1=HST[k % 2], op=ALU.add),
                    reads=[HST_res[k % 2]], writes=[H_res[k]])
                if after_k is not None:
                    after_k(k)

        def ffn(pre, g_pre, gh_idx):
            prenorm(g_pre)
            spill_H()
            ssi = [0]
            for hf in range(2):
                for f in range(NHALF):
                    pi, pres, (pg, pu) = ws.get(pre + "gu")
                    for ti, (a, b) in enumerate(TT):
                        n = b - a
                        bg, bu = balloc(), balloc()
                        for k in range(KC):
                            K.mm(banks[bg][:, 0:n], pg[:, k, :], U[:, k, a:b], start=(k == 0), stop=(k == KC - 1),
                                 reads=[pres, U_res], out_res=bank_res[bg], sig=False)
                        for k in range(KC):
                            K.mm(banks[bu][:, 0:n], pu[:, k, :], U[:, k, a:b], start=(k == 0), stop=(k == KC - 1),
                                 reads=[pres, U_res], out_res=bank_res[bu], sig=(k == KC - 1))
                        s = ssi[0] % 2
                        ssi[0] += 1
                        K.A(lambda bg=bg, n=n, s=s: nc.scalar.activation(out=SS[s][:, 0:n], in_=banks[bg][:, 0:n],
                                                                         func=AF.Silu),
                            reads=[bank_res[bg]], writes=[SS_res[s]])
                        K.V(lambda bu=bu, n=n, s=s, f=f, a=a, b=b: nc.vector.tensor_tensor(
                            out=AH[:, f, a:b], in0=SS[s][:, 0:n], in1=banks[bu][:, 0:n], op=ALU.mult),
                            reads=[SS_res[s], bank_res[bu]], writes=[AH_res[f]])
                        bfree(bg)
                        bfree(bu)
                    ws.done(pi)
                for d in range(KC):
                    pi, pres, (pd,) = ws.get(pre + "dn")
                    for ti, (a, b) in enumerate(TT):
                        n = b - a
                        bk = balloc()
                        for k in range(NHALF):
                            K.mm(banks[bk][:, 0:n], pd[:, k, :], AH[:, k, a:b], start=(k == 0), stop=(k == NHALF - 1),
                                 reads=[pres, AH_res[k]], out_res=bank_res[bk], sig=(k == NHALF - 1))
                        if hf == 0:
                            K.A(lambda bk=bk, n=n, d=d, a=a, b=b: nc.scalar.copy(out=H[:, d, a:b], in_=banks[bk][:, 0:n]),
                                reads=[bank_res[bk]], writes=[H_res[d]])
                        else:
                            K.V(lambda bk=bk, n=n, d=d, a=a, b=b: nc.vector.tensor_tensor(
                                out=H[:, d, a:b], in0=H[:, d, a:b], in1=banks[bk][:, 0:n], op=ALU.add),
                                reads=[bank_res[bk]], writes=[H_res[d]])
                        bfree(bk)
                    ws.done(pi)
            post_residual(lambda k: GH[:, gh_idx * KC + k:gh_idx * KC + k + 1])

        XDT = [calloc([6, 64], BF) for _ in range(2)]
        XDD = [calloc([6, 64], BF) for _ in range(2)]
        E32 = [calloc([3, 128], BF) for _ in range(2)]
        XDT_res = [Res("XDT0"), Res("XDT1")]
        XDD_res = [Res("XDD0"), Res("XDD1")]
        E32_res = [Res("E320"), Res("E321")]
        SL4 = calloc([3, 128], F32)
        FLAG = calloc([1], F32)
        K.dma(SP, ld_st, FLAG, flag_d, writes=[const_res])
        NEGM = calloc([128], F32)
        K.dma(SP, ld_st, NEGM, negm_d, writes=[const_res])
        assert coff[0] <= ARENA_B, coff[0]
        misc_st = K.stream("misc")
        slab_st = [K.stream("slab%d" % i) for i in range(4)]
        sout_st = K.stream("sout")
        yout_st = K.stream("yout")
        pt_st = K.stream("ptld")
        xs_st = K.stream("xsrc")
        xd_st = K.stream("xdst")
        cc_st = K.stream("cc")
        cc_st.inc = 1
        PAIRS = [[0, 1], [2, 3], [4, 5], [6, 7]]
        xch_i = [0]

        def exchange(src_ap, src_res, width, dst_ap, dst_res):
            i = xch_i[0]
            xch_i[0] += 1
            xsrc = nc.dram_tensor("xsrc%d" % i, [128, width], F32, kind="Internal").ap()
            xdst = nc.dram_tensor("xdst%d" % i, [256, width], F32, kind="Internal").ap()
            r1, r2 = Res("xsrc%d" % i), Res("xdst%d" % i)
            xs_v, xd_v = xsrc, xdst[0:128, :]
            if len(src_ap.shape) == 3:
                xs_v = xsrc.rearrange("p (a b) -> p a b", a=src_ap.shape[1])
                xd_v = xdst[0:128, :].rearrange("p (a b) -> p a b", a=src_ap.shape[1])
            K.dma(SP, xs_st, xs_v, src_ap, reads=[src_res], writes=[r1])
            K._pre(POOL, [r1], [r2])
            ins = nc.gpsimd.collective_compute("AllGather", ALU.bypass, replica_groups=PAIRS, ins=[xsrc],
                                               outs=[xdst])
            ins.then_inc(cc_st.sem, 1)
            cc_st.n += 1
            tok = ("d", cc_st, cc_st.n)
            K._post(tok, [r1], [r2])
            K.dma(SP, xd_st, dst_ap, xd_v, reads=[r2], writes=[dst_res])

        def linear_fm(tag, in_fn, in_res_fn, nk, evac):
            pi, pres, (pw,) = ws.get(tag)
            for ti, (a, b) in enumerate(TT):
                bk = balloc()
                for k in range(nk):
                    K.mm(banks[bk][:, 0:b - a], pw[:, k, :], in_fn(k, a, b), start=(k == 0), stop=(k == nk - 1),
                         reads=[pres, in_res_fn(k)], out_res=bank_res[bk], sig=(k == nk - 1))
                evac(ti, a, b, bk)
                bfree(bk)
            ws.done(pi)

        def mixer(p):
            prenorm(2)
            spill_H()
            K.barrier()
            hoff = [OFF_H]

            def halloc(shape, dt):
                esz = 4 if dt == F32 else 2
                n = (int(np.prod(shape)) * esz + 3) // 4 * 4
                o = hoff[0]
                hoff[0] += n
                assert hoff[0] <= OFF_X, ("H scratch overflow", hoff[0] - OFF_X)
                return view(o, shape, dt)

            LA = TT[-1][0]
            assert LA <= TP - 30
            DT = halloc([9, NH], F32)
            DTA = halloc([9, NH], F32)
            NACS = halloc([8, NH], F32)
            EXA = halloc([8, NH], F32)
            CD = halloc([8, NH], F32)
            W2 = halloc([8, NH], F32)
            ETOT = halloc([8, NH], F32)
            CDTOT = halloc([NH], F32)
            DECS = halloc([NH], F32)
            dt_res = Res("dt")
            mix_base = hoff[0]

            Vb = halloc([8, 30 + T], BF)
            MSS = [halloc([352], F32) for _ in range(2)]
            VT32 = halloc([8, 30 + TS], F32)
            HV32 = halloc([8, 30], F32)
            ACC = [halloc([T], F32) for _ in range(2)]
            CBF = [halloc([T], BF)] * 2
            CSQ = [halloc([T], BF)] * 2
            MEAN = halloc([T], F32)
            RSTD = halloc([T], F32)
            DGC = view(hoff[0] - 8 * T, [31, 128], BF)
            assert 31 * 128 * 2 <= 8 * T
            STC = [halloc([TS, 30], F32) for _ in range(2)]
            TMPS = halloc([TS, 30], F32)
            NB4 = TS // 4
            STG = [view(OFF_X + 8 * 2 * T + 4 * 8 * T + 4096 + i * 4 * NB4 * 128, [NB4, 128], F32) for i in range(2)]
            assert OFF_X + 8 * 2 * T + 4 * 8 * T + 4096 + 2 * 4 * NB4 * 128 <= OFF_C
            CO = view(OFF_X + 8 * 2 * T, [8, T], F32)
            TROW = view(OFF_X + 8 * 2 * T + 4 * 8 * T, [1024], F32)
            assert OFF_X + 8 * 2 * T + 4 * 8 * T + 4096 <= OFF_C
            Vb_res = [Res("Vb%d" % i) for i in range(8)]
            MSS_res = [Res("MSS0"), Res("MSS1")]
            vt_res = Res("vt32")
            hv_res = Res("hv32")
            ACC_res = [Res("ACC0"), Res("ACC1")]
            CBF_res = [Res("CBF0")] * 2
            CSQ_res = [Res("CSQ0")] * 2
            STG_res = [Res("STG0"), Res("STG1")]
            STC_res = [Res("stc0"), Res("stc1")]
            CO_res = [Res("CO%d" % i) for i in range(8)]
            ln_res = Res("ln")
            trow_res = Res("trow")
            tmps_res = Res("tmps")

            def cwc(cc, j):
                return CV[:, CV_CW + cc * 31 + j:CV_CW + cc * 31 + j + 1]

            mi = [0]
            for cc in range(8):
                pi, pres, (pa, pb) = ws.get("glu")
                for ti, (a, b) in enumerate(TT):
                    n = b - a
                    ba, bb = balloc(), balloc()
                    for k in range(KC):
                        K.mm(banks[ba][:, 0:n], pa[:, k, :], U[:, k, a:b], start=(k == 0), stop=(k == KC - 1),
                             reads=[pres, U_res], out_res=bank_res[ba], sig=False)
                    for k in range(KC):
                        K.mm(banks[bb][:, 0:n], pb[:, k, :], U[:, k, a:b], start=(k == 0), stop=(k == KC - 1),
                             reads=[pres, U_res], out_res=bank_res[bb], sig=(k == KC - 1))
                    s = mi[0] % 2
                    mi[0] += 1
                    K.A(lambda bb=bb, n=n, s=s: nc.scalar.activation(out=MSS[s][:, 0:n], in_=banks[bb][:, 0:n],
                                                                     func=AF.Sigmoid),
                        reads=[bank_res[bb]], writes=[MSS_res[s]])
                    K.V(lambda ba=ba, n=n, s=s, cc=cc, a=a, b=b: nc.vector.tensor_tensor(
                        out=Vb[:, cc, 30 + a:30 + b], in0=banks[ba][:, 0:n], in1=MSS[s][:, 0:n], op=ALU.mult),
                        reads=[bank_res[ba], MSS_res[s]], writes=[Vb_res[cc]])
                    if ti == len(TT) - 1:
                        K.V(lambda ba=ba, s=s, cc=cc, a=a: nc.vector.tensor_tensor(
                            out=VT32[:, cc, :], in0=banks[ba][:, TP - 30 - a:T - a], in1=MSS[s][:, TP - 30 - a:T - a],
                            op=ALU.mult), reads=[bank_res[ba], MSS_res[s]], writes=[vt_res])
                    bfree(ba)
                    bfree(bb)
                ws.done(pi)
            exchange(VT32[:, :, 0:30], vt_res, 240, HV32, hv_res)

            K.V(lambda: nc.vector.memset(DT, 0.0), writes=[dt_res])
            K.V(lambda: nc.vector.memset(DTA, 0.0), writes=[dt_res])
            pi, pres, (pdt,) = ws.get("dt")
            for c in range(9):
                M = 128 if c < 8 else TS
                bk = balloc()
                for k in range(KC):
                    K.mm(banks[bk][0:M, 0:NH], U[:, k, c * 128:c * 128 + M], pdt[:, k, :], start=(k == 0),
                         stop=(k == KC - 1), reads=[pres, U_res], out_res=bank_res[bk], sig=(k == KC - 1))
                K.V(lambda c=c, M=M, bk=bk: nc.vector.tensor_tensor(out=DT[0:M, c, :], in0=banks[bk][0:M, 0:NH],
                                                                     in1=CV[0:M, CV_BC:CV_BC + NH], op=ALU.add),
                    reads=[bank_res[bk], const_res], writes=[dt_res])
                bfree(bk)
            ws.done(pi)
            K.A(lambda: nc.scalar.activation(out=DT, in_=DT, func=AF.Exp), reads=[dt_res], writes=[dt_res])
            K.A(lambda: nc.scalar.activation(out=DT, in_=DT, func=AF.Ln, bias=1.0, scale=1.0), reads=[dt_res],
                writes=[dt_res])
            K.V(lambda: nc.vector.tensor_tensor(out=DTA, in0=DT, in1=ANEG.unsqueeze(1).to_broadcast([128, 9, NH]),
                                                op=ALU.mult), reads=[dt_res, const_res], writes=[dt_res])
            b1, b2 = balloc(), balloc()
            for c in range(8):
                K.mm(banks[b1][:, c * NH:(c + 1) * NH], TRI, DTA[:, c, :], start=True, stop=True,
                     reads=[dt_res, const_res], out_res=bank_res[b1], sig=False)
            for c in range(8):
                K.mm(banks[b2][:, c * NH:(c + 1) * NH], ONE, DTA[:, c, :], start=True, stop=True,
                     reads=[dt_res, const_res], out_res=bank_res[b2], sig=(c == 7))
            K.A(lambda: nc.scalar.activation(out=EXA, in_=banks[b1][:, 0:8 * NH], func=AF.Exp),
                reads=[bank_res[b1]], writes=[dt_res])
            K.V(lambda: nc.vector.tensor_scalar(out=NACS, in0=banks[b1][:, 0:8 * NH], scalar1=-1.0, scalar2=None,
                                                op0=ALU.mult), reads=[bank_res[b1]], writes=[dt_res])
            K.A(lambda: nc.scalar.activation(out=CD, in_=banks[b2][:, 0:8 * NH], func=AF.Exp),
                reads=[bank_res[b2]], writes=[dt_res])
            K.V(lambda: nc.vector.tensor_tensor(out=W2, in0=banks[b2][:, 0:8 * NH], in1=NACS, op=ALU.add),
                reads=[bank_res[b2], dt_res], writes=[dt_res])
            K.A(lambda: nc.scalar.activation(out=W2, in_=W2, func=AF.Exp), reads=[dt_res], writes=[dt_res])
            K.V(lambda: nc.vector.tensor_tensor(out=W2, in0=W2, in1=DT[:, 0:8, :], op=ALU.mult), reads=[dt_res],
                writes=[dt_res])
            K.A(lambda: nc.scalar.activation(out=DECS[0:TS, :], in_=DTA[0:TS, 8, :], func=AF.Exp), reads=[dt_res],
                writes=[dt_res])
            bfree(b1)
            bfree(b2)
            K.V(lambda: nc.vector.tensor_copy(out=ETOT[:, 0, :], in_=EXA[:, 0, :]), reads=[dt_res], writes=[dt_res])
            K.V(lambda: nc.vector.tensor_copy(out=CDTOT, in_=CD[:, 0, :]), reads=[dt_res], writes=[dt_res])
            for c in range(1, 8):
                K.V(lambda c=c: nc.vector.tensor_tensor(out=ETOT[:, c, :], in0=EXA[:, c, :], in1=CDTOT, op=ALU.mult),
                    reads=[dt_res], writes=[dt_res])
                K.V(lambda c=c: nc.vector.tensor_tensor(out=CDTOT, in0=CDTOT, in1=CD[:, c, :], op=ALU.mult),
                    reads=[dt_res], writes=[dt_res])

            K.dma(SP, out_st, o_xbs[p][:, 0:2, :], st_xbc[p].rearrange("(b j) c -> b j c", j=3)[:, 1:3, :])
            K.dma(SP, out_st, o_cms[p][:, 0:29, :], st_conv[p][:, 1:30, :])

            K.V(lambda: nc.vector.tensor_scalar(out=Vb[:, :, 0:30], in0=HV32, scalar1=FLAG[:, 0:1], scalar2=None,
                                                op0=ALU.mult), reads=[hv_res, const_res], writes=Vb_res)
            for cc in range(8):
                stc, stcr = STC[cc % 2], STC_res[cc % 2]
                sg, sr = STG[cc % 2], STG_res[cc % 2]
                for q4 in range(NB4):
                    K.dma(SP, misc_st, sg[0:120, q4, :],
                          st_conv[p][q4 * 4:(q4 + 1) * 4, :, cc * 128:(cc + 1) * 128].rearrange("b j c -> (b j) c"),
                          writes=[sr])
                bk = balloc()
                for q4 in range(NB4):
                    K.mm(banks[bk][:, q4 * 120:(q4 + 1) * 120], sg[0:120, q4, :], IDN[0:120, 0:120], start=True,
                         stop=True, reads=[sr, const_res], out_res=bank_res[bk], sig=(q4 == NB4 - 1), transpose=True)
                K.A(lambda bk=bk, stc=stc: nc.scalar.copy(out=stc.rearrange("p b j -> p (b j)"),
                                                          in_=banks[bk][:, 0:TS * 30]),
                    reads=[bank_res[bk]], writes=[stcr])
                bfree(bk)
                acc = ACC[cc % 2]
                ar = ACC_res[cc % 2]
                K.V(lambda cc=cc: nc.vector.tensor_tensor(
                    out=DGC, in0=IDB.unsqueeze(1).to_broadcast([128, 31, 128]),
                    in1=CV[:, CV_CW + cc * 31:CV_CW + cc * 31 + 31].unsqueeze(2).to_broadcast([128, 31, 128]),
                    op=ALU.mult), reads=[const_res], writes=[ln_res])
                for ti, (a, b) in enumerate(TT):
                    n = b - a
                    bk = balloc()
                    for j in range(31):
                        K.mm(banks[bk][:, 0:n], DGC[:, j, :], Vb[:, cc, j + a:j + b], start=(j == 0), stop=(j == 30),
                             reads=[ln_res, Vb_res[cc]], out_res=bank_res[bk], sig=(j == 30))
                    pe_ = min(b, TP)
                    K.A(lambda cc=cc, bk=bk, a=a, pe_=pe_: nc.scalar.activation(
                        out=CO[:, cc, a:pe_], in_=banks[bk][:, 0:pe_ - a], func=AF.Identity,
                        bias=CV[:, CV_CB + cc:CV_CB + cc + 1], scale=1.0),
                        reads=[bank_res[bk], const_res, STG_res[0], STG_res[1]], writes=[CO_res[cc]])
                    bfree(bk)
                K.V(lambda cc=cc, stc=stc: nc.vector.tensor_tensor(
                    out=TMPS, in0=stc,
                    in1=CV[:, CV_CW + cc * 31:CV_CW + cc * 31 + 30].unsqueeze(1).to_broadcast([128, TS, 30]),
                    op=ALU.mult), reads=[stcr, const_res], writes=[tmps_res])
                K.V(lambda acc=acc: nc.vector.tensor_reduce(out=acc[:, TP:T], in_=TMPS, axis=AX.X, op=ALU.add),
                    reads=[tmps_res], writes=[ar])
                K.V(lambda cc=cc, acc=acc: nc.vector.scalar_tensor_tensor(
                    out=acc[:, TP:T], in0=Vb[:, cc, 30 + TP:30 + T], scalar=cwc(cc, 30), in1=acc[:, TP:T],
                    op0=ALU.mult, op1=ALU.add), reads=[Vb_res[cc], const_res], writes=[ar])
                K.V(lambda cc=cc, acc=acc: nc.vector.tensor_scalar(
                    out=acc[:, TP:T], in0=acc[:, TP:T], scalar1=CV[:, CV_CB + cc:CV_CB + cc + 1], scalar2=None,
                    op0=ALU.add), reads=[const_res], writes=[ar])
                K.A(lambda cc=cc, acc=acc: nc.scalar.copy(out=CO[:, cc, TP:T], in_=acc[:, TP:T]),
                    reads=[ar, STG_res[0], STG_res[1]], writes=[CO_res[cc]])
            bm = [balloc() for _ in TT]
            bq = [balloc() for _ in TT]
            for cc in range(8):
                x = cc % 2
                K.A(lambda cc=cc, x=x: nc.scalar.copy(out=CBF[x], in_=CO[:, cc, :]), reads=[CO_res[cc]],
                    writes=[CBF_res[x]])
                K.A(lambda cc=cc, x=x: nc.scalar.activation(out=CSQ[x], in_=CO[:, cc, :], func=AF.Square),
                    reads=[CO_res[cc]], writes=[CSQ_res[x]])
                for ti, (a, b) in enumerate(TT):
                    K.mm(banks[bm[ti]][:, 0:b - a], ONB, CBF[x][:, a:b], start=(cc == 0), stop=(cc == 7),
                         reads=[CBF_res[x], const_res], out_res=bank_res[bm[ti]], sig=False)
                for ti, (a, b) in enumerate(TT):
                    K.mm(banks[bq[ti]][:, 0:b - a], ONB, CSQ[x][:, a:b], start=(cc == 0), stop=(cc == 7),
                         reads=[CSQ_res[x], const_res], out_res=bank_res[bq[ti]], sig=(ti == len(TT) - 1))
            for ti, (a, b) in enumerate(TT):
                K.A(lambda ti=ti, a=a, b=b: nc.scalar.mul(out=MEAN[:, a:b], in_=banks[bm[ti]][:, 0:b - a],
                                                          mul=1.0 / DCV), reads=[bank_res[bm[ti]]], writes=[ln_res])
                K.V(lambda ti=ti, a=a, b=b: nc.vector.tensor_scalar(out=RSTD[:, a:b], in0=banks[bq[ti]][:, 0:b - a],
                                                                    scalar1=1.0 / DCV, scalar2=None, op0=ALU.mult),
                    reads=[bank_res[bq[ti]]], writes=[ln_res])
            for b_ in bm + bq:
                bfree(b_)
            K.V(lambda: nc.vector.tensor_tensor(out=ACC[0], in0=MEAN, in1=MEAN, op=ALU.mult), reads=[ln_res],
                writes=[ACC_res[0]])
            K.V(lambda: nc.vector.tensor_tensor(out=RSTD, in0=RSTD, in1=ACC[0], op=ALU.subtract),
                reads=[ACC_res[0], ln_res], writes=[ln_res])
            K.A(lambda: nc.scalar.activation(out=RSTD, in_=RSTD, func=AF.Ln, bias=EPSC[:, 0:1], scale=1.0),
                reads=[ln_res, const_res], writes=[ln_res])
            K.A(lambda: nc.scalar.activation(out=RSTD, in_=RSTD, func=AF.Exp, scale=-0.5), reads=[ln_res],
                writes=[ln_res])
            for cc in range(8):
                acc = ACC[cc % 2]
                ar = ACC_res[cc % 2]
                K.V(lambda cc=cc, acc=acc: nc.vector.tensor_tensor(out=acc, in0=CO[:, cc, :], in1=MEAN,
                                                                   op=ALU.subtract),
                    reads=[CO_res[cc], ln_res], writes=[ar])
                K.V(lambda acc=acc: nc.vector.tensor_tensor(out=acc, in0=acc, in1=RSTD, op=ALU.mult),
                    reads=[ln_res], writes=[ar])
                K.A(lambda cc=cc, acc=acc: nc.scalar.activation(out=INT[:, cc, :], in_=acc, func=AF.Silu,
                                                                bias=CV[:, CV_LB + cc:CV_LB + cc + 1],
                                                                scale=CV[:, CV_LG + cc:CV_LG + cc + 1]),
                    reads=[ar, const_res], writes=[INT_res[cc]])
            bk0, bk1 = balloc(), balloc()
            for cc in range(8):
                bk = bk0 if cc < 4 else bk1
                K.mm(banks[bk][0:30 + TS, (cc % 4) * 128:(cc % 4 + 1) * 128], VT32[:, cc, :], IDN, start=True,
                     stop=True, reads=[vt_res, const_res], out_res=bank_res[bk], sig=(cc % 4 == 3), transpose=True)
            K.A(lambda: nc.scalar.copy(out=TROW[0:30 + TS, 0:512], in_=banks[bk0][0:30 + TS, :]),
                reads=[bank_res[bk0]], writes=[trow_res])
            K.A(lambda: nc.scalar.copy(out=TROW[0:30 + TS, 512:1024], in_=banks[bk1][0:30 + TS, :]),
                reads=[bank_res[bk1]], writes=[trow_res])
            bfree(bk0)
            bfree(bk1)
            K.dma(SP, out_st, o_cmp[p], TROW[0:30, :], reads=[trow_res])
            K.dma(SP, out_st, o_cms[p][:, 29, :], TROW[30:30 + TS, :], reads=[trow_res])
            K.barrier()

            hoff[0] = mix_base
            XP = [halloc([3 + T], BF) for _ in range(2)]
            XT32 = halloc([5, 3 + TS], F32)
            XF = halloc([5, T], BF)
            YT = halloc([3, T], F32)
            ACX = [YT[:, 0, :], YT[:, 1, :]]
            ZT = halloc([3, T], BF)
            XTM = [halloc([512], BF) for _ in range(2)]
            HT = halloc([384], F32)
            HTB = halloc([384], BF)
            HIN = halloc([384], F32)
            HINB = halloc([384], BF)
            MTALL = halloc([4 * 384], BF)
            MT = [MTALL[:, i * 384:(i + 1) * 384].rearrange("p (a b) -> p a b", a=3) for i in range(4)]
            Y1ALL = halloc([768], F32)
            Y1 = [Y1ALL[:, 0:384], Y1ALL[:, 384:768]]
            XROW = Y1ALL[:, 0:640]
            SSTG = Y1ALL[:, 0:640]
            SCTG = halloc([5, TS * 3], F32)
            TM3 = halloc([TS, 3], F32)
            EXPD = MTALL.bitcast(F32).rearrange("p (a b) -> p a b", a=6)
            DTS = halloc([6, TS], F32)
            DX = halloc([3, TS], F32)
            YS = halloc([3, TS], F32)
            SL = [halloc([3, 128], F32) for _ in range(3)] + [SL4]
            XS32 = halloc([2, TS], F32)
            T2 = [halloc([3, 128], F32)] * 2
            SROW = T2[0]
            GSQ = [XP[0][:, 0:T], XP[1][:, 0:T]]
            ACC3 = halloc([5, 3], F32)
            HX = halloc([5, 3], F32)
            HP = halloc([5, 6], F32)
            XP_res = [Res("XP0"), Res("XP1")]
            xt_res = Res("xt32")
            XF_res = [Res("XF%d" % i) for i in range(5)]
            ZT_res = [Res("ZT%d" % i) for i in range(3)]
            YT_res = [Res("YT%d" % i) for i in range(3)]
            ACX_res = [YT_res[0], YT_res[1]]
            XTM_res = [Res("XTM0"), Res("XTM1")]
            ht_res = Res("HT")
            htb_res = Res("HTB")
            hin_res = Res("HIN")
            hinb_res = Res("HINB")
            MT_res = [Res("MT%d" % i) for i in range(4)]
            Y1_res = [Res("Y10"), Res("Y11")]
            sct_res = Res("sctg")
            tm3_res = Res("tm3")
            smp_res = Res("smp")
            ys_res = Res("ys")
            SL_res = [Res("SL0"), Res("SL1"), Res("SL2"), Res("SL3")]
            T2_res = [Res("T20")] * 2
            srow_res = T2_res[0]
            GSQ_res = XP_res
            a3_res = Res("acc3")
            hx_res = Res("hx")
            hp_res = Res("hp")
            xi = [0]
            ci = [0]
            print("H scratch free bytes (SSD):", OFF_X - hoff[0], "const free:", ARENA_B - coff[0])
            K.V(lambda: nc.vector.memset(HP, 0.0), writes=[hp_res])

            for g in range(NG):
                h0 = g * HPG
                chs = [3 * g, 3 * g + 1, 3 * g + 2, 24 + g, 32 + g]
                for (c0, w_, s0) in [(g * 384, 384, 0), (DSSM + g * 128, 128, 384), (DSSM + 1024 + g * 128, 128, 512)]:
                    K.dma(SP, misc_st, SSTG[0:TS * 3, s0:s0 + w_], st_xbc[p][:, c0:c0 + w_], writes=Y1_res)
                bk = balloc()
                for i in range(5):
                    K.mm(banks[bk][:, i * TS * 3:(i + 1) * TS * 3], SSTG[0:TS * 3, i * 128:(i + 1) * 128],
                         IDN[0:TS * 3, 0:TS * 3], start=True, stop=True, reads=Y1_res + [const_res],
                         out_res=bank_res[bk], sig=(i == 4), transpose=True)
                K.A(lambda bk=bk: nc.scalar.copy(out=SCTG.rearrange("p i x -> p (i x)"), in_=banks[bk][:, 0:5 * TS * 3]),
                    reads=[bank_res[bk]], writes=[sct_res])
                bfree(bk)

                def xbc_chunk(i, ch, pw, pres):
                    x = xi[0] % 2
                    xi[0] += 1
                    xp, xr = XP[x], XP_res[x]
                    for ti, (a, b) in enumerate(TT):
                        bk = balloc()
                        for k in range(KC):
                            K.mm(banks[bk][:, 0:b - a], pw[:, k, :], U[:, k, a:b], start=(k == 0), stop=(k == KC - 1),
                                 reads=[pres, U_res], out_res=bank_res[bk], sig=(k == KC - 1))
                        K.A(lambda bk=bk, a=a, b=b, xp=xp: nc.scalar.copy(out=xp[:, 3 + a:3 + b],
                                                                          in_=banks[bk][:, 0:b - a]),
                            reads=[bank_res[bk]], writes=[xr])
                        if ti == len(TT) - 1:
                            K.V(lambda bk=bk, a=a, i=i: nc.vector.tensor_copy(out=XT32[:, i, :],
                                                                              in_=banks[bk][:, TP - 3 - a:T - a]),
                                reads=[bank_res[bk]], writes=[xt_res])
                        bfree(bk)
                    K.V(lambda xp=xp: nc.vector.memset(xp[:, 0:3], 0.0), writes=[xr])
                    acc, ar = ACX[x], ACX_res[x]

                    def swc(j):
                        return CV[:, CV_SW + ch * 4 + j:CV_SW + ch * 4 + j + 1]
                    sbc = CV[:, CV_SB + ch:CV_SB + ch + 1]
                    K.V(lambda: nc.vector.tensor_scalar(out=acc[:, 0:TP], in0=xp[:, 0:TP], scalar1=swc(0), scalar2=sbc,
                                                        op0=ALU.mult, op1=ALU.add), reads=[xr, const_res], writes=[ar])
                    for j in range(1, 4):
                        K.V(lambda j=j: nc.vector.scalar_tensor_tensor(out=acc[:, 0:TP], in0=xp[:, j:j + TP],
                                                                       scalar=swc(j), in1=acc[:, 0:TP], op0=ALU.mult,
                                                                       op1=ALU.add), reads=[xr, const_res], writes=[ar])
                    K.V(lambda: nc.vector.tensor_tensor(
                        out=TM3, in0=SCTG[:, i, :].rearrange("p (b j) -> p b j", j=3),
                        in1=CV[:, CV_SW + ch * 4:CV_SW + ch * 4 + 3].unsqueeze(1).to_broadcast([128, TS, 3]),
                        op=ALU.mult), reads=[sct_res, const_res], writes=[tm3_res])
                    K.V(lambda: nc.vector.tensor_reduce(out=acc[:, TP:T], in_=TM3, axis=AX.X, op=ALU.add),
                        reads=[tm3_res], writes=[ar])
                    K.V(lambda: nc.vector.scalar_tensor_tensor(out=acc[:, TP:T], in0=xp[:, 3 + TP:3 + T], scalar=swc(3),
                                                               in1=acc[:, TP:T], op0=ALU.mult, op1=ALU.add),
                        reads=[xr, const_res], writes=[ar])
                    K.V(lambda: nc.vector.tensor_scalar(out=acc[:, TP:T], in0=acc[:, TP:T], scalar1=sbc, scalar2=None,
                                                        op0=ALU.add), reads=[const_res], writes=[ar])
                    K.V(lambda: nc.vector.tensor_copy(out=ACC3[:, i, :], in_=acc[:, 0:3]), reads=[ar], writes=[a3_res])
                    K.A(lambda: nc.scalar.activation(out=XF[:, i, :], in_=acc, func=AF.Silu), reads=[ar],
                        writes=[XF_res[i]])

                for i in range(3):
                    pi, pres, (pw,) = ws.get("xs")
                    xbc_chunk(i, chs[i], pw, pres)
                    ws.done(pi)
                pi, pres, (pwb, pwc) = ws.get("bc")
                xbc_chunk(3, chs[3], pwb, pres)
                xbc_chunk(4, chs[4], pwc, pres)
                ws.done(pi)
                exchange(XT32[:, :, 0:3], xt_res, 15, HX, hx_res)
                bk, bk2 = balloc(), balloc()
                for i in range(5):
                    bb_ = bk if i < 4 else bk2
                    K.mm(banks[bb_][0:3 + TS, (i % 4) * 128:(i % 4 + 1) * 128], XT32[:, i, :], IDN, start=True,
                         stop=True, reads=[xt_res, const_res], out_res=bank_res[bb_], sig=(i >= 3), transpose=True)
                K.A(lambda bk=bk: nc.scalar.copy(out=XROW[0:3 + TS, 0:512], in_=banks[bk][0:3 + TS, 0:512]),
                    reads=[bank_res[bk]], writes=Y1_res)
                K.A(lambda bk2=bk2: nc.scalar.copy(out=XROW[0:3 + TS, 512:640], in_=banks[bk2][0:3 + TS, 0:128]),
                    reads=[bank_res[bk2]], writes=Y1_res)
                bfree(bk)
                bfree(bk2)
                for (c0, w_, s0) in [(g * 384, 384, 0), (DSSM + g * 128, 128, 384), (DSSM + 1024 + g * 128, 128, 512)]:
                    K.dma(SP, out_st, o_xbp[p][:, c0:c0 + w_], XROW[0:3, s0:s0 + w_], reads=Y1_res)
                    K.dma(SP, out_st, o_xbs[p][:, 2, c0:c0 + w_], XROW[3:3 + TS, s0:s0 + w_], reads=Y1_res)
                for i in range(3):
                    def zevac(ti, a, b, bk, i=i):
                        K.A(lambda: nc.scalar.activation(out=ZT[:, i, a:b], in_=banks[bk][:, 0:b - a], func=AF.Silu),
                            reads=[bank_res[bk]], writes=[ZT_res[i]])
                    linear_fm("z", lambda k, a, b: U[:, k, a:b], lambda k: U_res, KC, zevac)
                K.V(lambda: nc.vector.tensor_scalar(out=HP[:, :, 0:3], in0=HX, scalar1=FLAG[:, 0:1], scalar2=None,
                                                    op0=ALU.mult), reads=[hx_res, const_res], writes=[hp_res])
                for i in range(5):
                    ch = chs[i]
                    for j in range(3):
                        K.V(lambda i=i, j=j, ch=ch: nc.vector.scalar_tensor_tensor(
                            out=ACC3[:, i, :], in0=HP[:, i, j:j + 3],
                            scalar=CV[:, CV_SW + ch * 4 + j:CV_SW + ch * 4 + j + 1], in1=ACC3[:, i, :], op0=ALU.mult,
                            op1=ALU.add), reads=[hp_res, const_res], writes=[a3_res])
                    K.A(lambda i=i: nc.scalar.activation(out=XF[:, i, 0:3], in_=ACC3[:, i, :], func=AF.Silu),
                        reads=[a3_res], writes=[XF_res[i]])
                K.V(lambda: nc.vector.memset(HT, 0.0), writes=[ht_res])
                K.V(lambda: nc.vector.memset(HTB, 0.0), writes=[htb_res])
                dsk = CV[:, CV_BC + 96 + h0:CV_BC + 96 + h0 + HPG].unsqueeze(2).to_broadcast([128, HPG, 64])
                def stage_a(c):
                    cs = slice(c * 128, (c + 1) * 128)
                    x = c % 2
                    bk = balloc()
                    bkv = banks[bk][:, 0:256].bitcast(BF)
                    for i in range(4):
                        K.mm(bkv[:, i * 128:(i + 1) * 128], XF[:, i, cs], IDB, start=True, stop=True,
                             reads=[XF_res[i], const_res], out_res=bank_res[bk], sig=(i == 3), transpose=True)
                    K.A(lambda bkv=bkv, x=x: nc.scalar.copy(out=XTM[x], in_=bkv), reads=[bank_res[bk]],
                        writes=[XTM_res[x]])
                    bfree(bk)
                    xs3 = XTM[x][:, 0:384].rearrange("p (r q) -> p r q", r=HPG)
                    bkc = balloc()
                    K.mm(banks[bkc][:, 0:128], XF[:, 3, cs], XF[:, 4, cs], start=True, stop=True,
                         reads=[XF_res[3], XF_res[4]], out_res=bank_res[bkc], sig=True)
                    K.V(lambda x=x, c=c, xs3=xs3: nc.vector.tensor_tensor(
                        out=XDT[x], in0=xs3, in1=DT[:, c, h0:h0 + HPG].unsqueeze(2).to_broadcast([128, HPG, 64]),
                        op=ALU.mult), reads=[XTM_res[x], dt_res], writes=[XDT_res[x]])
                    K.V(lambda x=x, c=c, xs3=xs3: nc.vector.tensor_tensor(
                        out=XDD[x], in0=xs3, in1=W2[:, c, h0:h0 + HPG].unsqueeze(2).to_broadcast([128, HPG, 64]),
                        op=ALU.mult), reads=[XTM_res[x], dt_res], writes=[XDD_res[x]])
                    bkd = [balloc(), balloc()]
                    for hh in range(2):
                        for j in range(3):
                            h = h0 + hh * 3 + j
                            K.mm(banks[bkd[hh]][:, j * 128:(j + 1) * 128],
                                 DTA[:, c, h:h + 1].to_broadcast([128, 128]), TRI, start=True, stop=False,
                                 reads=[dt_res, const_res], out_res=bank_res[bkd[hh]], sig=False)
                            K.mm(banks[bkd[hh]][:, j * 128:(j + 1) * 128], IDN, NEGM, start=False, stop=True,
                                 reads=[const_res], out_res=bank_res[bkd[hh]], sig=(j == 2))
                    for hh in range(2):
                        bk = bkd[hh]
                        e = E32[hh]
                        for j in range(3):
                            h = h0 + hh * 3 + j
                            K.A(lambda bk=bk, j=j, h=h, e=e, c=c: nc.scalar.activation(
                                out=e[:, j, :], in_=banks[bk][:, j * 128:(j + 1) * 128], func=AF.Exp,
                                bias=NACS[:, c, h:h + 1], scale=1.0),
                                reads=[bank_res[bk], dt_res], writes=[E32_res[hh]])
                        bfree(bk)
                        m = MT[2 * x + hh]
                        K.V(lambda e=e, m=m, bkc=bkc: nc.vector.tensor_tensor(
                            out=m, in0=e, in1=banks[bkc][:, 0:128].unsqueeze(1).to_broadcast([128, 3, 128]),
                            op=ALU.mult), reads=[E32_res[hh], bank_res[bkc]], writes=[MT_res[2 * x + hh]])
                    bfree(bkc)

                def stage_b(c):
                    cs = slice(c * 128, (c + 1) * 128)
                    x = c % 2
                    xs3 = XTM[x][:, 0:384].rearrange("p (r q) -> p r q", r=HPG)
                    bks = balloc()
                    K.mm(banks[bks][:, 0:384], XTM[x][:, 384:512], XDD[x].rearrange("p r q -> p (r q)"), start=True,
                         stop=True, reads=[XTM_res[x], XDD_res[x]], out_res=bank_res[bks], sig=True)
                    bky = balloc()
                    for r in range(HPG):
                        m = MT[2 * x + r // 3]
                        K.mm(banks[bky][:, r * 64:(r + 1) * 64], m[:, r % 3, :], XDT[x][:, r, :], start=True, stop=True,
                             reads=[MT_res[2 * x + r // 3], XDT_res[x]], out_res=bank_res[bky], sig=(r == HPG - 1))
                    bko = None
                    if c > 0:
                        bko = balloc()
                        K.mm(banks[bko][:, 0:384], XF[:, 4, cs], HTB, start=True, stop=True, reads=[XF_res[4], htb_res],
                             out_res=bank_res[bko], sig=True)
                    if c > 0:
                        K.V(lambda c=c: nc.vector.tensor_tensor(
                            out=HT.rearrange("p (r q) -> p r q", r=HPG), in0=HT.rearrange("p (r q) -> p r q", r=HPG),
                            in1=CD[:, c, h0:h0 + HPG].unsqueeze(2).to_broadcast([128, HPG, 64]), op=ALU.mult),
                            reads=[dt_res], writes=[ht_res])
                    K.V(lambda bks=bks: nc.vector.tensor_tensor(out=HT, in0=HT, in1=banks[bks][:, 0:384], op=ALU.add),
                        reads=[bank_res[bks]], writes=[ht_res])
                    bfree(bks)
                    if c < NCH - 1:
                        K.A(lambda: nc.scalar.copy(out=HTB, in_=HT), reads=[ht_res], writes=[htb_res])
                    if c > 0:
                        K.V(lambda bko=bko, x=x, c=c: nc.vector.tensor_tensor(
                            out=Y1[x].rearrange("p (r q) -> p r q", r=HPG),
                            in0=banks[bko][:, 0:384].rearrange("p (r q) -> p r q", r=HPG),
                            in1=EXA[:, c, h0:h0 + HPG].unsqueeze(2).to_broadcast([128, HPG, 64]), op=ALU.mult),
                            reads=[bank_res[bko], dt_res], writes=[Y1_res[x]])
                        bfree(bko)
                        K.V(lambda bky=bky, x=x: nc.vector.tensor_tensor(out=Y1[x], in0=Y1[x],
                                                                         in1=banks[bky][:, 0:384], op=ALU.add),
                            reads=[bank_res[bky]], writes=[Y1_res[x]])
                    else:
                        K.A(lambda bky=bky, x=x: nc.scalar.copy(out=Y1[x], in_=banks[bky][:, 0:384]),
                            reads=[bank_res[bky]], writes=[Y1_res[x]])
                    bfree(bky)

                def stage_b_tail(c):
                    cs = slice(c * 128, (c + 1) * 128)
                    x = c % 2
                    bk = balloc()
                    for i in range(3):
                        K.mm(banks[bk][:, i * 128:(i + 1) * 128], Y1[x][:, i * 128:(i + 1) * 128], IDN, start=True,
                             stop=True, reads=[Y1_res[x], const_res], out_res=bank_res[bk], sig=(i == 2),
                             transpose=True)
                    K.A(lambda bk=bk, cs=cs: nc.scalar.copy(out=YT[:, :, cs],
                                                            in_=banks[bk][:, 0:384].rearrange("p (i q) -> p i q", i=3)),
                        reads=[bank_res[bk]], writes=YT_res)
                    bfree(bk)

                stage_a(0)
                for c in range(NCH):
                    if c + 1 < NCH:
                        stage_a(c + 1)
                    if c > 0:
                        stage_b_tail(c - 1)
                    stage_b(c)
                stage_b_tail(NCH - 1)
                exchange(HT, ht_res, 384, HIN, hin_res)
                for i in range(3):
                    K.V(lambda i=i: nc.vector.scalar_tensor_tensor(
                        out=YT[:, i, 0:TP], in0=XF[:, i, 0:TP], scalar=CV[:, CV_DF + 3 * g + i:CV_DF + 3 * g + i + 1],
                        in1=YT[:, i, 0:TP], op0=ALU.mult, op1=ALU.add),
                        reads=[XF_res[i], const_res], writes=[YT_res[i]])
                K.V(lambda: nc.vector.tensor_copy(
                    out=EXPD[0:TS, 0:3, :].rearrange("p a (h q) -> p (a h) q", h=2),
                    in_=DT[0:TS, 8, h0:h0 + HPG].unsqueeze(2).to_broadcast([TS, HPG, 64])),
                    reads=[dt_res], writes=[smp_res] + MT_res)
                K.V(lambda: nc.vector.tensor_copy(
                    out=EXPD[0:TS, 3:6, :].rearrange("p a (h q) -> p (a h) q", h=2),
                    in_=DECS[0:TS, h0:h0 + HPG].unsqueeze(2).to_broadcast([TS, HPG, 64])),
                    reads=[dt_res], writes=[smp_res] + MT_res)
                bk = balloc()
                for j in range(6):
                    K.mm(banks[bk][:, j * TS:(j + 1) * TS], EXPD[0:TS, j, :], IDN[0:TS, 0:TS], start=True, stop=True,
                         reads=[smp_res, const_res] + MT_res, out_res=bank_res[bk], sig=(j == 5), transpose=True)
                K.A(lambda bk=bk: nc.scalar.copy(out=DTS, in_=banks[bk][:, 0:6 * TS].rearrange("p (j b) -> p j b", j=6)),
                    reads=[bank_res[bk]], writes=[smp_res])
                bfree(bk)
                K.V(lambda: nc.vector.tensor_tensor(out=DX, in0=DTS[:, 0:3, :], in1=XF[:, 0:3, TP:T], op=ALU.mult),
                    reads=[smp_res, XF_res[0], XF_res[1], XF_res[2]], writes=[smp_res])
                K.V(lambda: nc.vector.tensor_copy(out=XS32, in_=XF[:, 3:5, TP:T]), reads=[XF_res[3], XF_res[4]],
                    writes=[smp_res])
                def slab_load(bb):
                    K.dma(SP, slab_st[bb % 4], SL[bb % 4],
                          st_ssm[p][bb, g * 384:(g + 1) * 384, :].rearrange("(i q) n -> q i n", q=128),
                          writes=[SL_res[bb % 4]])
                for bb in range(3):
                    slab_load(bb)

                def bc_prep(bb):
                    bk_ = balloc()
                    for w_ in range(2):
                        K.mm(banks[bk_][:, w_ * 128:(w_ + 1) * 128],
                             XS32[:, w_, bb:bb + 1].to_broadcast([128, 128]), IDN, start=True, stop=True,
                             reads=[smp_res, const_res], out_res=bank_res[bk_], sig=(w_ == 1))
                    return bk_
                nxt_bk = bc_prep(0)
                for b_ in range(TS):
                    x = b_ % 2
                    x3 = b_ % 4
                    sl, slr = SL[x3], SL_res[x3]
                    t2, t2r = T2[0], T2_res[0]
                    bk = nxt_bk
                    if b_ + 1 < TS:
                        nxt_bk = bc_prep(b_ + 1)
                    for blk in range(3):
                        K.A(lambda bk=bk, b_=b_, blk=blk: nc.scalar.activation(
                            out=t2[:, blk, :], in_=banks[bk][:, 0:128], func=AF.Copy, scale=DX[:, blk, b_:b_ + 1]),
                            reads=[bank_res[bk], smp_res], writes=[t2r])
                    for blk in range(3):
                        K.V(lambda b_=b_, sl=sl, blk=blk: nc.vector.scalar_tensor_tensor(
                            out=sl[:, blk, :], in0=sl[:, blk, :], scalar=DTS[:, 3 + blk, b_:b_ + 1], in1=t2[:, blk, :],
                            op0=ALU.mult, op1=ALU.add), reads=[smp_res, t2r], writes=[slr])
                    K.dma(ACT, sout_st, o_sss[p][b_, g * 384:(g + 1) * 384, :].rearrange("(i q) n -> q i n", q=128), sl,
                          reads=[slr])
                    for blk in range(3):
                        K.V(lambda bk=bk, b_=b_, sl=sl, blk=blk: nc.vector.scalar_tensor_tensor(
                            out=Y1[x][:, blk * 128:(blk + 1) * 128], in0=sl[:, blk, :], scalar=1.0,
                            in1=banks[bk][:, 128:256], op0=ALU.mult, op1=ALU.mult, accum_out=YS[:, blk, b_:b_ + 1]),
                            reads=[slr, bank_res[bk]], writes=[Y1_res[x], ys_res])
                    bfree(bk)
                    if b_ + 3 < TS:
                        slab_load(b_ + 3)
                for i in range(3):
                    K.V(lambda i=i: nc.vector.scalar_tensor_tensor(
                        out=YT[:, i, TP:T], in0=XF[:, i, TP:T], scalar=CV[:, CV_DF + 3 * g + i:CV_DF + 3 * g + i + 1],
                        in1=YS[:, i, :], op0=ALU.mult, op1=ALU.add),
                        reads=[XF_res[i], ys_res, const_res], writes=[YT_res[i]])
                K.V(lambda: nc.vector.tensor_scalar(out=HIN, in0=HIN, scalar1=FLAG[:, 0:1], scalar2=None, op0=ALU.mult),
                    reads=[const_res], writes=[hin_res])
                K.A(lambda: nc.scalar.copy(out=HINB, in_=HIN), reads=[hin_res], writes=[hinb_res])
                for c in range(NCH):
                    cs = slice(c * 128, (c + 1) * 128)
                    x = c % 2
                    bko = balloc()
                    K.mm(banks[bko][:, 0:384], XF[:, 4, cs], HINB, start=True, stop=True, reads=[XF_res[4], hinb_res],
                         out_res=bank_res[bko], sig=True)
                    K.V(lambda bko=bko, x=x, c=c: nc.vector.tensor_tensor(
                        out=Y1[x].rearrange("p (r q) -> p r q", r=HPG),
                        in0=banks[bko][:, 0:384].rearrange("p (r q) -> p r q", r=HPG),
                        in1=ETOT[:, c, h0:h0 + HPG].unsqueeze(2).to_broadcast([128, HPG, 64]), op=ALU.mult),
                        reads=[bank_res[bko], dt_res], writes=[Y1_res[x]])
                    bfree(bko)
                    bk = balloc()
                    for i in range(3):
                        K.mm(banks[bk][:, i * 128:(i + 1) * 128], Y1[x][:, i * 128:(i + 1) * 128], IDN, start=True,
                             stop=True, reads=[Y1_res[x], const_res], out_res=bank_res[bk], sig=(i == 2),
                             transpose=True)
                    K.V(lambda bk=bk, cs=cs: nc.vector.tensor_tensor(
                        out=YT[:, :, cs], in0=YT[:, :, cs], in1=banks[bk][:, 0:384].rearrange("p (i q) -> p i q", i=3),
                        op=ALU.add), reads=[bank_res[bk]], writes=YT_res)
                    bfree(bk)
                K.V(lambda: nc.vector.tensor_tensor(
                    out=HIN.rearrange("p (r q) -> p r q", r=HPG), in0=HIN.rearrange("p (r q) -> p r q", r=HPG),
                    in1=CDTOT[:, h0:h0 + HPG].unsqueeze(2).to_broadcast([128, HPG, 64]), op=ALU.mult),
                    reads=[dt_res, hinb_res], writes=[hin_res])
                K.V(lambda: nc.vector.tensor_tensor(out=HT, in0=HT, in1=HIN, op=ALU.add), reads=[hin_res],
                    writes=[ht_res])
                bk = balloc()
                for i in range(3):
                    K.mm(banks[bk][:, i * 128:(i + 1) * 128], HT[:, i * 128:(i + 1) * 128], IDN, start=True, stop=True,
                         reads=[ht_res, const_res], out_res=bank_res[bk], sig=(i == 2), transpose=True)
                K.A(lambda bk=bk: nc.scalar.copy(out=SROW, in_=banks[bk][:, 0:384].rearrange("p (i q) -> p i q", i=3)),
                    reads=[bank_res[bk]], writes=[srow_res])
                bfree(bk)
                K.dma(SP, out_st, o_ssp[p][g * 384:(g + 1) * 384, :].rearrange("(i q) n -> q i n", q=128), SROW,
                      reads=[srow_res])
                for i in range(3):
                    K.V(lambda i=i: nc.vector.tensor_tensor(out=YT[:, i, :], in0=YT[:, i, :], in1=ZT[:, i, :],
                                                            op=ALU.mult), reads=[ZT_res[i]], writes=[YT_res[i]])
                bks = [balloc() for _ in TT]
                for i in range(3):
                    x = i % 2
                    K.A(lambda i=i, x=x: nc.scalar.activation(out=GSQ[x], in_=YT[:, i, :], func=AF.Square),
                        reads=[YT_res[i]], writes=[GSQ_res[x]])
                    for ti, (a, b) in enumerate(TT):
                        K.mm(banks[bks[ti]][:, 0:b - a], ONB, GSQ[x][:, a:b], start=(i == 0), stop=(i == 2),
                             reads=[GSQ_res[x], const_res], out_res=bank_res[bks[ti]], sig=(ti == len(TT) - 1))
                for ti, (a, b) in enumerate(TT):
                    K.A(lambda ti=ti, a=a, b=b: nc.scalar.activation(out=RS[:, a:b], in_=banks[bks[ti]][:, 0:b - a],
                                                                    func=AF.Ln, bias=EPSC[:, 0:1], scale=1.0 / 384),
                        reads=[bank_res[bks[ti]], const_res], writes=[RS_res])
                K.A(lambda: nc.scalar.activation(out=RS, in_=RS, func=AF.Exp, scale=-0.5), reads=[RS_res],
                    writes=[RS_res])
                for b_ in bks:
                    bfree(b_)
                for i in range(3):
                    K.V(lambda i=i: nc.vector.scalar_tensor_tensor(
                        out=INT[:, 8 + 3 * g + i, :], in0=YT[:, i, :],
                        scalar=CV[:, CV_NG + 3 * g + i:CV_NG + 3 * g + i + 1], in1=RS, op0=ALU.mult, op1=ALU.mult),
                        reads=[YT_res[i], RS_res, const_res], writes=[INT_res[8 + 3 * g + i]])
            K.barrier()
            for d in range(KC):
                def oevac(ti, a, b, bk, d=d):
                    K.A(lambda: nc.scalar.copy(out=H[:, d, a:b], in_=banks[bk][:, 0:b - a]), reads=[bank_res[bk]],
                        writes=[H_res[d]])
                linear_fm("out", lambda k, a, b: INT[:, k, a:b], lambda k: INT_res[k], 32, oevac)
            K.barrier()
            post_residual(lambda k: CV[:, CV_G + 3 * 16 + k:CV_G + 3 * 16 + k + 1])

        def ple(p):
            prenorm(6)
            spill_H()
            K.barrier()
            PT = view(OFF_X, [2, T], BF)
            pt_res = Res("PT")
            K.dma(POOL, pt_st, PT, pT[p].rearrange("k q t -> q k t"), writes=[pt_res])
            si = [0]
            for d in range(KC):
                pi, pres, (pg, pp) = ws.get("ple")
                for ti, (a, b) in enumerate(TT):
                    n = b - a
                    bg, bp = balloc(), balloc()
                    for k in range(KC):
                        K.mm(banks[bg][:, 0:n], pg[:, k, :], U[:, k, a:b], start=(k == 0), stop=(k == KC - 1),
                             reads=[pres, U_res], out_res=bank_res[bg], sig=False)
                    for k in range(2):
                        K.mm(banks[bp][:, 0:n], pp[:, k, :], PT[:, k, a:b], start=(k == 0), stop=(k == 1),
                             reads=[pres, pt_res], out_res=bank_res[bp], sig=(k == 1))
                    s = si[0] % 2
                    si[0] += 1
                    K.A(lambda bg=bg, n=n, s=s: nc.scalar.activation(out=SS[s][:, 0:n], in_=banks[bg][:, 0:n],
                                                                     func=AF.Sigmoid),
                        reads=[bank_res[bg]], writes=[SS_res[s]])
                    K.V(lambda bp=bp, n=n, s=s, d=d, a=a, b=b: nc.vector.tensor_tensor(
                        out=H[:, d, a:b], in0=SS[s][:, 0:n], in1=banks[bp][:, 0:n], op=ALU.mult),
                        reads=[SS_res[s], bank_res[bp]], writes=[H_res[d]])
                    bfree(bg)
                    bfree(bp)
                ws.done(pi)
            K.barrier()
            post_residual(lambda k: CV[:, CV_G + 7 * 16 + k:CV_G + 7 * 16 + k + 1],
                          after_k=lambda k: K.dma(ACT, yout_st, yT[p, k], H[:, k, :], reads=[H_res[k]]))

        for p in range(NP):
            for k in range(KC):
                K.dma(SP, ld_st, H[:, k, :], xT[p, k], writes=[H_res[k]])
            ffn("w_ffn1", 0, 0)
            if stage >= 2:
                mixer(p)
            if stage >= 3:
                ffn("w_ffn2", 4, 1)
            if stage >= 4:
                ple(p)
            K.barrier()
            if stage < 4:
                for k in range(KC):
                    K.dma(SP, out_st, yT[p, k], H[:, k, :], reads=[H_res[k]])
            K.barrier()
        SP.raw.wait_ge(out_st.sem, 16 * out_st.n)
    return nc


def _cvec(inp):
    cv = np.zeros((128, CV_N), np.float32)
    names = ["norm_ffn1_pre", "norm_ffn1_post", "norm_mix_pre", "norm_mix_post",
             "norm_ffn2_pre", "norm_ffn2_post", "norm_ple_pre", "norm_ple_post"]
    for i, n in enumerate(names):
        cv[:, CV_G + i * 16:CV_G + (i + 1) * 16] = inp[n].reshape(16, 128).T
    cw = inp["conv_mod_w"].reshape(31, 8, 128)
    cv[:, CV_CW:CV_CW + 248] = cw.transpose(2, 1, 0).reshape(128, 248)
    cv[:, CV_CB:CV_CB + 8] = inp["conv_mod_b"].reshape(8, 128).T
    cv[:, CV_LG:CV_LG + 8] = inp["conv_mod_ln_g"].reshape(8, 128).T
    cv[:, CV_LB:CV_LB + 8] = inp["conv_mod_ln_b"].reshape(8, 128).T
    sw = inp["ssm_conv_w"].reshape(4, 40, 128)
    cv[:, CV_SW:CV_SW + 160] = sw.transpose(2, 1, 0).reshape(128, 160)
    cv[:, CV_SB:CV_SB + 40] = inp["ssm_conv_b"].reshape(40, 128).T
    cv[:, CV_NG:CV_NG + 24] = inp["ssm_norm_g"].reshape(24, 128).T
    cv[:, CV_BC:CV_BC + 48] = inp["dt_bias"].reshape(1, 48)
    cv[:, CV_BC + 48:CV_BC + 96] = inp["a_log"].reshape(1, 48)
    cv[:, CV_BC + 96:CV_BC + 144] = inp["d_skip"].reshape(1, 48)
    cv[:, CV_DF:CV_DF + 24] = np.repeat(inp["d_skip"].reshape(48), 64).reshape(24, 128).T
    return cv


def run(inputs, stage=99, NP=1, TS=16):
    inp = {k: np.ascontiguousarray(np.asarray(v, dtype=np.float32)) for k, v in inputs.items()}
    T = TP + TS
    xp = inp["x_prompt"]
    xs = inp["x_sample"].reshape(128, D)
    pp = inp["p_prompt"].reshape(4, 2048, 256)
    psm = inp["p_sample"].reshape(128, 256)
    shared = {"cvec": _cvec(inp), "ident": np.eye(128, dtype=np.float32),
              "tri": np.triu(np.ones((128, 128), np.float32))}
    for nm in ["w_ffn1_gate", "w_ffn1_up", "w_ffn1_down", "w_in", "w_out", "w_ffn2_gate", "w_ffn2_up",
               "w_ffn2_down", "w_ple_gate", "w_ple_proj"]:
        shared[nm] = inp[nm][0]
    in_maps = []
    for c in range(8):
        s = c // 2
        hf = c % 2
        m = dict(shared)
        m["flag"] = np.full((128, 1), float(hf), np.float32)
        m["negm"] = ((np.triu(np.ones((128, 128), np.float32)) - 1.0) * 30000.0).astype(np.float32)
        xTc = np.zeros((NP, KC, 128, T), np.float32)
        pTc = np.zeros((NP, 2, 128, T), np.float32)
        stc = np.zeros((NP, TS, 30, DCV), np.float32)
        stx = np.zeros((NP, TS * 3, DXBC), np.float32)
        sts = np.zeros((NP, TS, DSSM, DST), np.float32)
        for p in range(NP):
            b0 = (c * NP + p) * TS
            xt = np.concatenate([xp[s, hf * TP:(hf + 1) * TP], xs[b0:b0 + TS]], 0)
            xTc[p] = xt.T.reshape(KC, 128, T)
            pt = np.concatenate([pp[s, hf * TP:(hf + 1) * TP], psm[b0:b0 + TS]], 0)
            pTc[p] = pt.T.reshape(2, 128, T)
            stc[p] = inp["state_conv_mod"][0, b0:b0 + TS]
            stx[p] = inp["state_ssm_conv"][0, b0:b0 + TS].reshape(TS * 3, DXBC)
            sts[p] = inp["state_ssm"][0, b0:b0 + TS].reshape(TS, DSSM, DST)
        m.update({"xT": xTc, "pT": pTc, "st_conv": stc, "st_xbc": stx, "st_ssm": sts})
        in_maps.append(m)
    nc = build_program(NP, TS, stage)
    res = run_bass_kernel_spmd(nc, in_maps, core_ids=list(range(8)))
    R = res.results
    y_prompt = np.zeros((4, 2048, D), np.float32)
    y_sample = np.zeros((128, 1, D), np.float32)
    cmp_ = np.zeros((1, 4, 30, DCV), np.float32)
    xbp = np.zeros((1, 4, 3, DXBC), np.float32)
    ssp = np.zeros((1, 4, NH, 64, DST), np.float32)
    cms = np.zeros((1, 128, 30, DCV), np.float32)
    xbs = np.zeros((1, 128, 3, DXBC), np.float32)
    sss = np.zeros((1, 128, NH, 64, DST), np.float32)
    for c in range(8):
        r = R[c]
        s = c // 2
        for p in range(NP):
            b0 = (c * NP + p) * TS
            yt = r["yT"][p].reshape(D, T).T
            y_prompt[s, (c % 2) * TP:(c % 2 + 1) * TP] = yt[:TP]
            y_sample[b0:b0 + TS, 0] = yt[TP:]
            cms[0, b0:b0 + TS] = r["o_cms"][p]
            xbs[0, b0:b0 + TS] = r["o_xbs"][p]
            sss[0, b0:b0 + TS] = r["o_sss"][p].reshape(TS, NH, 64, DST)
        if c % 2 == 1:
            cmp_[0, s] = r["o_cmp"][NP - 1]
            xbp[0, s] = r["o_xbp"][NP - 1]
            ssp[0, s] = r["o_ssp"][NP - 1].reshape(NH, 64, DST)
    return (y_prompt, y_sample, cmp_, xbp, ssp, cms, xbs, sss)


def kernel(**inputs):
    return run(inputs)
```

```python
import numpy as np
import concourse.bass as bass
import concourse.mybir as mybir
from concourse.bass_utils import run_bass_kernel_spmd
from contextlib import ExitStack

F32 = mybir.dt.float32
BF = mybir.dt.bfloat16
ALU = mybir.AluOpType
AF = mybir.ActivationFunctionType
AX = mybir.AxisListType

D = 2048
KC = 16
DFF = 5632
NHALF = 22
DCV = 1024
DSSM = 3072
DXBC = 5120
NH = 48
NG = 8
HPG = 6
DST = 128
EPS = 1e-6
S_Z = 2048
S_X = 5120
S_B = 8192
S_C = 9216
S_DT = 10240
TP = 1024
NCH = 8
NSLOT = 3

CV_G = 0
CV_CW = 128
CV_CB = CV_CW + 248
CV_LG = CV_CB + 8
CV_LB = CV_LG + 8
CV_SW = CV_LB + 8
CV_SB = CV_SW + 160
CV_NG = CV_SB + 40
CV_BC = CV_NG + 24
CV_DF = CV_BC + 144
CV_N = CV_DF + 24


class Res:
    __slots__ = ("name", "w", "r", "excl")

    def __init__(self, name, excl=False):
        self.name = name
        self.w = None
        self.r = []
        self.excl = excl


class Eng:
    def __init__(self, raw, sem, name):
        self.raw = raw
        self.sem = sem
        self.name = name
        self.n = 0
        self.waited = {}


class Stream:
    def __init__(self, sem, name):
        self.sem = sem
        self.name = name
        self.n = 0
        self.inc = 16


class Builder:
    def __init__(self, nc, es):
        self.nc = nc
        self.es = es
        self.nsem = 0
        self.pe = Eng(nc.tensor, self.sem("pe"), "pe")
        self.act = Eng(nc.scalar, self.sem("act"), "act")
        self.dve = Eng(nc.vector, self.sem("dve"), "dve")
        self.pool = Eng(nc.gpsimd, self.sem("pool"), "pool")
        self.sp = Eng(nc.sync, self.sem("sp"), "sp")
        self.engs = [self.pe, self.act, self.dve, self.pool, self.sp]
        self.streams = []
        self.pe_pending = []
        self.pe_out_pending = []

    def sem(self, name):
        self.nsem += 1
        return self.es.enter_context(self.nc.semaphore(name))

    def stream(self, name):
        s = Stream(self.sem("d_" + name), name)
        self.streams.append(s)
        return s

    def wait(self, eng, tok):
        if tok is None:
            return
        if tok[0] == "e":
            src, val = tok[1], tok[2]
            if src is eng and eng is self.pe:
                return
            sem = src.sem
            key = src.name
        else:
            st = tok[1]
            sem = st.sem
            val = st.inc * st.n
            key = "d_" + st.name
        if eng.waited.get(key, 0) >= val:
            return
        eng.raw.wait_ge(sem, val)
        eng.waited[key] = val

    def _pre(self, eng, reads, writes):
        for r in reads:
            self.wait(eng, r.w)
            if r.excl:
                for t in r.r:
                    if t[0] != "e" or t[1] is not eng:
                        self.wait(eng, t)
        for w in writes:
            assert w not in self.pe_pending or eng is self.pe, "PE pending read on " + w.name
            self.wait(eng, w.w)
            for t in w.r:
                self.wait(eng, t)

    def _post(self, tok, reads, writes):
        for r in reads:
            r.r.append(tok)
            if len(r.r) > 24:
                r.r = r.r[-24:]
        for w in writes:
            w.w = tok
            w.r = []

    def op(self, eng, fn, reads=(), writes=()):
        self._pre(eng, reads, writes)
        ins = fn()
        ins.then_inc(eng.sem, 1)
        eng.n += 1
        tok = ("e", eng, eng.n)
        self._post(tok, reads, writes)
        return tok

    def A(self, fn, reads=(), writes=()):
        return self.op(self.act, fn, reads, writes)

    def V(self, fn, reads=(), writes=()):
        return self.op(self.dve, fn, reads, writes)

    def dma(self, q, st, out, in_, reads=(), writes=()):
        self._pre(q, reads, writes)
        ins = q.raw.dma_start(out=out, in_=in_)
        ins.then_inc(st.sem, 16)
        st.n += 1
        tok = ("d", st, st.n)
        self._post(tok, reads, writes)
        return tok

    def mm(self, out, lhsT, rhs, start, stop, reads, out_res, sig, transpose=False):
        pe = self.pe
        for r in reads:
            self.wait(pe, r.w)
            if r not in self.pe_pending:
                self.pe_pending.append(r)
        if start:
            self.wait(pe, out_res.w)
            for t in out_res.r:
                self.wait(pe, t)
        if transpose:
            ins = pe.raw.transpose(out=out, in_=lhsT, identity=rhs)
        else:
            ins = pe.raw.matmul(out, lhsT, rhs, start=start, stop=stop)
        if stop and out_res not in self.pe_out_pending:
            self.pe_out_pending.append(out_res)
        if sig:
            ins.then_inc(pe.sem, 1)
            pe.n += 1
            tok = ("e", pe, pe.n)
            for r in self.pe_pending:
                r.r.append(tok)
            self.pe_pending = []
            for o in self.pe_out_pending:
                o.w = tok
                o.r = []
            self.pe_out_pending = []
        return ins

    def barrier(self, engs=None):
        assert not self.pe_pending and not self.pe_out_pending
        engs = engs or [self.pe, self.act, self.dve, self.sp]
        for e in engs:
            for f in [self.pe, self.act, self.dve]:
                if f is not e and f.n > 0:
                    self.wait(e, ("e", f, f.n))
            for st in self.streams:
                if st.n > 0 and not st.name.startswith("slot"):
                    self.wait(e, ("d", st, st.n))


def tt_tiles(T):
    a = (T + 2) // 3
    b = (T - a + 1) // 2
    return [(0, a), (a, a + b), (a + b, T)]


def build_program(NP, TS, stage=99):
    T = TP + TS
    TT = tt_tiles(T)
    nc = bass.Bass("TRN2", target_bir_lowering=False)

    def din(name, shape):
        return nc.dram_tensor(name, list(shape), F32, kind="ExternalInput").ap()

    def dout(name, shape):
        return nc.dram_tensor(name, list(shape), F32, kind="ExternalOutput").ap()

    xT = din("xT", [NP, KC, 128, T])
    pT = din("pT", [NP, 2, 128, T])
    st_conv = din("st_conv", [NP, TS, 30, DCV])
    st_xbc = din("st_xbc", [NP, TS * 3, DXBC])
    st_ssm = din("st_ssm", [NP, TS, DSSM, DST])
    cvec_d = din("cvec", [128, CV_N])
    ident_d = din("ident", [128, 128])
    tri_d = din("tri", [128, 128])
    flag_d = din("flag", [128, 1])
    negm_d = din("negm", [128, 128])
    W = {}
    for nm, shp in [("w_ffn1_gate", (D, DFF)), ("w_ffn1_up", (D, DFF)), ("w_ffn1_down", (DFF, D)),
                    ("w_in", (D, 10288)), ("w_out", (4096, D)),
                    ("w_ffn2_gate", (D, DFF)), ("w_ffn2_up", (D, DFF)), ("w_ffn2_down", (DFF, D)),
                    ("w_ple_gate", (D, D)), ("w_ple_proj", (256, D))]:
        W[nm] = din(nm, shp)
    yT = dout("yT", [NP, KC, 128, T])
    o_cms = dout("o_cms", [NP, TS, 30, DCV])
    o_xbs = dout("o_xbs", [NP, TS, 3, DXBC])
    o_sss = dout("o_sss", [NP, TS, DSSM, DST])
    o_cmp = dout("o_cmp", [NP, 30, DCV])
    o_xbp = dout("o_xbp", [NP, 3, DXBC])
    o_ssp = dout("o_ssp", [NP, DSSM, DST])
    hsp = nc.dram_tensor("hsp", [KC, 128, T], F32, kind="Internal").ap()

    with ExitStack() as es:
        K = Builder(nc, es)
        PE, ACT, DVE, POOL, SP = K.pe, K.act, K.dve, K.pool, K.sp

        ARENA_B = 188000
        arena = es.enter_context(nc.sbuf_tensor("arena", [128, ARENA_B // 4], F32))
        slots = [es.enter_context(nc.sbuf_tensor("slot%d" % i, [128, 4096], BF)) for i in range(NSLOT)]
        slot_res = [Res("slot%d" % i) for i in range(NSLOT)]
        slot_st = [K.stream("slot%d" % i) for i in range(NSLOT)]
        banks = [es.enter_context(nc.psum_tensor("bank%d" % i, [128, 512], F32)) for i in range(8)]
        bank_res = [Res("bank%d" % i, excl=True) for i in range(8)]
        free_banks = list(range(8))

        def balloc():
            assert free_banks, "out of PSUM banks"
            return free_banks.pop(0)

        def bfree(b):
            free_banks.append(b)

        def view(off, shape, dt):
            esz = 4 if dt == F32 else 2
            n = int(np.prod(shape))
            nb = (n * esz + 3) // 4 * 4
            assert off % 4 == 0
            assert off + nb <= ARENA_B, (off, nb)
            ap = arena[:, off // 4: off // 4 + nb // 4]
            if dt != F32:
                ap = ap.bitcast(dt)[:, 0:n]
            if len(shape) == 2:
                return ap.rearrange("p (a b) -> p a b", a=shape[0])
            if len(shape) == 3:
                return ap.rearrange("p (a b c) -> p a b c", a=shape[0], b=shape[1])
            return ap

        OFF_U = 0
        OFF_H = OFF_U + 2 * KC * T
        OFF_X = OFF_H + 4 * KC * T
        OFF_C = OFF_X + 4 * KC * T
        U = view(OFF_U, [KC, T], BF)
        H = view(OFF_H, [KC, T], F32)
        AH = view(OFF_X, [NHALF, T], BF)
        INT = view(OFF_X, [32, T], BF)
        U_res = Res("U")
        H_res = [Res("H%d" % i) for i in range(KC)]
        AH_res = [Res("AH%d" % i) for i in range(NHALF)]
        INT_res = [Res("INT%d" % i) for i in range(32)]
        hsp_res = [Res("hsp%d" % i) for i in range(KC)]

        coff = [OFF_C]

        def calloc(shape, dt):
            esz = 4 if dt == F32 else 2
            n = int(np.prod(shape)) * esz
            n = (n + 3) // 4 * 4
            o = coff[0]
            coff[0] += n
            return view(o, shape, dt)

        CV = calloc([CV_N], F32)
        IDN = calloc([128], F32)
        TRI = calloc([128], F32)
        ONE = calloc([128], F32)
        IDB = calloc([128], BF)
        ONB = calloc([128], BF)
        EPSC = calloc([1], F32)
        GH = calloc([2 * KC], F32)
        ANEG = calloc([NH], F32)
        RS = calloc([T], F32)
        SQ = [calloc([T], BF) for _ in range(2)]
        SQ_res = [Res("SQ0"), Res("SQ1")]
        RS_res = Res("RS")
        const_res = Res("const")
        OFF_XS = OFF_X + 2 * NHALF * T
        SS = [view(OFF_XS + i * 4 * 352, [352], F32) for i in range(2)]
        SS_res = [Res("SS0"), Res("SS1")]
        HST = [view(OFF_XS + 2 * 4 * 352 + i * 4 * T, [T], F32) for i in range(2)]
        HST_res = [Res("HST0"), Res("HST1")]
        assert OFF_XS + 2 * 4 * 352 + 2 * 4 * T <= OFF_C

        ld_st = K.stream("ld")
        sp_st = K.stream("spill")
        hst_st = [K.stream("hst0"), K.stream("hst1")]
        out_st = K.stream("out")

        K.dma(SP, ld_st, CV, cvec_d, writes=[const_res])
        K.dma(SP, ld_st, IDN, ident_d, writes=[const_res])
        K.dma(SP, ld_st, TRI, tri_d, writes=[const_res])
        K.V(lambda: nc.vector.memset(ONE, 1.0), writes=[const_res])
        K.V(lambda: nc.vector.memset(ONB, 1.0), writes=[const_res])
        K.V(lambda: nc.vector.memset(EPSC, EPS), writes=[const_res])
        K.V(lambda: nc.vector.tensor_copy(out=IDB, in_=IDN), reads=[const_res], writes=[const_res])
        K.V(lambda: nc.vector.tensor_scalar(out=GH[:, 0:KC], in0=CV[:, CV_G + 16:CV_G + 32], scalar1=0.5,
                                            scalar2=None, op0=ALU.mult), reads=[const_res], writes=[const_res])
        K.V(lambda: nc.vector.tensor_scalar(out=GH[:, KC:2 * KC], in0=CV[:, CV_G + 80:CV_G + 96], scalar1=0.5,
                                            scalar2=None, op0=ALU.mult), reads=[const_res], writes=[const_res])
        K.A(lambda: nc.scalar.activation(out=ANEG, in_=CV[:, CV_BC + 48:CV_BC + 96], func=AF.Exp),
            reads=[const_res], writes=[const_res])
        K.V(lambda: nc.vector.tensor_scalar(out=ANEG, in0=ANEG, scalar1=-1.0, scalar2=None, op0=ALU.mult),
            reads=[const_res], writes=[const_res])

        class WS:
            def __init__(self):
                self.plan = []
                self.issued = 0
                self.next_i = 0
                self.done_upto = 0

            def pump(self):
                while self.issued < len(self.plan) and self.issued < self.done_upto + NSLOT:
                    i = self.issued
                    s = i % NSLOT
                    off = 0
                    for ent in self.plan[i][1]:
                        w, k0, kc, f0 = ent[:4]
                        fw = ent[4] if len(ent) > 4 else 128
                        dst = slots[s][:, off:off + kc * fw].rearrange("p (k f) -> p k f", k=kc)
                        src = w[k0 * 128:(k0 + kc) * 128, f0:f0 + fw].rearrange("(k p) f -> p k f", p=128)
                        K.dma(POOL, slot_st[s], dst, src, writes=[slot_res[s]])
                        off += kc * fw
                    self.issued += 1

            def get(self, tag):
                i = self.next_i
                assert self.plan[i][0] == tag, (self.plan[i][0], tag)
                self.pump()
                assert i < self.issued
                self.next_i += 1
                s = i % NSLOT
                views = []
                off = 0
                for ent in self.plan[i][1]:
                    w, k0, kc, f0 = ent[:4]
                    fw = ent[4] if len(ent) > 4 else 128
                    views.append(slots[s][:, off:off + kc * fw].rearrange("p (k f) -> p k f", k=kc))
                    off += kc * fw
                return i, slot_res[s], views

            def done(self, i):
                assert i == self.done_upto
                self.done_upto += 1
                self.pump()

        ws = WS()

        def plan_ffn(pre):
            wg, wu, wd = W[pre + "_gate"], W[pre + "_up"], W[pre + "_down"]
            for hf in range(2):
                for f in range(NHALF):
                    fc = hf * NHALF + f
                    ws.plan.append((pre + "gu", [(wg, 0, KC, fc * 128), (wu, 0, KC, fc * 128)]))
                for d in range(KC):
                    ws.plan.append((pre + "dn", [(wd, hf * NHALF, NHALF, d * 128)]))

        def plan_mixer():
            wi = W["w_in"]
            for cc in range(8):
                ws.plan.append(("glu", [(wi, 0, KC, cc * 128), (wi, 0, KC, 1024 + cc * 128)]))
            ws.plan.append(("dt", [(wi, 0, KC, S_DT, NH)]))
            for g in range(NG):
                for i in range(3):
                    ws.plan.append(("xs", [(wi, 0, KC, S_X + g * 384 + i * 128)]))
                ws.plan.append(("bc", [(wi, 0, KC, S_B + g * 128), (wi, 0, KC, S_C + g * 128)]))
                for i in range(3):
                    ws.plan.append(("z", [(wi, 0, KC, S_Z + g * 384 + i * 128)]))
            for d in range(KC):
                ws.plan.append(("out", [(W["w_out"], 0, 32, d * 128)]))

        def plan_ple():
            for d in range(KC):
                ws.plan.append(("ple", [(W["w_ple_gate"], 0, KC, d * 128), (W["w_ple_proj"], 0, 2, d * 128)]))

        for p in range(NP):
            plan_ffn("w_ffn1")
            if stage >= 2:
                plan_mixer()
            if stage >= 3:
                plan_ffn("w_ffn2")
            if stage >= 4:
                plan_ple()

        def stats_to_RS(src_fn, n_k, src_res_fn, scale):
            bks = [balloc() for _ in TT]
            for k in range(n_k):
                sq = SQ[k % 2]
                sr = SQ_res[k % 2]
                K.A(lambda k=k, sq=sq: nc.scalar.activation(out=sq, in_=src_fn(k), func=AF.Square),
                    reads=[src_res_fn(k)], writes=[sr])
                for ti, (a, b) in enumerate(TT):
                    K.mm(banks[bks[ti]][:, 0:b - a], ONB, sq[:, a:b], start=(k == 0), stop=(k == n_k - 1),
                         reads=[sr, const_res], out_res=bank_res[bks[ti]], sig=(ti == len(TT) - 1))
            for ti, (a, b) in enumerate(TT):
                K.A(lambda ti=ti, a=a, b=b: nc.scalar.activation(out=RS[:, a:b], in_=banks[bks[ti]][:, 0:b - a],
                                                                func=AF.Ln, bias=EPSC[:, 0:1], scale=scale),
                    reads=[bank_res[bks[ti]], const_res], writes=[RS_res])
            K.A(lambda: nc.scalar.activation(out=RS, in_=RS, func=AF.Exp, scale=-0.5), reads=[RS_res], writes=[RS_res])
            for b in bks:
                bfree(b)

        def prenorm(gi):
            stats_to_RS(lambda k: H[:, k, :], KC, lambda k: H_res[k], 1.0 / D)
            for k in range(KC):
                K.V(lambda k=k: nc.vector.scalar_tensor_tensor(out=U[:, k, :], in0=H[:, k, :],
                                                               scalar=CV[:, CV_G + gi * 16 + k:CV_G + gi * 16 + k + 1],
                                                               in1=RS, op0=ALU.mult, op1=ALU.mult),
                    reads=[H_res[k], RS_res, const_res], writes=[U_res])

        def spill_H():
            for k in range(KC):
                K.dma(SP, sp_st, hsp[k], H[:, k, :], reads=[H_res[k]], writes=[hsp_res[k]])

        def post_residual(gcol_fn):
            stats_to_RS(lambda k: H[:, k, :], KC, lambda k: H_res[k], 1.0 / D)
            for k in range(KC):
                K.dma(SP, hst_st[k % 2], HST[k % 2], hsp[k], reads=[hsp_res[k]], writes=[HST_res[k % 2]])
                K.V(lambda k=k: nc.vector.scalar_tensor_tensor(out=H[:, k, :], in0=H[:, k, :], scalar=gcol_fn(k),
                                                               in1=RS, op0=ALU.mult, op1=ALU.mult),
                    reads=[RS_res, const_res], writes=[H_res[k]])
                K.V(lambda k=k: nc.vector.tensor_tensor(out=H[:, k, :], in0=H[:, k, :], in1=HST[k % 2], op=ALU.add),
                    reads=[HST_res[k % 2]], writes=[H_res[k]])

        def ffn(pre, g_pre, gh_idx):
            prenorm(g_pre)
            spill_H()
            ssi = [0]
            for hf in range(2):
                for f in range(NHALF):
                    pi, pres, (pg, pu) = ws.get(pre + "gu")
                    for ti, (a, b) in enumerate(TT):
                        n = b - a
                        bg, bu = balloc(), balloc()
                        for k in range(KC):
                            K.mm(banks[bg][:, 0:n], pg[:, k, :], U[:, k, a:b], start=(k == 0), stop=(k == KC - 1),
                                 reads=[pres, U_res], out_res=bank_res[bg], sig=False)
                        for k in range(KC):
                            K.mm(banks[bu][:, 0:n], pu[:, k, :], U[:, k, a:b], start=(k == 0), stop=(k == KC - 1),
                                 reads=[pres, U_res], out_res=bank_res[bu], sig=(k == KC - 1))
                        s = ssi[0] % 2
                        ssi[0] += 1
                        K.A(lambda bg=bg, n=n, s=s: nc.scalar.activation(out=SS[s][:, 0:n], in_=banks[bg][:, 0:n],
                                                                         func=AF.Silu),
                            reads=[bank_res[bg]], writes=[SS_res[s]])
                        K.V(lambda bu=bu, n=n, s=s, f=f, a=a, b=b: nc.vector.tensor_tensor(
                            out=AH[:, f, a:b], in0=SS[s][:, 0:n], in1=banks[bu][:, 0:n], op=ALU.mult),
                            reads=[SS_res[s], bank_res[bu]], writes=[AH_res[f]])
                        bfree(bg)
                        bfree(bu)
                    ws.done(pi)
                for d in range(KC):
                    pi, pres, (pd,) = ws.get(pre + "dn")
                    for ti, (a, b) in enumerate(TT):
                        n = b - a
                        bk = balloc()
                        for k in range(NHALF):
                            K.mm(banks[bk][:, 0:n], pd[:, k, :], AH[:, k, a:b], start=(k == 0), stop=(k == NHALF - 1),
                                 reads=[pres, AH_res[k]], out_res=bank_res[bk], sig=(k == NHALF - 1))
                        if hf == 0:
                            K.A(lambda bk=bk, n=n, d=d, a=a, b=b: nc.scalar.copy(out=H[:, d, a:b], in_=banks[bk][:, 0:n]),
                                reads=[bank_res[bk]], writes=[H_res[d]])
                        else:
                            K.V(lambda bk=bk, n=n, d=d, a=a, b=b: nc.vector.tensor_tensor(
                                out=H[:, d, a:b], in0=H[:, d, a:b], in1=banks[bk][:, 0:n], op=ALU.add),
                                reads=[bank_res[bk]], writes=[H_res[d]])
                        bfree(bk)
                    ws.done(pi)
            post_residual(lambda k: GH[:, gh_idx * KC + k:gh_idx * KC + k + 1])

        XDT = [calloc([6, 64], BF) for _ in range(2)]
        XDD = [calloc([6, 64], BF) for _ in range(2)]
        E32 = [calloc([3, 128], BF) for _ in range(2)]
        CBM = [calloc([128], F32) for _ in range(2)]
        XDT_res = [Res("XDT0"), Res("XDT1")]
        XDD_res = [Res("XDD0"), Res("XDD1")]
        E32_res = [Res("E320"), Res("E321")]
        CBM_res = [Res("CBM0"), Res("CBM1")]
        DG = [calloc([128], BF) for _ in range(4)]
        DG_res = [Res("DG%d" % i) for i in range(4)]
        FLAG = calloc([1], F32)
        K.dma(SP, ld_st, FLAG, flag_d, writes=[const_res])
        NEGM = calloc([128], F32)
        K.dma(SP, ld_st, NEGM, negm_d, writes=[const_res])
        assert coff[0] <= ARENA_B, coff[0]
        misc_st = K.stream("misc")
        slab_st = [K.stream("slab0"), K.stream("slab1"), K.stream("slab2")]
        sout_st = K.stream("sout")
        pt_st = K.stream("ptld")
        xs_st = K.stream("xsrc")
        xd_st = K.stream("xdst")
        cc_st = K.stream("cc")
        cc_st.inc = 1
        PAIRS = [[0, 1], [2, 3], [4, 5], [6, 7]]
        xch_i = [0]

        def exchange(src_ap, src_res, width, dst_ap, dst_res):
            i = xch_i[0]
            xch_i[0] += 1
            xsrc = nc.dram_tensor("xsrc%d" % i, [128, width], F32, kind="Internal").ap()
            xdst = nc.dram_tensor("xdst%d" % i, [256, width], F32, kind="Internal").ap()
            r1, r2 = Res("xsrc%d" % i), Res("xdst%d" % i)
            xs_v, xd_v = xsrc, xdst[0:128, :]
            if len(src_ap.shape) == 3:
                xs_v = xsrc.rearrange("p (a b) -> p a b", a=src_ap.shape[1])
                xd_v = xdst[0:128, :].rearrange("p (a b) -> p a b", a=src_ap.shape[1])
            K.dma(SP, xs_st, xs_v, src_ap, reads=[src_res], writes=[r1])
            K._pre(POOL, [r1], [r2])
            ins = nc.gpsimd.collective_compute("AllGather", ALU.bypass, replica_groups=PAIRS, ins=[xsrc],
                                               outs=[xdst])
            ins.then_inc(cc_st.sem, 1)
            cc_st.n += 1
            tok = ("d", cc_st, cc_st.n)
            K._post(tok, [r1], [r2])
            K.dma(SP, xd_st, dst_ap, xd_v, reads=[r2], writes=[dst_res])

        def linear_fm(tag, in_fn, in_res_fn, nk, evac):
            pi, pres, (pw,) = ws.get(tag)
            for ti, (a, b) in enumerate(TT):
                bk = balloc()
                for k in range(nk):
                    K.mm(banks[bk][:, 0:b - a], pw[:, k, :], in_fn(k, a, b), start=(k == 0), stop=(k == nk - 1),
                         reads=[pres, in_res_fn(k)], out_res=bank_res[bk], sig=(k == nk - 1))
                evac(ti, a, b, bk)
                bfree(bk)
            ws.done(pi)

        def mixer(p):
            prenorm(2)
            spill_H()
            K.barrier()
            hoff = [OFF_H]

            def halloc(shape, dt):
                esz = 4 if dt == F32 else 2
                n = (int(np.prod(shape)) * esz + 3) // 4 * 4
                o = hoff[0]
                hoff[0] += n
                assert hoff[0] <= OFF_X, ("H scratch overflow", hoff[0] - OFF_X)
                return view(o, shape, dt)

            LA = TT[-1][0]
            assert LA <= TP - 30
            DT = halloc([9, NH], F32)
            DTA = halloc([9, NH], F32)
            NACS = halloc([8, NH], F32)
            EXA = halloc([8, NH], F32)
            CD = halloc([8, NH], F32)
            W2 = halloc([8, NH], F32)
            ETOT = halloc([8, NH], F32)
            CDTOT = halloc([NH], F32)
            DECS = halloc([NH], F32)
            dt_res = Res("dt")
            mix_base = hoff[0]

            Vb = halloc([8, 30 + T], BF)
            MSS = [halloc([352], F32) for _ in range(2)]
            VT32 = halloc([8, 30 + TS], F32)
            HV32 = halloc([8, 30], F32)
            ACC = [halloc([T], F32) for _ in range(2)]
            CBF = [halloc([T], BF)] * 2
            CSQ = [halloc([T], BF)] * 2
            MEAN = halloc([T], F32)
            RSTD = halloc([T], F32)
            DGC = view(hoff[0] - 8 * T, [31, 128], BF)
            assert 31 * 128 * 2 <= 8 * T
            STC = [halloc([TS, 30], F32) for _ in range(2)]
            TMPS = halloc([TS, 30], F32)
            NB4 = TS // 4
            STG = [view(OFF_X + 8 * 2 * T + 4 * 8 * T + 4096 + i * 4 * NB4 * 128, [NB4, 128], F32) for i in range(2)]
            assert OFF_X + 8 * 2 * T + 4 * 8 * T + 4096 + 2 * 4 * NB4 * 128 <= OFF_C
            CO = view(OFF_X + 8 * 2 * T, [8, T], F32)
            TROW = view(OFF_X + 8 * 2 * T + 4 * 8 * T, [1024], F32)
            assert OFF_X + 8 * 2 * T + 4 * 8 * T + 4096 <= OFF_C
            Vb_res = [Res("Vb%d" % i) for i in range(8)]
            MSS_res = [Res("MSS0"), Res("MSS1")]
            vt_res = Res("vt32")
            hv_res = Res("hv32")
            ACC_res = [Res("ACC0"), Res("ACC1")]
            CBF_res = [Res("CBF0")] * 2
            CSQ_res = [Res("CSQ0")] * 2
            STG_res = [Res("STG0"), Res("STG1")]
            STC_res = [Res("stc0"), Res("stc1")]
            CO_res = [Res("CO%d" % i) for i in range(8)]
            ln_res = Res("ln")
            trow_res = Res("trow")
            tmps_res = Res("tmps")

            def cwc(cc, j):
                return CV[:, CV_CW + cc * 31 + j:CV_CW + cc * 31 + j + 1]

            mi = [0]
            for cc in range(8):
                pi, pres, (pa, pb) = ws.get("glu")
                for ti, (a, b) in enumerate(TT):
                    n = b - a
                    ba, bb = balloc(), balloc()
                    for k in range(KC):
                        K.mm(banks[ba][:, 0:n], pa[:, k, :], U[:, k, a:b], start=(k == 0), stop=(k == KC - 1),
                             reads=[pres, U_res], out_res=bank_res[ba], sig=False)
                    for k in range(KC):
                        K.mm(banks[bb][:, 0:n], pb[:, k, :], U[:, k, a:b], start=(k == 0), stop=(k == KC - 1),
                             reads=[pres, U_res], out_res=bank_res[bb], sig=(k == KC - 1))
                    s = mi[0] % 2
                    mi[0] += 1
                    K.A(lambda bb=bb, n=n, s=s: nc.scalar.activation(out=MSS[s][:, 0:n], in_=banks[bb][:, 0:n],
                                                                     func=AF.Sigmoid),
                        reads=[bank_res[bb]], writes=[MSS_res[s]])
                    K.V(lambda ba=ba, n=n, s=s, cc=cc, a=a, b=b: nc.vector.tensor_tensor(
                        out=Vb[:, cc, 30 + a:30 + b], in0=banks[ba][:, 0:n], in1=MSS[s][:, 0:n], op=ALU.mult),
                        reads=[bank_res[ba], MSS_res[s]], writes=[Vb_res[cc]])
                    if ti == len(TT) - 1:
                        K.V(lambda ba=ba, s=s, cc=cc, a=a: nc.vector.tensor_tensor(
                            out=VT32[:, cc, :], in0=banks[ba][:, TP - 30 - a:T - a], in1=MSS[s][:, TP - 30 - a:T - a],
                            op=ALU.mult), reads=[bank_res[ba], MSS_res[s]], writes=[vt_res])
                    bfree(ba)
                    bfree(bb)
                ws.done(pi)
            exchange(VT32[:, :, 0:30], vt_res, 240, HV32, hv_res)

            K.V(lambda: nc.vector.memset(DT, 0.0), writes=[dt_res])
            K.V(lambda: nc.vector.memset(DTA, 0.0), writes=[dt_res])
            pi, pres, (pdt,) = ws.get("dt")
            for c in range(9):
                M = 128 if c < 8 else TS
                bk = balloc()
                for k in range(KC):
                    K.mm(banks[bk][0:M, 0:NH], U[:, k, c * 128:c * 128 + M], pdt[:, k, :], start=(k == 0),
                         stop=(k == KC - 1), reads=[pres, U_res], out_res=bank_res[bk], sig=(k == KC - 1))
                K.V(lambda c=c, M=M, bk=bk: nc.vector.tensor_tensor(out=DT[0:M, c, :], in0=banks[bk][0:M, 0:NH],
                                                                     in1=CV[0:M, CV_BC:CV_BC + NH], op=ALU.add),
                    reads=[bank_res[bk], const_res], writes=[dt_res])
                bfree(bk)
            ws.done(pi)
            K.A(lambda: nc.scalar.activation(out=DT, in_=DT, func=AF.Exp), reads=[dt_res], writes=[dt_res])
            K.A(lambda: nc.scalar.activation(out=DT, in_=DT, func=AF.Ln, bias=1.0, scale=1.0), reads=[dt_res],
                writes=[dt_res])
            K.V(lambda: nc.vector.tensor_tensor(out=DTA, in0=DT, in1=ANEG.unsqueeze(1).to_broadcast([128, 9, NH]),
                                                op=ALU.mult), reads=[dt_res, const_res], writes=[dt_res])
            b1, b2 = balloc(), balloc()
            for c in range(8):
                K.mm(banks[b1][:, c * NH:(c + 1) * NH], TRI, DTA[:, c, :], start=True, stop=True,
                     reads=[dt_res, const_res], out_res=bank_res[b1], sig=False)
            for c in range(8):
                K.mm(banks[b2][:, c * NH:(c + 1) * NH], ONE, DTA[:, c, :], start=True, stop=True,
                     reads=[dt_res, const_res], out_res=bank_res[b2], sig=(c == 7))
            K.A(lambda: nc.scalar.activation(out=EXA, in_=banks[b1][:, 0:8 * NH], func=AF.Exp),
                reads=[bank_res[b1]], writes=[dt_res])
            K.V(lambda: nc.vector.tensor_scalar(out=NACS, in0=banks[b1][:, 0:8 * NH], scalar1=-1.0, scalar2=None,
                                                op0=ALU.mult), reads=[bank_res[b1]], writes=[dt_res])
            K.A(lambda: nc.scalar.activation(out=CD, in_=banks[b2][:, 0:8 * NH], func=AF.Exp),
                reads=[bank_res[b2]], writes=[dt_res])
            K.V(lambda: nc.vector.tensor_tensor(out=W2, in0=banks[b2][:, 0:8 * NH], in1=NACS, op=ALU.add),
                reads=[bank_res[b2], dt_res], writes=[dt_res])
            K.A(lambda: nc.scalar.activation(out=W2, in_=W2, func=AF.Exp), reads=[dt_res], writes=[dt_res])
            K.V(lambda: nc.vector.tensor_tensor(out=W2, in0=W2, in1=DT[:, 0:8, :], op=ALU.mult), reads=[dt_res],
                writes=[dt_res])
            K.A(lambda: nc.scalar.activation(out=DECS[0:TS, :], in_=DTA[0:TS, 8, :], func=AF.Exp), reads=[dt_res],
                writes=[dt_res])
            bfree(b1)
            bfree(b2)
            K.V(lambda: nc.vector.tensor_copy(out=ETOT[:, 0, :], in_=EXA[:, 0, :]), reads=[dt_res], writes=[dt_res])
            K.V(lambda: nc.vector.tensor_copy(out=CDTOT, in_=CD[:, 0, :]), reads=[dt_res], writes=[dt_res])
            for c in range(1, 8):
                K.V(lambda c=c: nc.vector.tensor_tensor(out=ETOT[:, c, :], in0=EXA[:, c, :], in1=CDTOT, op=ALU.mult),
                    reads=[dt_res], writes=[dt_res])
                K.V(lambda c=c: nc.vector.tensor_tensor(out=CDTOT, in0=CDTOT, in1=CD[:, c, :], op=ALU.mult),
                    reads=[dt_res], writes=[dt_res])

            K.dma(SP, out_st, o_xbs[p][:, 0:2, :], st_xbc[p].rearrange("(b j) c -> b j c", j=3)[:, 1:3, :])
            K.dma(SP, out_st, o_cms[p][:, 0:29, :], st_conv[p][:, 1:30, :])

            K.V(lambda: nc.vector.tensor_scalar(out=Vb[:, :, 0:30], in0=HV32, scalar1=FLAG[:, 0:1], scalar2=None,
                                                op0=ALU.mult), reads=[hv_res, const_res], writes=Vb_res)
            for cc in range(8):
                stc, stcr = STC[cc % 2], STC_res[cc % 2]
                sg, sr = STG[cc % 2], STG_res[cc % 2]
                for q4 in range(NB4):
                    K.dma(SP, misc_st, sg[0:120, q4, :],
                          st_conv[p][q4 * 4:(q4 + 1) * 4, :, cc * 128:(cc + 1) * 128].rearrange("b j c -> (b j) c"),
                          writes=[sr])
                bk = balloc()
                for q4 in range(NB4):
                    K.mm(banks[bk][:, q4 * 120:(q4 + 1) * 120], sg[0:120, q4, :], IDN[0:120, 0:120], start=True,
                         stop=True, reads=[sr, const_res], out_res=bank_res[bk], sig=(q4 == NB4 - 1), transpose=True)
                K.A(lambda bk=bk, stc=stc: nc.scalar.copy(out=stc.rearrange("p b j -> p (b j)"),
                                                          in_=banks[bk][:, 0:TS * 30]),
                    reads=[bank_res[bk]], writes=[stcr])
                bfree(bk)
                acc = ACC[cc % 2]
                ar = ACC_res[cc % 2]
                K.V(lambda cc=cc: nc.vector.tensor_tensor(
                    out=DGC, in0=IDB.unsqueeze(1).to_broadcast([128, 31, 128]),
                    in1=CV[:, CV_CW + cc * 31:CV_CW + cc * 31 + 31].unsqueeze(2).to_broadcast([128, 31, 128]),
                    op=ALU.mult), reads=[const_res], writes=[ln_res])
                for ti, (a, b) in enumerate(TT):
                    n = b - a
                    bk = balloc()
                    for j in range(31):
                        K.mm(banks[bk][:, 0:n], DGC[:, j, :], Vb[:, cc, j + a:j + b], start=(j == 0), stop=(j == 30),
                             reads=[ln_res, Vb_res[cc]], out_res=bank_res[bk], sig=(j == 30))
                    pe_ = min(b, TP)
                    K.A(lambda cc=cc, bk=bk, a=a, pe_=pe_: nc.scalar.activation(
                        out=CO[:, cc, a:pe_], in_=banks[bk][:, 0:pe_ - a], func=AF.Identity,
                        bias=CV[:, CV_CB + cc:CV_CB + cc + 1], scale=1.0),
                        reads=[bank_res[bk], const_res, STG_res[0], STG_res[1]], writes=[CO_res[cc]])
                    bfree(bk)
                K.V(lambda cc=cc, stc=stc: nc.vector.tensor_tensor(
                    out=TMPS, in0=stc,
                    in1=CV[:, CV_CW + cc * 31:CV_CW + cc * 31 + 30].unsqueeze(1).to_broadcast([128, TS, 30]),
                    op=ALU.mult), reads=[stcr, const_res], writes=[tmps_res])
                K.V(lambda acc=acc: nc.vector.tensor_reduce(out=acc[:, TP:T], in_=TMPS, axis=AX.X, op=ALU.add),
                    reads=[tmps_res], writes=[ar])
                K.V(lambda cc=cc, acc=acc: nc.vector.scalar_tensor_tensor(
                    out=acc[:, TP:T], in0=Vb[:, cc, 30 + TP:30 + T], scalar=cwc(cc, 30), in1=acc[:, TP:T],
                    op0=ALU.mult, op1=ALU.add), reads=[Vb_res[cc], const_res], writes=[ar])
                K.V(lambda cc=cc, acc=acc: nc.vector.tensor_scalar(
                    out=acc[:, TP:T], in0=acc[:, TP:T], scalar1=CV[:, CV_CB + cc:CV_CB + cc + 1], scalar2=None,
                    op0=ALU.add), reads=[const_res], writes=[ar])
                K.A(lambda cc=cc, acc=acc: nc.scalar.copy(out=CO[:, cc, TP:T], in_=acc[:, TP:T]),
                    reads=[ar, STG_res[0], STG_res[1]], writes=[CO_res[cc]])
            bm = [balloc() for _ in TT]
            bq = [balloc() for _ in TT]
            for cc in range(8):
                x = cc % 2
                K.A(lambda cc=cc, x=x: nc.scalar.copy(out=CBF[x], in_=CO[:, cc, :]), reads=[CO_res[cc]],
                    writes=[CBF_res[x]])
                K.A(lambda cc=cc, x=x: nc.scalar.activation(out=CSQ[x], in_=CO[:, cc, :], func=AF.Square),
                    reads=[CO_res[cc]], writes=[CSQ_res[x]])
                for ti, (a, b) in enumerate(TT):
                    K.mm(banks[bm[ti]][:, 0:b - a], ONB, CBF[x][:, a:b], start=(cc == 0), stop=(cc == 7),
                         reads=[CBF_res[x], const_res], out_res=bank_res[bm[ti]], sig=False)
                for ti, (a, b) in enumerate(TT):
                    K.mm(banks[bq[ti]][:, 0:b - a], ONB, CSQ[x][:, a:b], start=(cc == 0), stop=(cc == 7),
                         reads=[CSQ_res[x], const_res], out_res=bank_res[bq[ti]], sig=(ti == len(TT) - 1))
            for ti, (a, b) in enumerate(TT):
                K.A(lambda ti=ti, a=a, b=b: nc.scalar.mul(out=MEAN[:, a:b], in_=banks[bm[ti]][:, 0:b - a],
                                                          mul=1.0 / DCV), reads=[bank_res[bm[ti]]], writes=[ln_res])
                K.V(lambda ti=ti, a=a, b=b: nc.vector.tensor_scalar(out=RSTD[:, a:b], in0=banks[bq[ti]][:, 0:b - a],
                                                                    scalar1=1.0 / DCV, scalar2=None, op0=ALU.mult),
                    reads=[bank_res[bq[ti]]], writes=[ln_res])
            for b_ in bm + bq:
                bfree(b_)
            K.V(lambda: nc.vector.tensor_tensor(out=ACC[0], in0=MEAN, in1=MEAN, op=ALU.mult), reads=[ln_res],
                writes=[ACC_res[0]])
            K.V(lambda: nc.vector.tensor_tensor(out=RSTD, in0=RSTD, in1=ACC[0], op=ALU.subtract),
                reads=[ACC_res[0], ln_res], writes=[ln_res])
            K.A(lambda: nc.scalar.activation(out=RSTD, in_=RSTD, func=AF.Ln, bias=EPSC[:, 0:1], scale=1.0),
                reads=[ln_res, const_res], writes=[ln_res])
            K.A(lambda: nc.scalar.activation(out=RSTD, in_=RSTD, func=AF.Exp, scale=-0.5), reads=[ln_res],
                writes=[ln_res])
            for cc in range(8):
                acc = ACC[cc % 2]
                ar = ACC_res[cc % 2]
                K.V(lambda cc=cc, acc=acc: nc.vector.tensor_tensor(out=acc, in0=CO[:, cc, :], in1=MEAN,
                                                                   op=ALU.subtract),
                    reads=[CO_res[cc], ln_res], writes=[ar])
                K.V(lambda acc=acc: nc.vector.tensor_tensor(out=acc, in0=acc, in1=RSTD, op=ALU.mult),
                    reads=[ln_res], writes=[ar])
                K.A(lambda cc=cc, acc=acc: nc.scalar.activation(out=INT[:, cc, :], in_=acc, func=AF.Silu,
                                                                bias=CV[:, CV_LB + cc:CV_LB + cc + 1],
                                                                scale=CV[:, CV_LG + cc:CV_LG + cc + 1]),
                    reads=[ar, const_res], writes=[INT_res[cc]])
            bk0, bk1 = balloc(), balloc()
            for cc in range(8):
                bk = bk0 if cc < 4 else bk1
                K.mm(banks[bk][0:30 + TS, (cc % 4) * 128:(cc % 4 + 1) * 128], VT32[:, cc, :], IDN, start=True,
                     stop=True, reads=[vt_res, const_res], out_res=bank_res[bk], sig=(cc % 4 == 3), transpose=True)
            K.A(lambda: nc.scalar.copy(out=TROW[0:30 + TS, 0:512], in_=banks[bk0][0:30 + TS, :]),
                reads=[bank_res[bk0]], writes=[trow_res])
            K.A(lambda: nc.scalar.copy(out=TROW[0:30 + TS, 512:1024], in_=banks[bk1][0:30 + TS, :]),
                reads=[bank_res[bk1]], writes=[trow_res])
            bfree(bk0)
            bfree(bk1)
            K.dma(SP, out_st, o_cmp[p], TROW[0:30, :], reads=[trow_res])
            K.dma(SP, out_st, o_cms[p][:, 29, :], TROW[30:30 + TS, :], reads=[trow_res])
            K.barrier()

            hoff[0] = mix_base
            XP = [halloc([3 + T], BF) for _ in range(2)]
            XT32 = halloc([5, 3 + TS], F32)
            XF = halloc([5, T], BF)
            YT = halloc([3, T], F32)
            ACX = [YT[:, 0, :], YT[:, 1, :]]
            ZT = halloc([3, T], BF)
            XTM = [halloc([512], BF) for _ in range(2)]
            HT = halloc([384], F32)
            HTB = halloc([384], BF)
            HIN = halloc([384], F32)
            HINB = halloc([384], BF)
            MTALL = halloc([4 * 384], BF)
            MT = [MTALL[:, i * 384:(i + 1) * 384].rearrange("p (a b) -> p a b", a=3) for i in range(4)]
            Y1ALL = halloc([768], F32)
            Y1 = [Y1ALL[:, 0:384], Y1ALL[:, 384:768]]
            XROW = Y1ALL[:, 0:640]
            SSTG = Y1ALL[:, 0:640]
            SCTG = halloc([5, TS * 3], F32)
            TM3 = halloc([TS, 3], F32)
            EXPD = MTALL.bitcast(F32).rearrange("p (a b) -> p a b", a=6)
            DTS = halloc([6, TS], F32)
            DX = halloc([3, TS], F32)
            YS = halloc([3, TS], F32)
            SL = [halloc([3, 128], F32) for _ in range(3)]
            XS32 = halloc([2, TS], F32)
            T2 = [halloc([3, 128], F32)] * 2
            SROW = T2[0]
            GSQ = [XP[0][:, 0:T], XP[1][:, 0:T]]
            ACC3 = halloc([5, 3], F32)
            HX = halloc([5, 3], F32)
            HP = halloc([5, 6], F32)
            XP_res = [Res("XP0"), Res("XP1")]
            xt_res = Res("xt32")
            XF_res = [Res("XF%d" % i) for i in range(5)]
            ZT_res = [Res("ZT%d" % i) for i in range(3)]
            YT_res = [Res("YT%d" % i) for i in range(3)]
            ACX_res = [YT_res[0], YT_res[1]]
            XTM_res = [Res("XTM0"), Res("XTM1")]
            ht_res = Res("HT")
            htb_res = Res("HTB")
            hin_res = Res("HIN")
            hinb_res = Res("HINB")
            MT_res = [Res("MT%d" % i) for i in range(4)]
            Y1_res = [Res("Y10"), Res("Y11")]
            sct_res = Res("sctg")
            tm3_res = Res("tm3")
            smp_res = Res("smp")
            ys_res = Res("ys")
            SL_res = [Res("SL0"), Res("SL1"), Res("SL2")]
            T2_res = [Res("T20")] * 2
            srow_res = T2_res[0]
            GSQ_res = XP_res
            a3_res = Res("acc3")
            hx_res = Res("hx")
            hp_res = Res("hp")
            xi = [0]
            ci = [0]
            print("H scratch free bytes (SSD):", OFF_X - hoff[0], "const free:", ARENA_B - coff[0])
            K.V(lambda: nc.vector.memset(HP, 0.0), writes=[hp_res])

            for g in range(NG):
                h0 = g * HPG
                chs = [3 * g, 3 * g + 1, 3 * g + 2, 24 + g, 32 + g]
                for (c0, w_, s0) in [(g * 384, 384, 0), (DSSM + g * 128, 128, 384), (DSSM + 1024 + g * 128, 128, 512)]:
                    K.dma(SP, misc_st, SSTG[0:TS * 3, s0:s0 + w_], st_xbc[p][:, c0:c0 + w_], writes=Y1_res)
                bk = balloc()
                for i in range(5):
                    K.mm(banks[bk][:, i * TS * 3:(i + 1) * TS * 3], SSTG[0:TS * 3, i * 128:(i + 1) * 128],
                         IDN[0:TS * 3, 0:TS * 3], start=True, stop=True, reads=Y1_res + [const_res],
                         out_res=bank_res[bk], sig=(i == 4), transpose=True)
                K.A(lambda bk=bk: nc.scalar.copy(out=SCTG.rearrange("p i x -> p (i x)"), in_=banks[bk][:, 0:5 * TS * 3]),
                    reads=[bank_res[bk]], writes=[sct_res])
                bfree(bk)

                def xbc_chunk(i, ch, pw, pres):
                    x = xi[0] % 2
                    xi[0] += 1
                    xp, xr = XP[x], XP_res[x]
                    for ti, (a, b) in enumerate(TT):
                        bk = balloc()
                        for k in range(KC):
                            K.mm(banks[bk][:, 0:b - a], pw[:, k, :], U[:, k, a:b], start=(k == 0), stop=(k == KC - 1),
                                 reads=[pres, U_res], out_res=bank_res[bk], sig=(k == KC - 1))
                        K.A(lambda bk=bk, a=a, b=b, xp=xp: nc.scalar.copy(out=xp[:, 3 + a:3 + b],
                                                                          in_=banks[bk][:, 0:b - a]),
                            reads=[bank_res[bk]], writes=[xr])
                        if ti == len(TT) - 1:
                            K.V(lambda bk=bk, a=a, i=i: nc.vector.tensor_copy(out=XT32[:, i, :],
                                                                              in_=banks[bk][:, TP - 3 - a:T - a]),
                                reads=[bank_res[bk]], writes=[xt_res])
                        bfree(bk)
                    K.V(lambda xp=xp: nc.vector.memset(xp[:, 0:3], 0.0), writes=[xr])
                    acc, ar = ACX[x], ACX_res[x]

                    def swc(j):
                        return CV[:, CV_SW + ch * 4 + j:CV_SW + ch * 4 + j + 1]
                    sbc = CV[:, CV_SB + ch:CV_SB + ch + 1]
                    K.V(lambda: nc.vector.tensor_scalar(out=acc[:, 0:TP], in0=xp[:, 0:TP], scalar1=swc(0), scalar2=sbc,
                                                        op0=ALU.mult, op1=ALU.add), reads=[xr, const_res], writes=[ar])
                    for j in range(1, 4):
                        K.V(lambda j=j: nc.vector.scalar_tensor_tensor(out=acc[:, 0:TP], in0=xp[:, j:j + TP],
                                                                       scalar=swc(j), in1=acc[:, 0:TP], op0=ALU.mult,
                                                                       op1=ALU.add), reads=[xr, const_res], writes=[ar])
                    K.V(lambda: nc.vector.tensor_tensor(
                        out=TM3, in0=SCTG[:, i, :].rearrange("p (b j) -> p b j", j=3),
                        in1=CV[:, CV_SW + ch * 4:CV_SW + ch * 4 + 3].unsqueeze(1).to_broadcast([128, TS, 3]),
                        op=ALU.mult), reads=[sct_res, const_res], writes=[tm3_res])
                    K.V(lambda: nc.vector.tensor_reduce(out=acc[:, TP:T], in_=TM3, axis=AX.X, op=ALU.add),
                        reads=[tm3_res], writes=[ar])
                    K.V(lambda: nc.vector.scalar_tensor_tensor(out=acc[:, TP:T], in0=xp[:, 3 + TP:3 + T], scalar=swc(3),
                                                               in1=acc[:, TP:T], op0=ALU.mult, op1=ALU.add),
                        reads=[xr, const_res], writes=[ar])
                    K.V(lambda: nc.vector.tensor_scalar(out=acc[:, TP:T], in0=acc[:, TP:T], scalar1=sbc, scalar2=None,
                                                        op0=ALU.add), reads=[const_res], writes=[ar])
                    K.V(lambda: nc.vector.tensor_copy(out=ACC3[:, i, :], in_=acc[:, 0:3]), reads=[ar], writes=[a3_res])
                    K.A(lambda: nc.scalar.activation(out=XF[:, i, :], in_=acc, func=AF.Silu), reads=[ar],
                        writes=[XF_res[i]])

                for i in range(3):
                    pi, pres, (pw,) = ws.get("xs")
                    xbc_chunk(i, chs[i], pw, pres)
                    ws.done(pi)
                pi, pres, (pwb, pwc) = ws.get("bc")
                xbc_chunk(3, chs[3], pwb, pres)
                xbc_chunk(4, chs[4], pwc, pres)
                ws.done(pi)
                exchange(XT32[:, :, 0:3], xt_res, 15, HX, hx_res)
                bk, bk2 = balloc(), balloc()
                for i in range(5):
                    bb_ = bk if i < 4 else bk2
                    K.mm(banks[bb_][0:3 + TS, (i % 4) * 128:(i % 4 + 1) * 128], XT32[:, i, :], IDN, start=True,
                         stop=True, reads=[xt_res, const_res], out_res=bank_res[bb_], sig=(i >= 3), transpose=True)
                K.A(lambda bk=bk: nc.scalar.copy(out=XROW[0:3 + TS, 0:512], in_=banks[bk][0:3 + TS, 0:512]),
                    reads=[bank_res[bk]], writes=Y1_res)
                K.A(lambda bk2=bk2: nc.scalar.copy(out=XROW[0:3 + TS, 512:640], in_=banks[bk2][0:3 + TS, 0:128]),
                    reads=[bank_res[bk2]], writes=Y1_res)
                bfree(bk)
                bfree(bk2)
                for (c0, w_, s0) in [(g * 384, 384, 0), (DSSM + g * 128, 128, 384), (DSSM + 1024 + g * 128, 128, 512)]:
                    K.dma(SP, out_st, o_xbp[p][:, c0:c0 + w_], XROW[0:3, s0:s0 + w_], reads=Y1_res)
                    K.dma(SP, out_st, o_xbs[p][:, 2, c0:c0 + w_], XROW[3:3 + TS, s0:s0 + w_], reads=Y1_res)
                for i in range(3):
                    def zevac(ti, a, b, bk, i=i):
                        K.A(lambda: nc.scalar.activation(out=ZT[:, i, a:b], in_=banks[bk][:, 0:b - a], func=AF.Silu),
                            reads=[bank_res[bk]], writes=[ZT_res[i]])
                    linear_fm("z", lambda k, a, b: U[:, k, a:b], lambda k: U_res, KC, zevac)
                K.V(lambda: nc.vector.tensor_scalar(out=HP[:, :, 0:3], in0=HX, scalar1=FLAG[:, 0:1], scalar2=None,
                                                    op0=ALU.mult), reads=[hx_res, const_res], writes=[hp_res])
                for i in range(5):
                    ch = chs[i]
                    for j in range(3):
                        K.V(lambda i=i, j=j, ch=ch: nc.vector.scalar_tensor_tensor(
                            out=ACC3[:, i, :], in0=HP[:, i, j:j + 3],
                            scalar=CV[:, CV_SW + ch * 4 + j:CV_SW + ch * 4 + j + 1], in1=ACC3[:, i, :], op0=ALU.mult,
                            op1=ALU.add), reads=[hp_res, const_res], writes=[a3_res])
                    K.A(lambda i=i: nc.scalar.activation(out=XF[:, i, 0:3], in_=ACC3[:, i, :], func=AF.Silu),
                        reads=[a3_res], writes=[XF_res[i]])
                K.V(lambda: nc.vector.memset(HT, 0.0), writes=[ht_res])
                K.V(lambda: nc.vector.memset(HTB, 0.0), writes=[htb_res])
                dsk = CV[:, CV_BC + 96 + h0:CV_BC + 96 + h0 + HPG].unsqueeze(2).to_broadcast([128, HPG, 64])
                def stage_a(c):
                    cs = slice(c * 128, (c + 1) * 128)
                    x = c % 2
                    bk = balloc()
                    bkv = banks[bk][:, 0:256].bitcast(BF)
                    for i in range(4):
                        K.mm(bkv[:, i * 128:(i + 1) * 128], XF[:, i, cs], IDB, start=True, stop=True,
                             reads=[XF_res[i], const_res], out_res=bank_res[bk], sig=(i == 3), transpose=True)
                    K.A(lambda bkv=bkv, x=x: nc.scalar.copy(out=XTM[x], in_=bkv), reads=[bank_res[bk]],
                        writes=[XTM_res[x]])
                    bfree(bk)
                    xs3 = XTM[x][:, 0:384].rearrange("p (r q) -> p r q", r=HPG)
                    bkc = balloc()
                    K.mm(banks[bkc][:, 0:128], XF[:, 3, cs], XF[:, 4, cs], start=True, stop=True,
                         reads=[XF_res[3], XF_res[4]], out_res=bank_res[bkc], sig=True)
                    K.V(lambda x=x, c=c, xs3=xs3: nc.vector.tensor_tensor(
                        out=XDT[x], in0=xs3, in1=DT[:, c, h0:h0 + HPG].unsqueeze(2).to_broadcast([128, HPG, 64]),
                        op=ALU.mult), reads=[XTM_res[x], dt_res], writes=[XDT_res[x]])
                    K.V(lambda x=x, c=c, xs3=xs3: nc.vector.tensor_tensor(
                        out=XDD[x], in0=xs3, in1=W2[:, c, h0:h0 + HPG].unsqueeze(2).to_broadcast([128, HPG, 64]),
                        op=ALU.mult), reads=[XTM_res[x], dt_res], writes=[XDD_res[x]])
                    bkd = [balloc(), balloc()]
                    for hh in range(2):
                        for j in range(3):
                            h = h0 + hh * 3 + j
                            K.mm(banks[bkd[hh]][:, j * 128:(j + 1) * 128],
                                 DTA[:, c, h:h + 1].to_broadcast([128, 128]), TRI, start=True, stop=False,
                                 reads=[dt_res, const_res], out_res=bank_res[bkd[hh]], sig=False)
                            K.mm(banks[bkd[hh]][:, j * 128:(j + 1) * 128], IDN, NEGM, start=False, stop=True,
                                 reads=[const_res], out_res=bank_res[bkd[hh]], sig=(j == 2))
                    for hh in range(2):
                        bk = bkd[hh]
                        e = E32[hh]
                        for j in range(3):
                            h = h0 + hh * 3 + j
                            K.A(lambda bk=bk, j=j, h=h, e=e, c=c: nc.scalar.activation(
                                out=e[:, j, :], in_=banks[bk][:, j * 128:(j + 1) * 128], func=AF.Exp,
                                bias=NACS[:, c, h:h + 1], scale=1.0),
                                reads=[bank_res[bk], dt_res], writes=[E32_res[hh]])
                        bfree(bk)
                        m = MT[2 * x + hh]
                        K.V(lambda e=e, m=m, bkc=bkc: nc.vector.tensor_tensor(
                            out=m, in0=e, in1=banks[bkc][:, 0:128].unsqueeze(1).to_broadcast([128, 3, 128]),
                            op=ALU.mult), reads=[E32_res[hh], bank_res[bkc]], writes=[MT_res[2 * x + hh]])
                    bfree(bkc)

                def stage_b(c):
                    cs = slice(c * 128, (c + 1) * 128)
                    x = c % 2
                    xs3 = XTM[x][:, 0:384].rearrange("p (r q) -> p r q", r=HPG)
                    bks = balloc()
                    K.mm(banks[bks][:, 0:384], XTM[x][:, 384:512], XDD[x].rearrange("p r q -> p (r q)"), start=True,
                         stop=True, reads=[XTM_res[x], XDD_res[x]], out_res=bank_res[bks], sig=True)
                    bky = balloc()
                    for r in range(HPG):
                        m = MT[2 * x + r // 3]
                        K.mm(banks[bky][:, r * 64:(r + 1) * 64], m[:, r % 3, :], XDT[x][:, r, :], start=True, stop=True,
                             reads=[MT_res[2 * x + r // 3], XDT_res[x]], out_res=bank_res[bky], sig=(r == HPG - 1))
                    bko = None
                    if c > 0:
                        bko = balloc()
                        K.mm(banks[bko][:, 0:384], XF[:, 4, cs], HTB, start=True, stop=True, reads=[XF_res[4], htb_res],
                             out_res=bank_res[bko], sig=True)
                    if c > 0:
                        K.V(lambda c=c: nc.vector.tensor_tensor(
                            out=HT.rearrange("p (r q) -> p r q", r=HPG), in0=HT.rearrange("p (r q) -> p r q", r=HPG),
                            in1=CD[:, c, h0:h0 + HPG].unsqueeze(2).to_broadcast([128, HPG, 64]), op=ALU.mult),
                            reads=[dt_res], writes=[ht_res])
                    K.V(lambda bks=bks: nc.vector.tensor_tensor(out=HT, in0=HT, in1=banks[bks][:, 0:384], op=ALU.add),
                        reads=[bank_res[bks]], writes=[ht_res])
                    bfree(bks)
                    if c < NCH - 1:
                        K.A(lambda: nc.scalar.copy(out=HTB, in_=HT), reads=[ht_res], writes=[htb_res])
                    if c > 0:
                        K.V(lambda bko=bko, x=x, c=c: nc.vector.tensor_tensor(
                            out=Y1[x].rearrange("p (r q) -> p r q", r=HPG),
                            in0=banks[bko][:, 0:384].rearrange("p (r q) -> p r q", r=HPG),
                            in1=EXA[:, c, h0:h0 + HPG].unsqueeze(2).to_broadcast([128, HPG, 64]), op=ALU.mult),
                            reads=[bank_res[bko], dt_res], writes=[Y1_res[x]])
                        bfree(bko)
                        K.V(lambda bky=bky, x=x: nc.vector.tensor_tensor(out=Y1[x], in0=Y1[x],
                                                                         in1=banks[bky][:, 0:384], op=ALU.add),
                            reads=[bank_res[bky]], writes=[Y1_res[x]])
                    else:
                        K.A(lambda bky=bky, x=x: nc.scalar.copy(out=Y1[x], in_=banks[bky][:, 0:384]),
                            reads=[bank_res[bky]], writes=[Y1_res[x]])
                    bfree(bky)

                def stage_b_tail(c):
                    cs = slice(c * 128, (c + 1) * 128)
                    x = c % 2
                    bk = balloc()
                    for i in range(3):
                        K.mm(banks[bk][:, i * 128:(i + 1) * 128], Y1[x][:, i * 128:(i + 1) * 128], IDN, start=True,
                             stop=True, reads=[Y1_res[x], const_res], out_res=bank_res[bk], sig=(i == 2),
                             transpose=True)
                    K.A(lambda bk=bk, cs=cs: nc.scalar.copy(out=YT[:, :, cs],
                                                            in_=banks[bk][:, 0:384].rearrange("p (i q) -> p i q", i=3)),
                        reads=[bank_res[bk]], writes=YT_res)
                    bfree(bk)

                stage_a(0)
                for c in range(NCH):
                    if c + 1 < NCH:
                        stage_a(c + 1)
                    if c > 0:
                        stage_b_tail(c - 1)
                    stage_b(c)
                stage_b_tail(NCH - 1)
                exchange(HT, ht_res, 384, HIN, hin_res)
                for i in range(3):
                    K.V(lambda i=i: nc.vector.scalar_tensor_tensor(
                        out=YT[:, i, 0:TP], in0=XF[:, i, 0:TP], scalar=CV[:, CV_DF + 3 * g + i:CV_DF + 3 * g + i + 1],
                        in1=YT[:, i, 0:TP], op0=ALU.mult, op1=ALU.add),
                        reads=[XF_res[i], const_res], writes=[YT_res[i]])
                K.V(lambda: nc.vector.tensor_copy(
                    out=EXPD[0:TS, 0:3, :].rearrange("p a (h q) -> p (a h) q", h=2),
                    in_=DT[0:TS, 8, h0:h0 + HPG].unsqueeze(2).to_broadcast([TS, HPG, 64])),
                    reads=[dt_res], writes=[smp_res] + MT_res)
                K.V(lambda: nc.vector.tensor_copy(
                    out=EXPD[0:TS, 3:6, :].rearrange("p a (h q) -> p (a h) q", h=2),
                    in_=DECS[0:TS, h0:h0 + HPG].unsqueeze(2).to_broadcast([TS, HPG, 64])),
                    reads=[dt_res], writes=[smp_res] + MT_res)
                bk = balloc()
                for j in range(6):
                    K.mm(banks[bk][:, j * TS:(j + 1) * TS], EXPD[0:TS, j, :], IDN[0:TS, 0:TS], start=True, stop=True,
                         reads=[smp_res, const_res] + MT_res, out_res=bank_res[bk], sig=(j == 5), transpose=True)
                K.A(lambda bk=bk: nc.scalar.copy(out=DTS, in_=banks[bk][:, 0:6 * TS].rearrange("p (j b) -> p j b", j=6)),
                    reads=[bank_res[bk]], writes=[smp_res])
                bfree(bk)
                K.V(lambda: nc.vector.tensor_tensor(out=DX, in0=DTS[:, 0:3, :], in1=XF[:, 0:3, TP:T], op=ALU.mult),
                    reads=[smp_res, XF_res[0], XF_res[1], XF_res[2]], writes=[smp_res])
                K.V(lambda: nc.vector.tensor_copy(out=XS32, in_=XF[:, 3:5, TP:T]), reads=[XF_res[3], XF_res[4]],
                    writes=[smp_res])
                def corr_gen():
                    K.V(lambda: nc.vector.tensor_scalar(out=HIN, in0=HIN, scalar1=FLAG[:, 0:1], scalar2=None, op0=ALU.mult),
                        reads=[const_res], writes=[hin_res])
                    K.A(lambda: nc.scalar.copy(out=HINB, in_=HIN), reads=[hin_res], writes=[hinb_res])
                    for c in range(NCH):
                        cs = slice(c * 128, (c + 1) * 128)
                        x = c % 2
                        bko = balloc()
                        K.mm(banks[bko][:, 0:384], XF[:, 4, cs], HINB, start=True, stop=True, reads=[XF_res[4], hinb_res],
                             out_res=bank_res[bko], sig=True)
                        K.V(lambda bko=bko, x=x, c=c: nc.vector.tensor_tensor(
                            out=Y1[x].rearrange("p (r q) -> p r q", r=HPG),
                            in0=banks[bko][:, 0:384].rearrange("p (r q) -> p r q", r=HPG),
                            in1=ETOT[:, c, h0:h0 + HPG].unsqueeze(2).to_broadcast([128, HPG, 64]), op=ALU.mult),
                            reads=[bank_res[bko], dt_res], writes=[Y1_res[x]])
                        bfree(bko)
                        bk = balloc()
                        for i in range(3):
                            K.mm(banks[bk][:, i * 128:(i + 1) * 128], Y1[x][:, i * 128:(i + 1) * 128], IDN, start=True,
                                 stop=True, reads=[Y1_res[x], const_res], out_res=bank_res[bk], sig=(i == 2),
                                 transpose=True)
                        K.V(lambda bk=bk, cs=cs: nc.vector.tensor_tensor(
                            out=YT[:, :, cs], in0=YT[:, :, cs], in1=banks[bk][:, 0:384].rearrange("p (i q) -> p i q", i=3),
                            op=ALU.add), reads=[bank_res[bk]], writes=YT_res)
                        bfree(bk)
                        yield
                cg = corr_gen()
                def slab_load(bb):
                    K.dma(SP, slab_st[bb % 3], SL[bb % 3],
                          st_ssm[p][bb, g * 384:(g + 1) * 384, :].rearrange("(i q) n -> q i n", q=128),
                          writes=[SL_res[bb % 3]])
                for bb in range(3):
                    slab_load(bb)

                def bc_prep(bb):
                    xx = bb % 2
                    bk_ = balloc()
                    for w_ in range(2):
                        dg, dgr = DG[2 * xx + w_], DG_res[2 * xx + w_]
                        K.A(lambda dg=dg, w_=w_, bb=bb: nc.scalar.activation(
                            out=dg, in_=IDB, func=AF.Copy, scale=XS32[:, w_, bb:bb + 1]),
                            reads=[smp_res, const_res], writes=[dgr])
                        K.mm(banks[bk_][:, w_ * 128:(w_ + 1) * 128], ONB, dg, start=True, stop=True,
                             reads=[dgr, const_res], out_res=bank_res[bk_], sig=(w_ == 1))
                    return bk_
                nxt_bk = bc_prep(0)
                for b_ in range(TS):
                    x = b_ % 2
                    x3 = b_ % 3
                    sl, slr = SL[x3], SL_res[x3]
                    t2, t2r = T2[0], T2_res[0]
                    bk = nxt_bk
                    if b_ + 1 < TS:
                        nxt_bk = bc_prep(b_ + 1)
                    for blk in range(3):
                        K.A(lambda bk=bk, b_=b_, blk=blk: nc.scalar.activation(
                            out=t2[:, blk, :], in_=banks[bk][:, 0:128], func=AF.Copy, scale=DX[:, blk, b_:b_ + 1]),
                            reads=[bank_res[bk], smp_res], writes=[t2r])
                    for blk in range(3):
                        K.V(lambda b_=b_, sl=sl, blk=blk: nc.vector.scalar_tensor_tensor(
                            out=sl[:, blk, :], in0=sl[:, blk, :], scalar=DTS[:, 3 + blk, b_:b_ + 1], in1=t2[:, blk, :],
                            op0=ALU.mult, op1=ALU.add), reads=[smp_res, t2r], writes=[slr])
                    K.dma(SP, sout_st, o_sss[p][b_, g * 384:(g + 1) * 384, :].rearrange("(i q) n -> q i n", q=128), sl,
                          reads=[slr])
                    for blk in range(3):
                        K.V(lambda bk=bk, b_=b_, sl=sl, blk=blk: nc.vector.scalar_tensor_tensor(
                            out=Y1[x][:, blk * 128:(blk + 1) * 128], in0=sl[:, blk, :], scalar=1.0,
                            in1=banks[bk][:, 128:256], op0=ALU.mult, op1=ALU.mult, accum_out=YS[:, blk, b_:b_ + 1]),
                            reads=[slr, bank_res[bk]], writes=[Y1_res[x], ys_res])
                    bfree(bk)
                    if b_ + 3 < TS:
                        slab_load(b_ + 3)
                    if b_ >= 4:
                        next(cg, None)
                for i in range(3):
                    K.V(lambda i=i: nc.vector.scalar_tensor_tensor(
                        out=YT[:, i, TP:T], in0=XF[:, i, TP:T], scalar=CV[:, CV_DF + 3 * g + i:CV_DF + 3 * g + i + 1],
                        in1=YS[:, i, :], op0=ALU.mult, op1=ALU.add),
                        reads=[XF_res[i], ys_res, const_res], writes=[YT_res[i]])
                for _ in cg:
                    pass
                K.V(lambda: nc.vector.tensor_tensor(
                    out=HIN.rearrange("p (r q) -> p r q", r=HPG), in0=HIN.rearrange("p (r q) -> p r q", r=HPG),
                    in1=CDTOT[:, h0:h0 + HPG].unsqueeze(2).to_broadcast([128, HPG, 64]), op=ALU.mult),
                    reads=[dt_res, hinb_res], writes=[hin_res])
                K.V(lambda: nc.vector.tensor_tensor(out=HT, in0=HT, in1=HIN, op=ALU.add), reads=[hin_res],
                    writes=[ht_res])
                bk = balloc()
                for i in range(3):
                    K.mm(banks[bk][:, i * 128:(i + 1) * 128], HT[:, i * 128:(i + 1) * 128], IDN, start=True, stop=True,
                         reads=[ht_res, const_res], out_res=bank_res[bk], sig=(i == 2), transpose=True)
                K.A(lambda bk=bk: nc.scalar.copy(out=SROW, in_=banks[bk][:, 0:384].rearrange("p (i q) -> p i q", i=3)),
                    reads=[bank_res[bk]], writes=[srow_res])
                bfree(bk)
                K.dma(SP, out_st, o_ssp[p][g * 384:(g + 1) * 384, :].rearrange("(i q) n -> q i n", q=128), SROW,
                      reads=[srow_res])
                for i in range(3):
                    K.V(lambda i=i: nc.vector.tensor_tensor(out=YT[:, i, :], in0=YT[:, i, :], in1=ZT[:, i, :],
                                                            op=ALU.mult), reads=[ZT_res[i]], writes=[YT_res[i]])
                bks = [balloc() for _ in TT]
                for i in range(3):
                    x = i % 2
                    K.A(lambda i=i, x=x: nc.scalar.activation(out=GSQ[x], in_=YT[:, i, :], func=AF.Square),
                        reads=[YT_res[i]], writes=[GSQ_res[x]])
                    for ti, (a, b) in enumerate(TT):
                        K.mm(banks[bks[ti]][:, 0:b - a], ONB, GSQ[x][:, a:b], start=(i == 0), stop=(i == 2),
                             reads=[GSQ_res[x], const_res], out_res=bank_res[bks[ti]], sig=(ti == len(TT) - 1))
                for ti, (a, b) in enumerate(TT):
                    K.A(lambda ti=ti, a=a, b=b: nc.scalar.activation(out=RS[:, a:b], in_=banks[bks[ti]][:, 0:b - a],
                                                                    func=AF.Ln, bias=EPSC[:, 0:1], scale=1.0 / 384),
                        reads=[bank_res[bks[ti]], const_res], writes=[RS_res])
                K.A(lambda: nc.scalar.activation(out=RS, in_=RS, func=AF.Exp, scale=-0.5), reads=[RS_res],
                    writes=[RS_res])
                for b_ in bks:
                    bfree(b_)
                for i in range(3):
                    K.V(lambda i=i: nc.vector.scalar_tensor_tensor(
                        out=INT[:, 8 + 3 * g + i, :], in0=YT[:, i, :],
                        scalar=CV[:, CV_NG + 3 * g + i:CV_NG + 3 * g + i + 1], in1=RS, op0=ALU.mult, op1=ALU.mult),
                        reads=[YT_res[i], RS_res, const_res], writes=[INT_res[8 + 3 * g + i]])
            K.barrier()
            for d in range(KC):
                def oevac(ti, a, b, bk, d=d):
                    K.A(lambda: nc.scalar.copy(out=H[:, d, a:b], in_=banks[bk][:, 0:b - a]), reads=[bank_res[bk]],
                        writes=[H_res[d]])
                linear_fm("out", lambda k, a, b: INT[:, k, a:b], lambda k: INT_res[k], 32, oevac)
            K.barrier()
            post_residual(lambda k: CV[:, CV_G + 3 * 16 + k:CV_G + 3 * 16 + k + 1])

        def ple(p):
            prenorm(6)
            spill_H()
            K.barrier()
            PT = view(OFF_X, [2, T], BF)
            pt_res = Res("PT")
            K.dma(POOL, pt_st, PT, pT[p].rearrange("k q t -> q k t"), writes=[pt_res])
            si = [0]
            for d in range(KC):
                pi, pres, (pg, pp) = ws.get("ple")
                for ti, (a, b) in enumerate(TT):
                    n = b - a
                    bg, bp = balloc(), balloc()
                    for k in range(KC):
                        K.mm(banks[bg][:, 0:n], pg[:, k, :], U[:, k, a:b], start=(k == 0), stop=(k == KC - 1),
                             reads=[pres, U_res], out_res=bank_res[bg], sig=False)
                    for k in range(2):
                        K.mm(banks[bp][:, 0:n], pp[:, k, :], PT[:, k, a:b], start=(k == 0), stop=(k == 1),
                             reads=[pres, pt_res], out_res=bank_res[bp], sig=(k == 1))
                    s = si[0] % 2
                    si[0] += 1
                    K.A(lambda bg=bg, n=n, s=s: nc.scalar.activation(out=SS[s][:, 0:n], in_=banks[bg][:, 0:n],
                                                                     func=AF.Sigmoid),
                        reads=[bank_res[bg]], writes=[SS_res[s]])
                    K.V(lambda bp=bp, n=n, s=s, d=d, a=a, b=b: nc.vector.tensor_tensor(
                        out=H[:, d, a:b], in0=SS[s][:, 0:n], in1=banks[bp][:, 0:n], op=ALU.mult),
                        reads=[SS_res[s], bank_res[bp]], writes=[H_res[d]])
                    bfree(bg)
                    bfree(bp)
                ws.done(pi)
            K.barrier()
            post_residual(lambda k: CV[:, CV_G + 7 * 16 + k:CV_G + 7 * 16 + k + 1])

        for p in range(NP):
            for k in range(KC):
                K.dma(SP, ld_st, H[:, k, :], xT[p, k], writes=[H_res[k]])
            ffn("w_ffn1", 0, 0)
            if stage >= 2:
                mixer(p)
            if stage >= 3:
                ffn("w_ffn2", 4, 1)
            if stage >= 4:
                ple(p)
            K.barrier()
            for k in range(KC):
                K.dma(SP, out_st, yT[p, k], H[:, k, :], reads=[H_res[k]])
            K.barrier()
        SP.raw.wait_ge(out_st.sem, 16 * out_st.n)
    return nc


def _cvec(inp):
    cv = np.zeros((128, CV_N), np.float32)
    names = ["norm_ffn1_pre", "norm_ffn1_post", "norm_mix_pre", "norm_mix_post",
             "norm_ffn2_pre", "norm_ffn2_post", "norm_ple_pre", "norm_ple_post"]
    for i, n in enumerate(names):
        cv[:, CV_G + i * 16:CV_G + (i + 1) * 16] = inp[n].reshape(16, 128).T
    cw = inp["conv_mod_w"].reshape(31, 8, 128)
    cv[:, CV_CW:CV_CW + 248] = cw.transpose(2, 1, 0).reshape(128, 248)
    cv[:, CV_CB:CV_CB + 8] = inp["conv_mod_b"].reshape(8, 128).T
    cv[:, CV_LG:CV_LG + 8] = inp["conv_mod_ln_g"].reshape(8, 128).T
    cv[:, CV_LB:CV_LB + 8] = inp["conv_mod_ln_b"].reshape(8, 128).T
    sw = inp["ssm_conv_w"].reshape(4, 40, 128)
    cv[:, CV_SW:CV_SW + 160] = sw.transpose(2, 1, 0).reshape(128, 160)
    cv[:, CV_SB:CV_SB + 40] = inp["ssm_conv_b"].reshape(40, 128).T
    cv[:, CV_NG:CV_NG + 24] = inp["ssm_norm_g"].reshape(24, 128).T
    cv[:, CV_BC:CV_BC + 48] = inp["dt_bias"].reshape(1, 48)
    cv[:, CV_BC + 48:CV_BC + 96] = inp["a_log"].reshape(1, 48)
    cv[:, CV_BC + 96:CV_BC + 144] = inp["d_skip"].reshape(1, 48)
    cv[:, CV_DF:CV_DF + 24] = np.repeat(inp["d_skip"].reshape(48), 64).reshape(24, 128).T
    return cv


def run(inputs, stage=99, NP=1, TS=16):
    inp = {k: np.ascontiguousarray(np.asarray(v, dtype=np.float32)) for k, v in inputs.items()}
    T = TP + TS
    xp = inp["x_prompt"]
    xs = inp["x_sample"].reshape(128, D)
    pp = inp["p_prompt"].reshape(4, 2048, 256)
    psm = inp["p_sample"].reshape(128, 256)
    shared = {"cvec": _cvec(inp), "ident": np.eye(128, dtype=np.float32),
              "tri": np.triu(np.ones((128, 128), np.float32))}
    for nm in ["w_ffn1_gate", "w_ffn1_up", "w_ffn1_down", "w_in", "w_out", "w_ffn2_gate", "w_ffn2_up",
               "w_ffn2_down", "w_ple_gate", "w_ple_proj"]:
        shared[nm] = inp[nm][0]
    in_maps = []
    for c in range(8):
        s = c // 2
        hf = c % 2
        m = dict(shared)
        m["flag"] = np.full((128, 1), float(hf), np.float32)
        m["negm"] = ((np.triu(np.ones((128, 128), np.float32)) - 1.0) * 30000.0).astype(np.float32)
        xTc = np.zeros((NP, KC, 128, T), np.float32)
        pTc = np.zeros((NP, 2, 128, T), np.float32)
        stc = np.zeros((NP, TS, 30, DCV), np.float32)
        stx = np.zeros((NP, TS * 3, DXBC), np.float32)
        sts = np.zeros((NP, TS, DSSM, DST), np.float32)
        for p in range(NP):
            b0 = (c * NP + p) * TS
            xt = np.concatenate([xp[s, hf * TP:(hf + 1) * TP], xs[b0:b0 + TS]], 0)
            xTc[p] = xt.T.reshape(KC, 128, T)
            pt = np.concatenate([pp[s, hf * TP:(hf + 1) * TP], psm[b0:b0 + TS]], 0)
            pTc[p] = pt.T.reshape(2, 128, T)
            stc[p] = inp["state_conv_mod"][0, b0:b0 + TS]
            stx[p] = inp["state_ssm_conv"][0, b0:b0 + TS].reshape(TS * 3, DXBC)
            sts[p] = inp["state_ssm"][0, b0:b0 + TS].reshape(TS, DSSM, DST)
        m.update({"xT": xTc, "pT": pTc, "st_conv": stc, "st_xbc": stx, "st_ssm": sts})
        in_maps.append(m)
    nc = build_program(NP, TS, stage)
    res = run_bass_kernel_spmd(nc, in_maps, core_ids=list(range(8)))
    R = res.results
    y_prompt = np.zeros((4, 2048, D), np.float32)
    y_sample = np.zeros((128, 1, D), np.float32)
    cmp_ = np.zeros((1, 4, 30, DCV), np.float32)
    xbp = np.zeros((1, 4, 3, DXBC), np.float32)
    ssp = np.zeros((1, 4, NH, 64, DST), np.float32)
    cms = np.zeros((1, 128, 30, DCV), np.float32)
    xbs = np.zeros((1, 128, 3, DXBC), np.float32)
    sss = np.zeros((1, 128, NH, 64, DST), np.float32)
    for c in range(8):
        r = R[c]
        s = c // 2
        for p in range(NP):
            b0 = (c * NP + p) * TS
            yt = r["yT"][p].reshape(D, T).T
            y_prompt[s, (c % 2) * TP:(c % 2 + 1) * TP] = yt[:TP]
            y_sample[b0:b0 + TS, 0] = yt[TP:]
            cms[0, b0:b0 + TS] = r["o_cms"][p]
            xbs[0, b0:b0 + TS] = r["o_xbs"][p]
            sss[0, b0:b0 + TS] = r["o_sss"][p].reshape(TS, NH, 64, DST)
        if c % 2 == 1:
            cmp_[0, s] = r["o_cmp"][NP - 1]
            xbp[0, s] = r["o_xbp"][NP - 1]
            ssp[0, s] = r["o_ssp"][NP - 1].reshape(NH, 64, DST)
    return (y_prompt, y_sample, cmp_, xbp, ssp, cms, xbs, sss)


def kernel(**inputs):
    return run(inputs)
```

```python
import numpy as np
import concourse.bass as bass
import concourse.mybir as mybir
from concourse.bass_utils import run_bass_kernel_spmd
from contextlib import ExitStack

F32 = mybir.dt.float32
BF = mybir.dt.bfloat16
ALU = mybir.AluOpType
AF = mybir.ActivationFunctionType
AX = mybir.AxisListType

D = 2048
KC = 16
DFF = 5632
NHALF = 22
DCV = 1024
DSSM = 3072
DXBC = 5120
NH = 48
NG = 8
HPG = 6
DST = 128
EPS = 1e-6
S_Z = 2048
S_X = 5120
S_B = 8192
S_C = 9216
S_DT = 10240
TP = 1024
NCH = 8
NSLOT = 3

CV_G = 0
CV_CW = 128
CV_CB = CV_CW + 248
CV_LG = CV_CB + 8
CV_LB = CV_LG + 8
CV_SW = CV_LB + 8
CV_SB = CV_SW + 160
CV_NG = CV_SB + 40
CV_BC = CV_NG + 24
CV_DF = CV_BC + 144
CV_N = CV_DF + 24


class Res:
    __slots__ = ("name", "w", "r", "excl")

    def __init__(self, name, excl=False):
        self.name = name
        self.w = None
        self.r = []
        self.excl = excl


class Eng:
    def __init__(self, raw, sem, name):
        self.raw = raw
        self.sem = sem
        self.name = name
        self.n = 0
        self.waited = {}


class Stream:
    def __init__(self, sem, name):
        self.sem = sem
        self.name = name
        self.n = 0
        self.inc = 16


class Builder:
    def __init__(self, nc, es):
        self.nc = nc
        self.es = es
        self.nsem = 0
        self.pe = Eng(nc.tensor, self.sem("pe"), "pe")
        self.act = Eng(nc.scalar, self.sem("act"), "act")
        self.dve = Eng(nc.vector, self.sem("dve"), "dve")
        self.pool = Eng(nc.gpsimd, self.sem("pool"), "pool")
        self.sp = Eng(nc.sync, self.sem("sp"), "sp")
        self.engs = [self.pe, self.act, self.dve, self.pool, self.sp]
        self.streams = []
        self.pe_pending = []
        self.pe_out_pending = []

    def sem(self, name):
        self.nsem += 1
        return self.es.enter_context(self.nc.semaphore(name))

    def stream(self, name):
        s = Stream(self.sem("d_" + name), name)
        self.streams.append(s)
        return s

    def wait(self, eng, tok):
        if tok is None:
            return
        if tok[0] == "e":
            src, val = tok[1], tok[2]
            if src is eng and eng is self.pe:
                return
            sem = src.sem
            key = src.name
        else:
            st = tok[1]
            sem = st.sem
            val = st.inc * st.n
            key = "d_" + st.name
        if eng.waited.get(key, 0) >= val:
            return
        eng.raw.wait_ge(sem, val)
        eng.waited[key] = val

    def _pre(self, eng, reads, writes):
        for r in reads:
            self.wait(eng, r.w)
            if r.excl:
                for t in r.r:
                    if t[0] != "e" or t[1] is not eng:
                        self.wait(eng, t)
        for w in writes:
            assert w not in self.pe_pending or eng is self.pe, "PE pending read on " + w.name
            self.wait(eng, w.w)
            for t in w.r:
                self.wait(eng, t)

    def _post(self, tok, reads, writes):
        for r in reads:
            r.r.append(tok)
            if len(r.r) > 24:
                r.r = r.r[-24:]
        for w in writes:
            w.w = tok
            w.r = []

    def op(self, eng, fn, reads=(), writes=()):
        self._pre(eng, reads, writes)
        ins = fn()
        ins.then_inc(eng.sem, 1)
        eng.n += 1
        tok = ("e", eng, eng.n)
        self._post(tok, reads, writes)
        return tok

    def A(self, fn, reads=(), writes=()):
        return self.op(self.act, fn, reads, writes)

    def V(self, fn, reads=(), writes=()):
        return self.op(self.dve, fn, reads, writes)

    def dma(self, q, st, out, in_, reads=(), writes=()):
        self._pre(q, reads, writes)
        ins = q.raw.dma_start(out=out, in_=in_)
        ins.then_inc(st.sem, 16)
        st.n += 1
        tok = ("d", st, st.n)
        self._post(tok, reads, writes)
        return tok

    def mm(self, out, lhsT, rhs, start, stop, reads, out_res, sig, transpose=False):
        pe = self.pe
        for r in reads:
            self.wait(pe, r.w)
            if r not in self.pe_pending:
                self.pe_pending.append(r)
        if start:
            self.wait(pe, out_res.w)
            for t in out_res.r:
                self.wait(pe, t)
        if transpose:
            ins = pe.raw.transpose(out=out, in_=lhsT, identity=rhs)
        else:
            ins = pe.raw.matmul(out, lhsT, rhs, start=start, stop=stop)
        if stop and out_res not in self.pe_out_pending:
            self.pe_out_pending.append(out_res)
        if sig:
            ins.then_inc(pe.sem, 1)
            pe.n += 1
            tok = ("e", pe, pe.n)
            for r in self.pe_pending:
                r.r.append(tok)
            self.pe_pending = []
            for o in self.pe_out_pending:
                o.w = tok
                o.r = []
            self.pe_out_pending = []
        return ins

    def barrier(self, engs=None):
        assert not self.pe_pending and not self.pe_out_pending
        engs = engs or [self.pe, self.act, self.dve, self.sp]
        for e in engs:
            for f in [self.pe, self.act, self.dve]:
                if f is not e and f.n > 0:
                    self.wait(e, ("e", f, f.n))
            for st in self.streams:
                if st.n > 0 and not st.name.startswith("slot"):
                    self.wait(e, ("d", st, st.n))


def tt_tiles(T):
    a = (T + 2) // 3
    b = (T - a + 1) // 2
    return [(0, a), (a, a + b), (a + b, T)]


def build_program(NP, TS, stage=99):
    T = TP + TS
    TT = tt_tiles(T)
    nc = bass.Bass("TRN2", target_bir_lowering=False)

    def din(name, shape):
        return nc.dram_tensor(name, list(shape), F32, kind="ExternalInput").ap()

    def dout(name, shape):
        return nc.dram_tensor(name, list(shape), F32, kind="ExternalOutput").ap()

    xT = din("xT", [NP, KC, 128, T])
    pT = din("pT", [NP, 2, 128, T])
    st_conv = din("st_conv", [NP, TS, 30, DCV])
    st_xbc = din("st_xbc", [NP, TS * 3, DXBC])
    st_ssm = din("st_ssm", [NP, TS, DSSM, DST])
    cvec_d = din("cvec", [128, CV_N])
    ident_d = din("ident", [128, 128])
    tri_d = din("tri", [128, 128])
    flag_d = din("flag", [128, 1])
    negm_d = din("negm", [128, 128])
    W = {}
    for nm, shp in [("w_ffn1_gate", (D, DFF)), ("w_ffn1_up", (D, DFF)), ("w_ffn1_down", (DFF, D)),
                    ("w_in", (D, 10288)), ("w_out", (4096, D)),
                    ("w_ffn2_gate", (D, DFF)), ("w_ffn2_up", (D, DFF)), ("w_ffn2_down", (DFF, D)),
                    ("w_ple_gate", (D, D)), ("w_ple_proj", (256, D))]:
        W[nm] = din(nm, shp)
    yT = dout("yT", [NP, KC, 128, T])
    o_cms = dout("o_cms", [NP, TS, 30, DCV])
    o_xbs = dout("o_xbs", [NP, TS, 3, DXBC])
    o_sss = dout("o_sss", [NP, TS, DSSM, DST])
    o_cmp = dout("o_cmp", [NP, 30, DCV])
    o_xbp = dout("o_xbp", [NP, 3, DXBC])
    o_ssp = dout("o_ssp", [NP, DSSM, DST])
    hsp = nc.dram_tensor("hsp", [KC, 128, T], F32, kind="Internal").ap()

    with ExitStack() as es:
        K = Builder(nc, es)
        PE, ACT, DVE, POOL, SP = K.pe, K.act, K.dve, K.pool, K.sp

        ARENA_B = 188000
        arena = es.enter_context(nc.sbuf_tensor("arena", [128, ARENA_B // 4], F32))
        slots = [es.enter_context(nc.sbuf_tensor("slot%d" % i, [128, 4096], BF)) for i in range(NSLOT)]
        slot_res = [Res("slot%d" % i) for i in range(NSLOT)]
        slot_st = [K.stream("slot%d" % i) for i in range(NSLOT)]
        banks = [es.enter_context(nc.psum_tensor("bank%d" % i, [128, 512], F32)) for i in range(8)]
        bank_res = [Res("bank%d" % i, excl=True) for i in range(8)]
        free_banks = list(range(8))

        def balloc():
            assert free_banks, "out of PSUM banks"
            return free_banks.pop(0)

        def bfree(b):
            free_banks.append(b)

        def view(off, shape, dt):
            esz = 4 if dt == F32 else 2
            n = int(np.prod(shape))
            nb = (n * esz + 3) // 4 * 4
            assert off % 4 == 0
            assert off + nb <= ARENA_B, (off, nb)
            ap = arena[:, off // 4: off // 4 + nb // 4]
            if dt != F32:
                ap = ap.bitcast(dt)[:, 0:n]
            if len(shape) == 2:
                return ap.rearrange("p (a b) -> p a b", a=shape[0])
            if len(shape) == 3:
                return ap.rearrange("p (a b c) -> p a b c", a=shape[0], b=shape[1])
            return ap

        OFF_U = 0
        OFF_H = OFF_U + 2 * KC * T
        OFF_X = OFF_H + 4 * KC * T
        OFF_C = OFF_X + 4 * KC * T
        U = view(OFF_U, [KC, T], BF)
        H = view(OFF_H, [KC, T], F32)
        AH = view(OFF_X, [NHALF, T], BF)
        INT = view(OFF_X, [32, T], BF)
        U_res = Res("U")
        H_res = [Res("H%d" % i) for i in range(KC)]
        AH_res = [Res("AH%d" % i) for i in range(NHALF)]
        INT_res = [Res("INT%d" % i) for i in range(32)]
        hsp_res = [Res("hsp%d" % i) for i in range(KC)]

        coff = [OFF_C]

        def calloc(shape, dt):
            esz = 4 if dt == F32 else 2
            n = int(np.prod(shape)) * esz
            n = (n + 3) // 4 * 4
            o = coff[0]
            coff[0] += n
            return view(o, shape, dt)

        CV = calloc([CV_N], F32)
        IDN = calloc([128], F32)
        TRI = calloc([128], F32)
        ONE = calloc([128], F32)
        IDB = calloc([128], BF)
        ONB = calloc([128], BF)
        EPSC = calloc([1], F32)
        GH = calloc([2 * KC], F32)
        ANEG = calloc([NH], F32)
        RS = calloc([T], F32)
        SQ = [calloc([T], BF) for _ in range(2)]
        SQ_res = [Res("SQ0"), Res("SQ1")]
        RS_res = Res("RS")
        const_res = Res("const")
        OFF_XS = OFF_X + 2 * NHALF * T
        SS = [view(OFF_XS + i * 4 * 352, [352], F32) for i in range(2)]
        SS_res = [Res("SS0"), Res("SS1")]
        HST = [view(OFF_XS + 2 * 4 * 352 + i * 4 * T, [T], F32) for i in range(2)]
        HST_res = [Res("HST0"), Res("HST1")]
        assert OFF_XS + 2 * 4 * 352 + 2 * 4 * T <= OFF_C

        ld_st = K.stream("ld")
        sp_st = K.stream("spill")
        hst_st = [K.stream("hst0"), K.stream("hst1")]
        out_st = K.stream("out")

        K.dma(SP, ld_st, CV, cvec_d, writes=[const_res])
        K.dma(SP, ld_st, IDN, ident_d, writes=[const_res])
        K.dma(SP, ld_st, TRI, tri_d, writes=[const_res])
        K.V(lambda: nc.vector.memset(ONE, 1.0), writes=[const_res])
        K.V(lambda: nc.vector.memset(ONB, 1.0), writes=[const_res])
        K.V(lambda: nc.vector.memset(EPSC, EPS), writes=[const_res])
        K.V(lambda: nc.vector.tensor_copy(out=IDB, in_=IDN), reads=[const_res], writes=[const_res])
        K.V(lambda: nc.vector.tensor_scalar(out=GH[:, 0:KC], in0=CV[:, CV_G + 16:CV_G + 32], scalar1=0.5,
                                            scalar2=None, op0=ALU.mult), reads=[const_res], writes=[const_res])
        K.V(lambda: nc.vector.tensor_scalar(out=GH[:, KC:2 * KC], in0=CV[:, CV_G + 80:CV_G + 96], scalar1=0.5,
                                            scalar2=None, op0=ALU.mult), reads=[const_res], writes=[const_res])
        K.A(lambda: nc.scalar.activation(out=ANEG, in_=CV[:, CV_BC + 48:CV_BC + 96], func=AF.Exp),
            reads=[const_res], writes=[const_res])
        K.V(lambda: nc.vector.tensor_scalar(out=ANEG, in0=ANEG, scalar1=-1.0, scalar2=None, op0=ALU.mult),
            reads=[const_res], writes=[const_res])

        class WS:
            def __init__(self):
                self.plan = []
                self.issued = 0
                self.next_i = 0
                self.done_upto = 0

            def pump(self):
                while self.issued < len(self.plan) and self.issued < self.done_upto + NSLOT:
                    i = self.issued
                    s = i % NSLOT
                    off = 0
                    for ent in self.plan[i][1]:
                        w, k0, kc, f0 = ent[:4]
                        fw = ent[4] if len(ent) > 4 else 128
                        dst = slots[s][:, off:off + kc * fw].rearrange("p (k f) -> p k f", k=kc)
                        src = w[k0 * 128:(k0 + kc) * 128, f0:f0 + fw].rearrange("(k p) f -> p k f", p=128)
                        K.dma(POOL, slot_st[s], dst, src, writes=[slot_res[s]])
                        off += kc * fw
                    self.issued += 1

            def get(self, tag):
                i = self.next_i
                assert self.plan[i][0] == tag, (self.plan[i][0], tag)
                self.pump()
                assert i < self.issued
                self.next_i += 1
                s = i % NSLOT
                views = []
                off = 0
                for ent in self.plan[i][1]:
                    w, k0, kc, f0 = ent[:4]
                    fw = ent[4] if len(ent) > 4 else 128
                    views.append(slots[s][:, off:off + kc * fw].rearrange("p (k f) -> p k f", k=kc))
                    off += kc * fw
                return i, slot_res[s], views

            def done(self, i):
                assert i == self.done_upto
                self.done_upto += 1
                self.pump()

        ws = WS()

        def plan_ffn(pre):
            wg, wu, wd = W[pre + "_gate"], W[pre + "_up"], W[pre + "_down"]
            for hf in range(2):
                for f in range(NHALF):
                    fc = hf * NHALF + f
                    ws.plan.append((pre + "gu", [(wg, 0, KC, fc * 128), (wu, 0, KC, fc * 128)]))
                for d in range(KC):
                    ws.plan.append((pre + "dn", [(wd, hf * NHALF, NHALF, d * 128)]))

        def plan_mixer():
            wi = W["w_in"]
            for cc in range(8):
                ws.plan.append(("glu", [(wi, 0, KC, cc * 128), (wi, 0, KC, 1024 + cc * 128)]))
            ws.plan.append(("dt", [(wi, 0, KC, S_DT, NH)]))
            for g in range(NG):
                for i in range(3):
                    ws.plan.append(("xs", [(wi, 0, KC, S_X + g * 384 + i * 128)]))
                ws.plan.append(("bc", [(wi, 0, KC, S_B + g * 128), (wi, 0, KC, S_C + g * 128)]))
                for i in range(3):
                    ws.plan.append(("z", [(wi, 0, KC, S_Z + g * 384 + i * 128)]))
            for d in range(KC):
                ws.plan.append(("out", [(W["w_out"], 0, 32, d * 128)]))

        def plan_ple():
            for d in range(KC):
                ws.plan.append(("ple", [(W["w_ple_gate"], 0, KC, d * 128), (W["w_ple_proj"], 0, 2, d * 128)]))

        for p in range(NP):
            plan_ffn("w_ffn1")
            if stage >= 2:
                plan_mixer()
            if stage >= 3:
                plan_ffn("w_ffn2")
            if stage >= 4:
                plan_ple()

        def stats_to_RS(src_fn, n_k, src_res_fn, scale):
            bks = [balloc() for _ in TT]
            for k in range(n_k):
                sq = SQ[k % 2]
                sr = SQ_res[k % 2]
                K.A(lambda k=k, sq=sq: nc.scalar.activation(out=sq, in_=src_fn(k), func=AF.Square),
                    reads=[src_res_fn(k)], writes=[sr])
                for ti, (a, b) in enumerate(TT):
                    K.mm(banks[bks[ti]][:, 0:b - a], ONB, sq[:, a:b], start=(k == 0), stop=(k == n_k - 1),
                         reads=[sr, const_res], out_res=bank_res[bks[ti]], sig=(ti == len(TT) - 1))
            for ti, (a, b) in enumerate(TT):
                K.A(lambda ti=ti, a=a, b=b: nc.scalar.activation(out=RS[:, a:b], in_=banks[bks[ti]][:, 0:b - a],
                                                                func=AF.Ln, bias=EPSC[:, 0:1], scale=scale),
                    reads=[bank_res[bks[ti]], const_res], writes=[RS_res])
            K.A(lambda: nc.scalar.activation(out=RS, in_=RS, func=AF.Exp, scale=-0.5), reads=[RS_res], writes=[RS_res])
            for b in bks:
                bfree(b)

        def prenorm(gi):
            stats_to_RS(lambda k: H[:, k, :], KC, lambda k: H_res[k], 1.0 / D)
            for k in range(KC):
                K.V(lambda k=k: nc.vector.scalar_tensor_tensor(out=U[:, k, :], in0=H[:, k, :],
                                                               scalar=CV[:, CV_G + gi * 16 + k:CV_G + gi * 16 + k + 1],
                                                               in1=RS, op0=ALU.mult, op1=ALU.mult),
                    reads=[H_res[k], RS_res, const_res], writes=[U_res])

        def spill_H():
            for k in range(KC):
                K.dma(SP, sp_st, hsp[k], H[:, k, :], reads=[H_res[k]], writes=[hsp_res[k]])

        def post_residual(gcol_fn, after_k=None):
            stats_to_RS(lambda k: H[:, k, :], KC, lambda k: H_res[k], 1.0 / D)
            for k in range(KC):
                K.dma(SP, hst_st[k % 2], HST[k % 2], hsp[k], reads=[hsp_res[k]], writes=[HST_res[k % 2]])
                K.V(lambda k=k: nc.vector.scalar_tensor_tensor(out=H[:, k, :], in0=H[:, k, :], scalar=gcol_fn(k),
                                                               in1=RS, op0=ALU.mult, op1=ALU.mult),
                    reads=[RS_res, const_res], writes=[H_res[k]])
                K.V(lambda k=k: nc.vector.tensor_tensor(out=H[:, k, :], in0=H[:, k, :], in1=HST[k % 2], op=ALU.add),
                    reads=[HST_res[k % 2]], writes=[H_res[k]])
                if after_k is not None:
                    after_k(k)

        def ffn(pre, g_pre, gh_idx):
            prenorm(g_pre)
            spill_H()
            ssi = [0]
            for hf in range(2):
                for f in range(NHALF):
                    pi, pres, (pg, pu) = ws.get(pre + "gu")
                    for ti, (a, b) in enumerate(TT):
                        n = b - a
                        bg, bu = balloc(), balloc()
                        for k in range(KC):
                            K.mm(banks[bg][:, 0:n], pg[:, k, :], U[:, k, a:b], start=(k == 0), stop=(k == KC - 1),
                                 reads=[pres, U_res], out_res=bank_res[bg], sig=False)
                        for k in range(KC):
                            K.mm(banks[bu][:, 0:n], pu[:, k, :], U[:, k, a:b], start=(k == 0), stop=(k == KC - 1),
                                 reads=[pres, U_res], out_res=bank_res[bu], sig=(k == KC - 1))
                        s = ssi[0] % 2
                        ssi[0] += 1
                        K.A(lambda bg=bg, n=n, s=s: nc.scalar.activation(out=SS[s][:, 0:n], in_=banks[bg][:, 0:n],
                                                                         func=AF.Silu),
                            reads=[bank_res[bg]], writes=[SS_res[s]])
                        K.V(lambda bu=bu, n=n, s=s, f=f, a=a, b=b: nc.vector.tensor_tensor(
                            out=AH[:, f, a:b], in0=SS[s][:, 0:n], in1=banks[bu][:, 0:n], op=ALU.mult),
                            reads=[SS_res[s], bank_res[bu]], writes=[AH_res[f]])
                        bfree(bg)
                        bfree(bu)
                    ws.done(pi)
                for d in range(KC):
                    pi, pres, (pd,) = ws.get(pre + "dn")
                    for ti, (a, b) in enumerate(TT):
                        n = b - a
                        bk = balloc()
                        for k in range(NHALF):
                            K.mm(banks[bk][:, 0:n], pd[:, k, :], AH[:, k, a:b], start=(k == 0), stop=(k == NHALF - 1),
                                 reads=[pres, AH_res[k]], out_res=bank_res[bk], sig=(k == NHALF - 1))
                        if hf == 0:
                            K.A(lambda bk=bk, n=n, d=d, a=a, b=b: nc.scalar.copy(out=H[:, d, a:b], in_=banks[bk][:, 0:n]),
                                reads=[bank_res[bk]], writes=[H_res[d]])
                        else:
                            K.V(lambda bk=bk, n=n, d=d, a=a, b=b: nc.vector.tensor_tensor(
                                out=H[:, d, a:b], in0=H[:, d, a:b], in1=banks[bk][:, 0:n], op=ALU.add),
                                reads=[bank_res[bk]], writes=[H_res[d]])
                        bfree(bk)
                    ws.done(pi)
            post_residual(lambda k: GH[:, gh_idx * KC + k:gh_idx * KC + k + 1])

        XDT = [calloc([6, 64], BF) for _ in range(2)]
        XDD = [calloc([6, 64], BF) for _ in range(2)]
        E32 = [calloc([3, 128], BF) for _ in range(2)]
        CBM = [calloc([128], F32) for _ in range(2)]
        XDT_res = [Res("XDT0"), Res("XDT1")]
        XDD_res = [Res("XDD0"), Res("XDD1")]
        E32_res = [Res("E320"), Res("E321")]
        CBM_res = [Res("CBM0"), Res("CBM1")]
        DG = [calloc([128], BF) for _ in range(4)]
        DG_res = [Res("DG%d" % i) for i in range(4)]
        FLAG = calloc([1], F32)
        K.dma(SP, ld_st, FLAG, flag_d, writes=[const_res])
        NEGM = calloc([128], F32)
        K.dma(SP, ld_st, NEGM, negm_d, writes=[const_res])
        assert coff[0] <= ARENA_B, coff[0]
        misc_st = K.stream("misc")
        slab_st = [K.stream("slab0"), K.stream("slab1"), K.stream("slab2")]
        sout_st = K.stream("sout")
        yout_st = K.stream("yout")
        pt_st = K.stream("ptld")
        xs_st = K.stream("xsrc")
        xd_st = K.stream("xdst")
        cc_st = K.stream("cc")
        cc_st.inc = 1
        PAIRS = [[0, 1], [2, 3], [4, 5], [6, 7]]
        xch_i = [0]

        def exchange(src_ap, src_res, width, dst_ap, dst_res):
            i = xch_i[0]
            xch_i[0] += 1
            xsrc = nc.dram_tensor("xsrc%d" % i, [128, width], F32, kind="Internal").ap()
            xdst = nc.dram_tensor("xdst%d" % i, [256, width], F32, kind="Internal").ap()
            r1, r2 = Res("xsrc%d" % i), Res("xdst%d" % i)
            xs_v, xd_v = xsrc, xdst[0:128, :]
            if len(src_ap.shape) == 3:
                xs_v = xsrc.rearrange("p (a b) -> p a b", a=src_ap.shape[1])
                xd_v = xdst[0:128, :].rearrange("p (a b) -> p a b", a=src_ap.shape[1])
            K.dma(SP, xs_st, xs_v, src_ap, reads=[src_res], writes=[r1])
            K._pre(POOL, [r1], [r2])
            ins = nc.gpsimd.collective_compute("AllGather", ALU.bypass, replica_groups=PAIRS, ins=[xsrc],
                                               outs=[xdst])
            ins.then_inc(cc_st.sem, 1)
            cc_st.n += 1
            tok = ("d", cc_st, cc_st.n)
            K._post(tok, [r1], [r2])
            K.dma(SP, xd_st, dst_ap, xd_v, reads=[r2], writes=[dst_res])

        def linear_fm(tag, in_fn, in_res_fn, nk, evac):
            pi, pres, (pw,) = ws.get(tag)
            for ti, (a, b) in enumerate(TT):
                bk = balloc()
                for k in range(nk):
                    K.mm(banks[bk][:, 0:b - a], pw[:, k, :], in_fn(k, a, b), start=(k == 0), stop=(k == nk - 1),
                         reads=[pres, in_res_fn(k)], out_res=bank_res[bk], sig=(k == nk - 1))
                evac(ti, a, b, bk)
                bfree(bk)
            ws.done(pi)

        def mixer(p):
            prenorm(2)
            spill_H()
            K.barrier()
            hoff = [OFF_H]

            def halloc(shape, dt):
                esz = 4 if dt == F32 else 2
                n = (int(np.prod(shape)) * esz + 3) // 4 * 4
                o = hoff[0]
                hoff[0] += n
                assert hoff[0] <= OFF_X, ("H scratch overflow", hoff[0] - OFF_X)
                return view(o, shape, dt)

            LA = TT[-1][0]
            assert LA <= TP - 30
            DT = halloc([9, NH], F32)
            DTA = halloc([9, NH], F32)
            NACS = halloc([8, NH], F32)
            EXA = halloc([8, NH], F32)
            CD = halloc([8, NH], F32)
            W2 = halloc([8, NH], F32)
            ETOT = halloc([8, NH], F32)
            CDTOT = halloc([NH], F32)
            DECS = halloc([NH], F32)
            dt_res = Res("dt")
            mix_base = hoff[0]

            Vb = halloc([8, 30 + T], BF)
            MSS = [halloc([352], F32) for _ in range(2)]
            VT32 = halloc([8, 30 + TS], F32)
            HV32 = halloc([8, 30], F32)
            ACC = [halloc([T], F32) for _ in range(2)]
            CBF = [halloc([T], BF)] * 2
            CSQ = [halloc([T], BF)] * 2
            MEAN = halloc([T], F32)
            RSTD = halloc([T], F32)
            DGC = view(hoff[0] - 8 * T, [31, 128], BF)
            assert 31 * 128 * 2 <= 8 * T
            STC = [halloc([TS, 30], F32) for _ in range(2)]
            TMPS = halloc([TS, 30], F32)
            NB4 = TS // 4
            STG = [view(OFF_X + 8 * 2 * T + 4 * 8 * T + 4096 + i * 4 * NB4 * 128, [NB4, 128], F32) for i in range(2)]
            assert OFF_X + 8 * 2 * T + 4 * 8 * T + 4096 + 2 * 4 * NB4 * 128 <= OFF_C
            CO = view(OFF_X + 8 * 2 * T, [8, T], F32)
            TROW = view(OFF_X + 8 * 2 * T + 4 * 8 * T, [1024], F32)
            assert OFF_X + 8 * 2 * T + 4 * 8 * T + 4096 <= OFF_C
            Vb_res = [Res("Vb%d" % i) for i in range(8)]
            MSS_res = [Res("MSS0"), Res("MSS1")]
            vt_res = Res("vt32")
            hv_res = Res("hv32")
            ACC_res = [Res("ACC0"), Res("ACC1")]
            CBF_res = [Res("CBF0")] * 2
            CSQ_res = [Res("CSQ0")] * 2
            STG_res = [Res("STG0"), Res("STG1")]
            STC_res = [Res("stc0"), Res("stc1")]
            CO_res = [Res("CO%d" % i) for i in range(8)]
            ln_res = Res("ln")
            trow_res = Res("trow")
            tmps_res = Res("tmps")

            def cwc(cc, j):
                return CV[:, CV_CW + cc * 31 + j:CV_CW + cc * 31 + j + 1]

            mi = [0]
            for cc in range(8):
                pi, pres, (pa, pb) = ws.get("glu")
                for ti, (a, b) in enumerate(TT):
                    n = b - a
                    ba, bb = balloc(), balloc()
                    for k in range(KC):
                        K.mm(banks[ba][:, 0:n], pa[:, k, :], U[:, k, a:b], start=(k == 0), stop=(k == KC - 1),
                             reads=[pres, U_res], out_res=bank_res[ba], sig=False)
                    for k in range(KC):
                        K.mm(banks[bb][:, 0:n], pb[:, k, :], U[:, k, a:b], start=(k == 0), stop=(k == KC - 1),
                             reads=[pres, U_res], out_res=bank_res[bb], sig=(k == KC - 1))
                    s = mi[0] % 2
                    mi[0] += 1
                    K.A(lambda bb=bb, n=n, s=s: nc.scalar.activation(out=MSS[s][:, 0:n], in_=banks[bb][:, 0:n],
                                                                     func=AF.Sigmoid),
                        reads=[bank_res[bb]], writes=[MSS_res[s]])
                    K.V(lambda ba=ba, n=n, s=s, cc=cc, a=a, b=b: nc.vector.tensor_tensor(
                        out=Vb[:, cc, 30 + a:30 + b], in0=banks[ba][:, 0:n], in1=MSS[s][:, 0:n], op=ALU.mult),
                        reads=[bank_res[ba], MSS_res[s]], writes=[Vb_res[cc]])
                    if ti == len(TT) - 1:
                        K.V(lambda ba=ba, s=s, cc=cc, a=a: nc.vector.tensor_tensor(
                            out=VT32[:, cc, :], in0=banks[ba][:, TP - 30 - a:T - a], in1=MSS[s][:, TP - 30 - a:T - a],
                            op=ALU.mult), reads=[bank_res[ba], MSS_res[s]], writes=[vt_res])
                    bfree(ba)
                    bfree(bb)
                ws.done(pi)
            exchange(VT32[:, :, 0:30], vt_res, 240, HV32, hv_res)

            K.V(lambda: nc.vector.memset(DT, 0.0), writes=[dt_res])
            K.V(lambda: nc.vector.memset(DTA, 0.0), writes=[dt_res])
            pi, pres, (pdt,) = ws.get("dt")
            for c in range(9):
                M = 128 if c < 8 else TS
                bk = balloc()
                for k in range(KC):
                    K.mm(banks[bk][0:M, 0:NH], U[:, k, c * 128:c * 128 + M], pdt[:, k, :], start=(k == 0),
                         stop=(k == KC - 1), reads=[pres, U_res], out_res=bank_res[bk], sig=(k == KC - 1))
                K.V(lambda c=c, M=M, bk=bk: nc.vector.tensor_tensor(out=DT[0:M, c, :], in0=banks[bk][0:M, 0:NH],
                                                                     in1=CV[0:M, CV_BC:CV_BC + NH], op=ALU.add),
                    reads=[bank_res[bk], const_res], writes=[dt_res])
                bfree(bk)
            ws.done(pi)
            K.A(lambda: nc.scalar.activation(out=DT, in_=DT, func=AF.Exp), reads=[dt_res], writes=[dt_res])
            K.A(lambda: nc.scalar.activation(out=DT, in_=DT, func=AF.Ln, bias=1.0, scale=1.0), reads=[dt_res],
                writes=[dt_res])
            K.V(lambda: nc.vector.tensor_tensor(out=DTA, in0=DT, in1=ANEG.unsqueeze(1).to_broadcast([128, 9, NH]),
                                                op=ALU.mult), reads=[dt_res, const_res], writes=[dt_res])
            b1, b2 = balloc(), balloc()
            for c in range(8):
                K.mm(banks[b1][:, c * NH:(c + 1) * NH], TRI, DTA[:, c, :], start=True, stop=True,
                     reads=[dt_res, const_res], out_res=bank_res[b1], sig=False)
            for c in range(8):
                K.mm(banks[b2][:, c * NH:(c + 1) * NH], ONE, DTA[:, c, :], start=True, stop=True,
                     reads=[dt_res, const_res], out_res=bank_res[b2], sig=(c == 7))
            K.A(lambda: nc.scalar.activation(out=EXA, in_=banks[b1][:, 0:8 * NH], func=AF.Exp),
                reads=[bank_res[b1]], writes=[dt_res])
            K.V(lambda: nc.vector.tensor_scalar(out=NACS, in0=banks[b1][:, 0:8 * NH], scalar1=-1.0, scalar2=None,
                                                op0=ALU.mult), reads=[bank_res[b1]], writes=[dt_res])
            K.A(lambda: nc.scalar.activation(out=CD, in_=banks[b2][:, 0:8 * NH], func=AF.Exp),
                reads=[bank_res[b2]], writes=[dt_res])
            K.V(lambda: nc.vector.tensor_tensor(out=W2, in0=banks[b2][:, 0:8 * NH], in1=NACS, op=ALU.add),
                reads=[bank_res[b2], dt_res], writes=[dt_res])
            K.A(lambda: nc.scalar.activation(out=W2, in_=W2, func=AF.Exp), reads=[dt_res], writes=[dt_res])
            K.V(lambda: nc.vector.tensor_tensor(out=W2, in0=W2, in1=DT[:, 0:8, :], op=ALU.mult), reads=[dt_res],
                writes=[dt_res])
            K.A(lambda: nc.scalar.activation(out=DECS[0:TS, :], in_=DTA[0:TS, 8, :], func=AF.Exp), reads=[dt_res],
                writes=[dt_res])
            bfree(b1)
            bfree(b2)
            K.V(lambda: nc.vector.tensor_copy(out=ETOT[:, 0, :], in_=EXA[:, 0, :]), reads=[dt_res], writes=[dt_res])
            K.V(lambda: nc.vector.tensor_copy(out=CDTOT, in_=CD[:, 0, :]), reads=[dt_res], writes=[dt_res])
            for c in range(1, 8):
                K.V(lambda c=c: nc.vector.tensor_tensor(out=ETOT[:, c, :], in0=EXA[:, c, :], in1=CDTOT, op=ALU.mult),
                    reads=[dt_res], writes=[dt_res])
                K.V(lambda c=c: nc.vector.tensor_tensor(out=CDTOT, in0=CDTOT, in1=CD[:, c, :], op=ALU.mult),
                    reads=[dt_res], writes=[dt_res])

            K.dma(SP, out_st, o_xbs[p][:, 0:2, :], st_xbc[p].rearrange("(b j) c -> b j c", j=3)[:, 1:3, :])
            K.dma(SP, out_st, o_cms[p][:, 0:29, :], st_conv[p][:, 1:30, :])

            K.V(lambda: nc.vector.tensor_scalar(out=Vb[:, :, 0:30], in0=HV32, scalar1=FLAG[:, 0:1], scalar2=None,
                                                op0=ALU.mult), reads=[hv_res, const_res], writes=Vb_res)
            for cc in range(8):
                stc, stcr = STC[cc % 2], STC_res[cc % 2]
                sg, sr = STG[cc % 2], STG_res[cc % 2]
                for q4 in range(NB4):
                    K.dma(SP, misc_st, sg[0:120, q4, :],
                          st_conv[p][q4 * 4:(q4 + 1) * 4, :, cc * 128:(cc + 1) * 128].rearrange("b j c -> (b j) c"),
                          writes=[sr])
                bk = balloc()
                for q4 in range(NB4):
                    K.mm(banks[bk][:, q4 * 120:(q4 + 1) * 120], sg[0:120, q4, :], IDN[0:120, 0:120], start=True,
                         stop=True, reads=[sr, const_res], out_res=bank_res[bk], sig=(q4 == NB4 - 1), transpose=True)
                K.A(lambda bk=bk, stc=stc: nc.scalar.copy(out=stc.rearrange("p b j -> p (b j)"),
                                                          in_=banks[bk][:, 0:TS * 30]),
                    reads=[bank_res[bk]], writes=[stcr])
                bfree(bk)
                acc = ACC[cc % 2]
                ar = ACC_res[cc % 2]
                K.V(lambda cc=cc: nc.vector.tensor_tensor(
                    out=DGC, in0=IDB.unsqueeze(1).to_broadcast([128, 31, 128]),
                    in1=CV[:, CV_CW + cc * 31:CV_CW + cc * 31 + 31].unsqueeze(2).to_broadcast([128, 31, 128]),
                    op=ALU.mult), reads=[const_res], writes=[ln_res])
                for ti, (a, b) in enumerate(TT):
                    n = b - a
                    bk = balloc()
                    for j in range(31):
                        K.mm(banks[bk][:, 0:n], DGC[:, j, :], Vb[:, cc, j + a:j + b], start=(j == 0), stop=(j == 30),
                             reads=[ln_res, Vb_res[cc]], out_res=bank_res[bk], sig=(j == 30))
                    pe_ = min(b, TP)
                    K.A(lambda cc=cc, bk=bk, a=a, pe_=pe_: nc.scalar.activation(
                        out=CO[:, cc, a:pe_], in_=banks[bk][:, 0:pe_ - a], func=AF.Identity,
                        bias=CV[:, CV_CB + cc:CV_CB + cc + 1], scale=1.0),
                        reads=[bank_res[bk], const_res, STG_res[0], STG_res[1]], writes=[CO_res[cc]])
                    bfree(bk)
                K.V(lambda cc=cc, stc=stc: nc.vector.tensor_tensor(
                    out=TMPS, in0=stc,
                    in1=CV[:, CV_CW + cc * 31:CV_CW + cc * 31 + 30].unsqueeze(1).to_broadcast([128, TS, 30]),
                    op=ALU.mult), reads=[stcr, const_res], writes=[tmps_res])
                K.V(lambda acc=acc: nc.vector.tensor_reduce(out=acc[:, TP:T], in_=TMPS, axis=AX.X, op=ALU.add),
                    reads=[tmps_res], writes=[ar])
                K.V(lambda cc=cc, acc=acc: nc.vector.scalar_tensor_tensor(
                    out=acc[:, TP:T], in0=Vb[:, cc, 30 + TP:30 + T], scalar=cwc(cc, 30), in1=acc[:, TP:T],
                    op0=ALU.mult, op1=ALU.add), reads=[Vb_res[cc], const_res], writes=[ar])
                K.V(lambda cc=cc, acc=acc: nc.vector.tensor_scalar(
                    out=acc[:, TP:T], in0=acc[:, TP:T], scalar1=CV[:, CV_CB + cc:CV_CB + cc + 1], scalar2=None,
                    op0=ALU.add), reads=[const_res], writes=[ar])
                K.A(lambda cc=cc, acc=acc: nc.scalar.copy(out=CO[:, cc, TP:T], in_=acc[:, TP:T]),
                    reads=[ar, STG_res[0], STG_res[1]], writes=[CO_res[cc]])
            bm = [balloc() for _ in TT]
            bq = [balloc() for _ in TT]
            for cc in range(8):
                x = cc % 2
                K.A(lambda cc=cc, x=x: nc.scalar.copy(out=CBF[x], in_=CO[:, cc, :]), reads=[CO_res[cc]],
                    writes=[CBF_res[x]])
                K.A(lambda cc=cc, x=x: nc.scalar.activation(out=CSQ[x], in_=CO[:, cc, :], func=AF.Square),
                    reads=[CO_res[cc]], writes=[CSQ_res[x]])
                for ti, (a, b) in enumerate(TT):
                    K.mm(banks[bm[ti]][:, 0:b - a], ONB, CBF[x][:, a:b], start=(cc == 0), stop=(cc == 7),
                         reads=[CBF_res[x], const_res], out_res=bank_res[bm[ti]], sig=False)
                for ti, (a, b) in enumerate(TT):
                    K.mm(banks[bq[ti]][:, 0:b - a], ONB, CSQ[x][:, a:b], start=(cc == 0), stop=(cc == 7),
                         reads=[CSQ_res[x], const_res], out_res=bank_res[bq[ti]], sig=(ti == len(TT) - 1))
            for ti, (a, b) in enumerate(TT):
                K.A(lambda ti=ti, a=a, b=b: nc.scalar.mul(out=MEAN[:, a:b], in_=banks[bm[ti]][:, 0:b - a],
                                                          mul=1.0 / DCV), reads=[bank_res[bm[ti]]], writes=[ln_res])
                K.V(lambda ti=ti, a=a, b=b: nc.vector.tensor_scalar(out=RSTD[:, a:b], in0=banks[bq[ti]][:, 0:b - a],
                                                                    scalar1=1.0 / DCV, scalar2=None, op0=ALU.mult),
                    reads=[bank_res[bq[ti]]], writes=[ln_res])
            for b_ in bm + bq:
                bfree(b_)
            K.V(lambda: nc.vector.tensor_tensor(out=ACC[0], in0=MEAN, in1=MEAN, op=ALU.mult), reads=[ln_res],
                writes=[ACC_res[0]])
            K.V(lambda: nc.vector.tensor_tensor(out=RSTD, in0=RSTD, in1=ACC[0], op=ALU.subtract),
                reads=[ACC_res[0], ln_res], writes=[ln_res])
            K.A(lambda: nc.scalar.activation(out=RSTD, in_=RSTD, func=AF.Ln, bias=EPSC[:, 0:1], scale=1.0),
                reads=[ln_res, const_res], writes=[ln_res])
            K.A(lambda: nc.scalar.activation(out=RSTD, in_=RSTD, func=AF.Exp, scale=-0.5), reads=[ln_res],
                writes=[ln_res])
            for cc in range(8):
                acc = ACC[cc % 2]
                ar = ACC_res[cc % 2]
                K.V(lambda cc=cc, acc=acc: nc.vector.tensor_tensor(out=acc, in0=CO[:, cc, :], in1=MEAN,
                                                                   op=ALU.subtract),
                    reads=[CO_res[cc], ln_res], writes=[ar])
                K.V(lambda acc=acc: nc.vector.tensor_tensor(out=acc, in0=acc, in1=RSTD, op=ALU.mult),
                    reads=[ln_res], writes=[ar])
                K.A(lambda cc=cc, acc=acc: nc.scalar.activation(out=INT[:, cc, :], in_=acc, func=AF.Silu,
                                                                bias=CV[:, CV_LB + cc:CV_LB + cc + 1],
                                                                scale=CV[:, CV_LG + cc:CV_LG + cc + 1]),
                    reads=[ar, const_res], writes=[INT_res[cc]])
            bk0, bk1 = balloc(), balloc()
            for cc in range(8):
                bk = bk0 if cc < 4 else bk1
                K.mm(banks[bk][0:30 + TS, (cc % 4) * 128:(cc % 4 + 1) * 128], VT32[:, cc, :], IDN, start=True,
                     stop=True, reads=[vt_res, const_res], out_res=bank_res[bk], sig=(cc % 4 == 3), transpose=True)
            K.A(lambda: nc.scalar.copy(out=TROW[0:30 + TS, 0:512], in_=banks[bk0][0:30 + TS, :]),
                reads=[bank_res[bk0]], writes=[trow_res])
            K.A(lambda: nc.scalar.copy(out=TROW[0:30 + TS, 512:1024], in_=banks[bk1][0:30 + TS, :]),
                reads=[bank_res[bk1]], writes=[trow_res])
            bfree(bk0)
            bfree(bk1)
            K.dma(SP, out_st, o_cmp[p], TROW[0:30, :], reads=[trow_res])
            K.dma(SP, out_st, o_cms[p][:, 29, :], TROW[30:30 + TS, :], reads=[trow_res])
            K.barrier()

            hoff[0] = mix_base
            XP = [halloc([3 + T], BF) for _ in range(2)]
            XT32 = halloc([5, 3 + TS], F32)
            XF = halloc([5, T], BF)
            YT = halloc([3, T], F32)
            ACX = [YT[:, 0, :], YT[:, 1, :]]
            ZT = halloc([3, T], BF)
            XTM = [halloc([512], BF) for _ in range(2)]
            HT = halloc([384], F32)
            HTB = halloc([384], BF)
            HIN = halloc([384], F32)
            HINB = halloc([384], BF)
            MTALL = halloc([4 * 384], BF)
            MT = [MTALL[:, i * 384:(i + 1) * 384].rearrange("p (a b) -> p a b", a=3) for i in range(4)]
            Y1ALL = halloc([768], F32)
            Y1 = [Y1ALL[:, 0:384], Y1ALL[:, 384:768]]
            XROW = Y1ALL[:, 0:640]
            SSTG = Y1ALL[:, 0:640]
            SCTG = halloc([5, TS * 3], F32)
            TM3 = halloc([TS, 3], F32)
            EXPD = MTALL.bitcast(F32).rearrange("p (a b) -> p a b", a=6)
            DTS = halloc([6, TS], F32)
            DX = halloc([3, TS], F32)
            YS = halloc([3, TS], F32)
            SL = [halloc([3, 128], F32) for _ in range(3)]
            XS32 = halloc([2, TS], F32)
            T2 = [halloc([3, 128], F32)] * 2
            SROW = T2[0]
            GSQ = [XP[0][:, 0:T], XP[1][:, 0:T]]
            ACC3 = halloc([5, 3], F32)
            HX = halloc([5, 3], F32)
            HP = halloc([5, 6], F32)
            XP_res = [Res("XP0"), Res("XP1")]
            xt_res = Res("xt32")
            XF_res = [Res("XF%d" % i) for i in range(5)]
            ZT_res = [Res("ZT%d" % i) for i in range(3)]
            YT_res = [Res("YT%d" % i) for i in range(3)]
            ACX_res = [YT_res[0], YT_res[1]]
            XTM_res = [Res("XTM0"), Res("XTM1")]
            ht_res = Res("HT")
            htb_res = Res("HTB")
            hin_res = Res("HIN")
            hinb_res = Res("HINB")
            MT_res = [Res("MT%d" % i) for i in range(4)]
            Y1_res = [Res("Y10"), Res("Y11")]
            sct_res = Res("sctg")
            tm3_res = Res("tm3")
            smp_res = Res("smp")
            ys_res = Res("ys")
            SL_res = [Res("SL0"), Res("SL1"), Res("SL2")]
            T2_res = [Res("T20")] * 2
            srow_res = T2_res[0]
            GSQ_res = XP_res
            a3_res = Res("acc3")
            hx_res = Res("hx")
            hp_res = Res("hp")
            xi = [0]
            ci = [0]
            print("H scratch free bytes (SSD):", OFF_X - hoff[0], "const free:", ARENA_B - coff[0])
            K.V(lambda: nc.vector.memset(HP, 0.0), writes=[hp_res])

            for g in range(NG):
                h0 = g * HPG
                chs = [3 * g, 3 * g + 1, 3 * g + 2, 24 + g, 32 + g]
                for (c0, w_, s0) in [(g * 384, 384, 0), (DSSM + g * 128, 128, 384), (DSSM + 1024 + g * 128, 128, 512)]:
                    K.dma(SP, misc_st, SSTG[0:TS * 3, s0:s0 + w_], st_xbc[p][:, c0:c0 + w_], writes=Y1_res)
                bk = balloc()
                for i in range(5):
                    K.mm(banks[bk][:, i * TS * 3:(i + 1) * TS * 3], SSTG[0:TS * 3, i * 128:(i + 1) * 128],
                         IDN[0:TS * 3, 0:TS * 3], start=True, stop=True, reads=Y1_res + [const_res],
                         out_res=bank_res[bk], sig=(i == 4), transpose=True)
                K.A(lambda bk=bk: nc.scalar.copy(out=SCTG.rearrange("p i x -> p (i x)"), in_=banks[bk][:, 0:5 * TS * 3]),
                    reads=[bank_res[bk]], writes=[sct_res])
                bfree(bk)

                def xbc_chunk(i, ch, pw, pres):
                    x = xi[0] % 2
                    xi[0] += 1
                    xp, xr = XP[x], XP_res[x]
                    for ti, (a, b) in enumerate(TT):
                        bk = balloc()
                        for k in range(KC):
                            K.mm(banks[bk][:, 0:b - a], pw[:, k, :], U[:, k, a:b], start=(k == 0), stop=(k == KC - 1),
                                 reads=[pres, U_res], out_res=bank_res[bk], sig=(k == KC - 1))
                        K.A(lambda bk=bk, a=a, b=b, xp=xp: nc.scalar.copy(out=xp[:, 3 + a:3 + b],
                                                                          in_=banks[bk][:, 0:b - a]),
                            reads=[bank_res[bk]], writes=[xr])
                        if ti == len(TT) - 1:
                            K.V(lambda bk=bk, a=a, i=i: nc.vector.tensor_copy(out=XT32[:, i, :],
                                                                              in_=banks[bk][:, TP - 3 - a:T - a]),
                                reads=[bank_res[bk]], writes=[xt_res])
                        bfree(bk)
                    K.V(lambda xp=xp: nc.vector.memset(xp[:, 0:3], 0.0), writes=[xr])
                    acc, ar = ACX[x], ACX_res[x]

                    def swc(j):
                        return CV[:, CV_SW + ch * 4 + j:CV_SW + ch * 4 + j + 1]
                    sbc = CV[:, CV_SB + ch:CV_SB + ch + 1]
                    K.V(lambda: nc.vector.tensor_scalar(out=acc[:, 0:TP], in0=xp[:, 0:TP], scalar1=swc(0), scalar2=sbc,
                                                        op0=ALU.mult, op1=ALU.add), reads=[xr, const_res], writes=[ar])
                    for j in range(1, 4):
                        K.V(lambda j=j: nc.vector.scalar_tensor_tensor(out=acc[:, 0:TP], in0=xp[:, j:j + TP],
                                                                       scalar=swc(j), in1=acc[:, 0:TP], op0=ALU.mult,
                                                                       op1=ALU.add), reads=[xr, const_res], writes=[ar])
                    K.V(lambda: nc.vector.tensor_tensor(
                        out=TM3, in0=SCTG[:, i, :].rearrange("p (b j) -> p b j", j=3),
                        in1=CV[:, CV_SW + ch * 4:CV_SW + ch * 4 + 3].unsqueeze(1).to_broadcast([128, TS, 3]),
                        op=ALU.mult), reads=[sct_res, const_res], writes=[tm3_res])
                    K.V(lambda: nc.vector.tensor_reduce(out=acc[:, TP:T], in_=TM3, axis=AX.X, op=ALU.add),
                        reads=[tm3_res], writes=[ar])
                    K.V(lambda: nc.vector.scalar_tensor_tensor(out=acc[:, TP:T], in0=xp[:, 3 + TP:3 + T], scalar=swc(3),
                                                               in1=acc[:, TP:T], op0=ALU.mult, op1=ALU.add),
                        reads=[xr, const_res], writes=[ar])
                    K.V(lambda: nc.vector.tensor_scalar(out=acc[:, TP:T], in0=acc[:, TP:T], scalar1=sbc, scalar2=None,
                                                        op0=ALU.add), reads=[const_res], writes=[ar])
                    K.V(lambda: nc.vector.tensor_copy(out=ACC3[:, i, :], in_=acc[:, 0:3]), reads=[ar], writes=[a3_res])
                    K.A(lambda: nc.scalar.activation(out=XF[:, i, :], in_=acc, func=AF.Silu), reads=[ar],
                        writes=[XF_res[i]])

                for i in range(3):
                    pi, pres, (pw,) = ws.get("xs")
                    xbc_chunk(i, chs[i], pw, pres)
                    ws.done(pi)
                pi, pres, (pwb, pwc) = ws.get("bc")
                xbc_chunk(3, chs[3], pwb, pres)
                xbc_chunk(4, chs[4], pwc, pres)
                ws.done(pi)
                exchange(XT32[:, :, 0:3], xt_res, 15, HX, hx_res)
                bk, bk2 = balloc(), balloc()
                for i in range(5):
                    bb_ = bk if i < 4 else bk2
                    K.mm(banks[bb_][0:3 + TS, (i % 4) * 128:(i % 4 + 1) * 128], XT32[:, i, :], IDN, start=True,
                         stop=True, reads=[xt_res, const_res], out_res=bank_res[bb_], sig=(i >= 3), transpose=True)
                K.A(lambda bk=bk: nc.scalar.copy(out=XROW[0:3 + TS, 0:512], in_=banks[bk][0:3 + TS, 0:512]),
                    reads=[bank_res[bk]], writes=Y1_res)
                K.A(lambda bk2=bk2: nc.scalar.copy(out=XROW[0:3 + TS, 512:640], in_=banks[bk2][0:3 + TS, 0:128]),
                    reads=[bank_res[bk2]], writes=Y1_res)
                bfree(bk)
                bfree(bk2)
                for (c0, w_, s0) in [(g * 384, 384, 0), (DSSM + g * 128, 128, 384), (DSSM + 1024 + g * 128, 128, 512)]:
                    K.dma(SP, out_st, o_xbp[p][:, c0:c0 + w_], XROW[0:3, s0:s0 + w_], reads=Y1_res)
                    K.dma(SP, out_st, o_xbs[p][:, 2, c0:c0 + w_], XROW[3:3 + TS, s0:s0 + w_], reads=Y1_res)
                for i in range(3):
                    def zevac(ti, a, b, bk, i=i):
                        K.A(lambda: nc.scalar.activation(out=ZT[:, i, a:b], in_=banks[bk][:, 0:b - a], func=AF.Silu),
                            reads=[bank_res[bk]], writes=[ZT_res[i]])
                    linear_fm("z", lambda k, a, b: U[:, k, a:b], lambda k: U_res, KC, zevac)
                K.V(lambda: nc.vector.tensor_scalar(out=HP[:, :, 0:3], in0=HX, scalar1=FLAG[:, 0:1], scalar2=None,
                                                    op0=ALU.mult), reads=[hx_res, const_res], writes=[hp_res])
                for i in range(5):
                    ch = chs[i]
                    for j in range(3):
                        K.V(lambda i=i, j=j, ch=ch: nc.vector.scalar_tensor_tensor(
                            out=ACC3[:, i, :], in0=HP[:, i, j:j + 3],
                            scalar=CV[:, CV_SW + ch * 4 + j:CV_SW + ch * 4 + j + 1], in1=ACC3[:, i, :], op0=ALU.mult,
                            op1=ALU.add), reads=[hp_res, const_res], writes=[a3_res])
                    K.A(lambda i=i: nc.scalar.activation(out=XF[:, i, 0:3], in_=ACC3[:, i, :], func=AF.Silu),
                        reads=[a3_res], writes=[XF_res[i]])
                K.V(lambda: nc.vector.memset(HT, 0.0), writes=[ht_res])
                K.V(lambda: nc.vector.memset(HTB, 0.0), writes=[htb_res])
                dsk = CV[:, CV_BC + 96 + h0:CV_BC + 96 + h0 + HPG].unsqueeze(2).to_broadcast([128, HPG, 64])
                def stage_a(c):
                    cs = slice(c * 128, (c + 1) * 128)
                    x = c % 2
                    bk = balloc()
                    bkv = banks[bk][:, 0:256].bitcast(BF)
                    for i in range(4):
                        K.mm(bkv[:, i * 128:(i + 1) * 128], XF[:, i, cs], IDB, start=True, stop=True,
                             reads=[XF_res[i], const_res], out_res=bank_res[bk], sig=(i == 3), transpose=True)
                    K.A(lambda bkv=bkv, x=x: nc.scalar.copy(out=XTM[x], in_=bkv), reads=[bank_res[bk]],
                        writes=[XTM_res[x]])
                    bfree(bk)
                    xs3 = XTM[x][:, 0:384].rearrange("p (r q) -> p r q", r=HPG)
                    bkc = balloc()
                    K.mm(banks[bkc][:, 0:128], XF[:, 3, cs], XF[:, 4, cs], start=True, stop=True,
                         reads=[XF_res[3], XF_res[4]], out_res=bank_res[bkc], sig=True)
                    K.V(lambda x=x, c=c, xs3=xs3: nc.vector.tensor_tensor(
                        out=XDT[x], in0=xs3, in1=DT[:, c, h0:h0 + HPG].unsqueeze(2).to_broadcast([128, HPG, 64]),
                        op=ALU.mult), reads=[XTM_res[x], dt_res], writes=[XDT_res[x]])
                    K.V(lambda x=x, c=c, xs3=xs3: nc.vector.tensor_tensor(
                        out=XDD[x], in0=xs3, in1=W2[:, c, h0:h0 + HPG].unsqueeze(2).to_broadcast([128, HPG, 64]),
                        op=ALU.mult), reads=[XTM_res[x], dt_res], writes=[XDD_res[x]])
                    bkd = [balloc(), balloc()]
                    for hh in range(2):
                        for j in range(3):
                            h = h0 + hh * 3 + j
                            K.mm(banks[bkd[hh]][:, j * 128:(j + 1) * 128],
                                 DTA[:, c, h:h + 1].to_broadcast([128, 128]), TRI, start=True, stop=False,
                                 reads=[dt_res, const_res], out_res=bank_res[bkd[hh]], sig=False)
                            K.mm(banks[bkd[hh]][:, j * 128:(j + 1) * 128], IDN, NEGM, start=False, stop=True,
                                 reads=[const_res], out_res=bank_res[bkd[hh]], sig=(j == 2))
                    for hh in range(2):
                        bk = bkd[hh]
                        e = E32[hh]
                        for j in range(3):
                            h = h0 + hh * 3 + j
                            K.A(lambda bk=bk, j=j, h=h, e=e, c=c: nc.scalar.activation(
                                out=e[:, j, :], in_=banks[bk][:, j * 128:(j + 1) * 128], func=AF.Exp,
                                bias=NACS[:, c, h:h + 1], scale=1.0),
                                reads=[bank_res[bk], dt_res], writes=[E32_res[hh]])
                        bfree(bk)
                        m = MT[2 * x + hh]
                        K.V(lambda e=e, m=m, bkc=bkc: nc.vector.tensor_tensor(
                            out=m, in0=e, in1=banks[bkc][:, 0:128].unsqueeze(1).to_broadcast([128, 3, 128]),
                            op=ALU.mult), reads=[E32_res[hh], bank_res[bkc]], writes=[MT_res[2 * x + hh]])
                    bfree(bkc)

                def stage_b(c):
                    cs = slice(c * 128, (c + 1) * 128)
                    x = c % 2
                    xs3 = XTM[x][:, 0:384].rearrange("p (r q) -> p r q", r=HPG)
                    bks = balloc()
                    K.mm(banks[bks][:, 0:384], XTM[x][:, 384:512], XDD[x].rearrange("p r q -> p (r q)"), start=True,
                         stop=True, reads=[XTM_res[x], XDD_res[x]], out_res=bank_res[bks], sig=True)
                    bky = balloc()
                    for r in range(HPG):
                        m = MT[2 * x + r // 3]
                        K.mm(banks[bky][:, r * 64:(r + 1) * 64], m[:, r % 3, :], XDT[x][:, r, :], start=True, stop=True,
                             reads=[MT_res[2 * x + r // 3], XDT_res[x]], out_res=bank_res[bky], sig=(r == HPG - 1))
                    bko = None
                    if c > 0:
                        bko = balloc()
                        K.mm(banks[bko][:, 0:384], XF[:, 4, cs], HTB, start=True, stop=True, reads=[XF_res[4], htb_res],
                             out_res=bank_res[bko], sig=True)
                    if c > 0:
                        K.V(lambda c=c: nc.vector.tensor_tensor(
                            out=HT.rearrange("p (r q) -> p r q", r=HPG), in0=HT.rearrange("p (r q) -> p r q", r=HPG),
                            in1=CD[:, c, h0:h0 + HPG].unsqueeze(2).to_broadcast([128, HPG, 64]), op=ALU.mult),
                            reads=[dt_res], writes=[ht_res])
                    K.V(lambda bks=bks: nc.vector.tensor_tensor(out=HT, in0=HT, in1=banks[bks][:, 0:384], op=ALU.add),
                        reads=[bank_res[bks]], writes=[ht_res])
                    bfree(bks)
                    if c < NCH - 1:
                        K.A(lambda: nc.scalar.copy(out=HTB, in_=HT), reads=[ht_res], writes=[htb_res])
                    if c > 0:
                        K.V(lambda bko=bko, x=x, c=c: nc.vector.tensor_tensor(
                            out=Y1[x].rearrange("p (r q) -> p r q", r=HPG),
                            in0=banks[bko][:, 0:384].rearrange("p (r q) -> p r q", r=HPG),
                            in1=EXA[:, c, h0:h0 + HPG].unsqueeze(2).to_broadcast([128, HPG, 64]), op=ALU.mult),
                            reads=[bank_res[bko], dt_res], writes=[Y1_res[x]])
                        bfree(bko)
                        K.V(lambda bky=bky, x=x: nc.vector.tensor_tensor(out=Y1[x], in0=Y1[x],
                                                                         in1=banks[bky][:, 0:384], op=ALU.add),
                            reads=[bank_res[bky]], writes=[Y1_res[x]])
                    else:
                        K.A(lambda bky=bky, x=x: nc.scalar.copy(out=Y1[x], in_=banks[bky][:, 0:384]),
                            reads=[bank_res[bky]], writes=[Y1_res[x]])
                    bfree(bky)

                def stage_b_tail(c):
                    cs = slice(c * 128, (c + 1) * 128)
                    x = c % 2
                    bk = balloc()
                    for i in range(3):
                        K.mm(banks[bk][:, i * 128:(i + 1) * 128], Y1[x][:, i * 128:(i + 1) * 128], IDN, start=True,
                             stop=True, reads=[Y1_res[x], const_res], out_res=bank_res[bk], sig=(i == 2),
                             transpose=True)
                    K.A(lambda bk=bk, cs=cs: nc.scalar.copy(out=YT[:, :, cs],
                                                            in_=banks[bk][:, 0:384].rearrange("p (i q) -> p i q", i=3)),
                        reads=[bank_res[bk]], writes=YT_res)
                    bfree(bk)

                stage_a(0)
                for c in range(NCH):
                    if c + 1 < NCH:
                        stage_a(c + 1)
                    if c > 0:
                        stage_b_tail(c - 1)
                    stage_b(c)
                stage_b_tail(NCH - 1)
                exchange(HT, ht_res, 384, HIN, hin_res)
                for i in range(3):
                    K.V(lambda i=i: nc.vector.scalar_tensor_tensor(
                        out=YT[:, i, 0:TP], in0=XF[:, i, 0:TP], scalar=CV[:, CV_DF + 3 * g + i:CV_DF + 3 * g + i + 1],
                        in1=YT[:, i, 0:TP], op0=ALU.mult, op1=ALU.add),
                        reads=[XF_res[i], const_res], writes=[YT_res[i]])
                K.V(lambda: nc.vector.tensor_copy(
                    out=EXPD[0:TS, 0:3, :].rearrange("p a (h q) -> p (a h) q", h=2),
                    in_=DT[0:TS, 8, h0:h0 + HPG].unsqueeze(2).to_broadcast([TS, HPG, 64])),
                    reads=[dt_res], writes=[smp_res] + MT_res)
                K.V(lambda: nc.vector.tensor_copy(
                    out=EXPD[0:TS, 3:6, :].rearrange("p a (h q) -> p (a h) q", h=2),
                    in_=DECS[0:TS, h0:h0 + HPG].unsqueeze(2).to_broadcast([TS, HPG, 64])),
                    reads=[dt_res], writes=[smp_res] + MT_res)
                bk = balloc()
                for j in range(6):
                    K.mm(banks[bk][:, j * TS:(j + 1) * TS], EXPD[0:TS, j, :], IDN[0:TS, 0:TS], start=True, stop=True,
                         reads=[smp_res, const_res] + MT_res, out_res=bank_res[bk], sig=(j == 5), transpose=True)
                K.A(lambda bk=bk: nc.scalar.copy(out=DTS, in_=banks[bk][:, 0:6 * TS].rearrange("p (j b) -> p j b", j=6)),
                    reads=[bank_res[bk]], writes=[smp_res])
                bfree(bk)
                K.V(lambda: nc.vector.tensor_tensor(out=DX, in0=DTS[:, 0:3, :], in1=XF[:, 0:3, TP:T], op=ALU.mult),
                    reads=[smp_res, XF_res[0], XF_res[1], XF_res[2]], writes=[smp_res])
                K.V(lambda: nc.vector.tensor_copy(out=XS32, in_=XF[:, 3:5, TP:T]), reads=[XF_res[3], XF_res[4]],
                    writes=[smp_res])
                def corr_gen():
                    K.V(lambda: nc.vector.tensor_scalar(out=HIN, in0=HIN, scalar1=FLAG[:, 0:1], scalar2=None, op0=ALU.mult),
                        reads=[const_res], writes=[hin_res])
                    K.A(lambda: nc.scalar.copy(out=HINB, in_=HIN), reads=[hin_res], writes=[hinb_res])
                    for c in range(NCH):
                        cs = slice(c * 128, (c + 1) * 128)
                        x = c % 2
                        bko = balloc()
                        K.mm(banks[bko][:, 0:384], XF[:, 4, cs], HINB, start=True, stop=True, reads=[XF_res[4], hinb_res],
                             out_res=bank_res[bko], sig=True)
                        K.V(lambda bko=bko, x=x, c=c: nc.vector.tensor_tensor(
                            out=Y1[x].rearrange("p (r q) -> p r q", r=HPG),
                            in0=banks[bko][:, 0:384].rearrange("p (r q) -> p r q", r=HPG),
                            in1=ETOT[:, c, h0:h0 + HPG].unsqueeze(2).to_broadcast([128, HPG, 64]), op=ALU.mult),
                            reads=[bank_res[bko], dt_res], writes=[Y1_res[x]])
                        bfree(bko)
                        bk = balloc()
                        for i in range(3):
                            K.mm(banks[bk][:, i * 128:(i + 1) * 128], Y1[x][:, i * 128:(i + 1) * 128], IDN, start=True,
                                 stop=True, reads=[Y1_res[x], const_res], out_res=bank_res[bk], sig=(i == 2),
                                 transpose=True)
                        K.V(lambda bk=bk, cs=cs: nc.vector.tensor_tensor(
                            out=YT[:, :, cs], in0=YT[:, :, cs], in1=banks[bk][:, 0:384].rearrange("p (i q) -> p i q", i=3),
                            op=ALU.add), reads=[bank_res[bk]], writes=YT_res)
                        bfree(bk)
                        yield
                cg = corr_gen()
                def slab_load(bb):
                    K.dma(SP, slab_st[bb % 3], SL[bb % 3],
                          st_ssm[p][bb, g * 384:(g + 1) * 384, :].rearrange("(i q) n -> q i n", q=128),
                          writes=[SL_res[bb % 3]])
                for bb in range(3):
                    slab_load(bb)

                def bc_prep(bb):
                    bk_ = balloc()
                    for w_ in range(2):
                        K.mm(banks[bk_][:, w_ * 128:(w_ + 1) * 128],
                             XS32[:, w_, bb:bb + 1].to_broadcast([128, 128]), IDN, start=True, stop=True,
                             reads=[smp_res, const_res], out_res=bank_res[bk_], sig=(w_ == 1))
                    return bk_
                nxt_bk = bc_prep(0)
                for b_ in range(TS):
                    x = b_ % 2
                    x3 = b_ % 3
                    sl, slr = SL[x3], SL_res[x3]
                    t2, t2r = T2[0], T2_res[0]
                    bk = nxt_bk
                    if b_ + 1 < TS:
                        nxt_bk = bc_prep(b_ + 1)
                    for blk in range(3):
                        K.A(lambda bk=bk, b_=b_, blk=blk: nc.scalar.activation(
                            out=t2[:, blk, :], in_=banks[bk][:, 0:128], func=AF.Copy, scale=DX[:, blk, b_:b_ + 1]),
                            reads=[bank_res[bk], smp_res], writes=[t2r])
                    for blk in range(3):
                        K.V(lambda b_=b_, sl=sl, blk=blk: nc.vector.scalar_tensor_tensor(
                            out=sl[:, blk, :], in0=sl[:, blk, :], scalar=DTS[:, 3 + blk, b_:b_ + 1], in1=t2[:, blk, :],
                            op0=ALU.mult, op1=ALU.add), reads=[smp_res, t2r], writes=[slr])
                    K.dma(SP, sout_st, o_sss[p][b_, g * 384:(g + 1) * 384, :].rearrange("(i q) n -> q i n", q=128), sl,
                          reads=[slr])
                    for blk in range(3):
                        K.V(lambda bk=bk, b_=b_, sl=sl, blk=blk: nc.vector.scalar_tensor_tensor(
                            out=Y1[x][:, blk * 128:(blk + 1) * 128], in0=sl[:, blk, :], scalar=1.0,
                            in1=banks[bk][:, 128:256], op0=ALU.mult, op1=ALU.mult, accum_out=YS[:, blk, b_:b_ + 1]),
                            reads=[slr, bank_res[bk]], writes=[Y1_res[x], ys_res])
                    bfree(bk)
                    if b_ + 3 < TS:
                        slab_load(b_ + 3)
                    if b_ >= 4:
                        next(cg, None)
                for i in range(3):
                    K.V(lambda i=i: nc.vector.scalar_tensor_tensor(
                        out=YT[:, i, TP:T], in0=XF[:, i, TP:T], scalar=CV[:, CV_DF + 3 * g + i:CV_DF + 3 * g + i + 1],
                        in1=YS[:, i, :], op0=ALU.mult, op1=ALU.add),
                        reads=[XF_res[i], ys_res, const_res], writes=[YT_res[i]])
                for _ in cg:
                    pass
                K.V(lambda: nc.vector.tensor_tensor(
                    out=HIN.rearrange("p (r q) -> p r q", r=HPG), in0=HIN.rearrange("p (r q) -> p r q", r=HPG),
                    in1=CDTOT[:, h0:h0 + HPG].unsqueeze(2).to_broadcast([128, HPG, 64]), op=ALU.mult),
                    reads=[dt_res, hinb_res], writes=[hin_res])
                K.V(lambda: nc.vector.tensor_tensor(out=HT, in0=HT, in1=HIN, op=ALU.add), reads=[hin_res],
                    writes=[ht_res])
                bk = balloc()
                for i in range(3):
                    K.mm(banks[bk][:, i * 128:(i + 1) * 128], HT[:, i * 128:(i + 1) * 128], IDN, start=True, stop=True,
                         reads=[ht_res, const_res], out_res=bank_res[bk], sig=(i == 2), transpose=True)
                K.A(lambda bk=bk: nc.scalar.copy(out=SROW, in_=banks[bk][:, 0:384].rearrange("p (i q) -> p i q", i=3)),
                    reads=[bank_res[bk]], writes=[srow_res])
                bfree(bk)
                K.dma(SP, out_st, o_ssp[p][g * 384:(g + 1) * 384, :].rearrange("(i q) n -> q i n", q=128), SROW,
                      reads=[srow_res])
                for i in range(3):
                    K.V(lambda i=i: nc.vector.tensor_tensor(out=YT[:, i, :], in0=YT[:, i, :], in1=ZT[:, i, :],
                                                            op=ALU.mult), reads=[ZT_res[i]], writes=[YT_res[i]])
                bks = [balloc() for _ in TT]
                for i in range(3):
                    x = i % 2
                    K.A(lambda i=i, x=x: nc.scalar.activation(out=GSQ[x], in_=YT[:, i, :], func=AF.Square),
                        reads=[YT_res[i]], writes=[GSQ_res[x]])
                    for ti, (a, b) in enumerate(TT):
                        K.mm(banks[bks[ti]][:, 0:b - a], ONB, GSQ[x][:, a:b], start=(i == 0), stop=(i == 2),
                             reads=[GSQ_res[x], const_res], out_res=bank_res[bks[ti]], sig=(ti == len(TT) - 1))
                for ti, (a, b) in enumerate(TT):
                    K.A(lambda ti=ti, a=a, b=b: nc.scalar.activation(out=RS[:, a:b], in_=banks[bks[ti]][:, 0:b - a],
                                                                    func=AF.Ln, bias=EPSC[:, 0:1], scale=1.0 / 384),
                        reads=[bank_res[bks[ti]], const_res], writes=[RS_res])
                K.A(lambda: nc.scalar.activation(out=RS, in_=RS, func=AF.Exp, scale=-0.5), reads=[RS_res],
                    writes=[RS_res])
                for b_ in bks:
                    bfree(b_)
                for i in range(3):
                    K.V(lambda i=i: nc.vector.scalar_tensor_tensor(
                        out=INT[:, 8 + 3 * g + i, :], in0=YT[:, i, :],
                        scalar=CV[:, CV_NG + 3 * g + i:CV_NG + 3 * g + i + 1], in1=RS, op0=ALU.mult, op1=ALU.mult),
                        reads=[YT_res[i], RS_res, const_res], writes=[INT_res[8 + 3 * g + i]])
            K.barrier()
            for d in range(KC):
                def oevac(ti, a, b, bk, d=d):
                    K.A(lambda: nc.scalar.copy(out=H[:, d, a:b], in_=banks[bk][:, 0:b - a]), reads=[bank_res[bk]],
                        writes=[H_res[d]])
                linear_fm("out", lambda k, a, b: INT[:, k, a:b], lambda k: INT_res[k], 32, oevac)
            K.barrier()
            post_residual(lambda k: CV[:, CV_G + 3 * 16 + k:CV_G + 3 * 16 + k + 1])

        def ple(p):
            prenorm(6)
            spill_H()
            K.barrier()
            PT = view(OFF_X, [2, T], BF)
            pt_res = Res("PT")
            K.dma(POOL, pt_st, PT, pT[p].rearrange("k q t -> q k t"), writes=[pt_res])
            si = [0]
            for d in range(KC):
                pi, pres, (pg, pp) = ws.get("ple")
                for ti, (a, b) in enumerate(TT):
                    n = b - a
                    bg, bp = balloc(), balloc()
                    for k in range(KC):
                        K.mm(banks[bg][:, 0:n], pg[:, k, :], U[:, k, a:b], start=(k == 0), stop=(k == KC - 1),
                             reads=[pres, U_res], out_res=bank_res[bg], sig=False)
                    for k in range(2):
                        K.mm(banks[bp][:, 0:n], pp[:, k, :], PT[:, k, a:b], start=(k == 0), stop=(k == 1),
                             reads=[pres, pt_res], out_res=bank_res[bp], sig=(k == 1))
                    s = si[0] % 2
                    si[0] += 1
                    K.A(lambda bg=bg, n=n, s=s: nc.scalar.activation(out=SS[s][:, 0:n], in_=banks[bg][:, 0:n],
                                                                     func=AF.Sigmoid),
                        reads=[bank_res[bg]], writes=[SS_res[s]])
                    K.V(lambda bp=bp, n=n, s=s, d=d, a=a, b=b: nc.vector.tensor_tensor(
                        out=H[:, d, a:b], in0=SS[s][:, 0:n], in1=banks[bp][:, 0:n], op=ALU.mult),
                        reads=[SS_res[s], bank_res[bp]], writes=[H_res[d]])
                    bfree(bg)
                    bfree(bp)
                ws.done(pi)
            K.barrier()
            post_residual(lambda k: CV[:, CV_G + 7 * 16 + k:CV_G + 7 * 16 + k + 1],
                          after_k=lambda k: K.dma(ACT, yout_st, yT[p, k], H[:, k, :], reads=[H_res[k]]))

        for p in range(NP):
            for k in range(KC):
                K.dma(SP, ld_st, H[:, k, :], xT[p, k], writes=[H_res[k]])
            ffn("w_ffn1", 0, 0)
            if stage >= 2:
                mixer(p)
            if stage >= 3:
                ffn("w_ffn2", 4, 1)
            if stage >= 4:
                ple(p)
            K.barrier()
            if stage < 4:
                for k in range(KC):
                    K.dma(SP, out_st, yT[p, k], H[:, k, :], reads=[H_res[k]])
            K.barrier()
        SP.raw.wait_ge(out_st.sem, 16 * out_st.n)
    return nc


def _cvec(inp):
    cv = np.zeros((128, CV_N), np.float32)
    names = ["norm_ffn1_pre", "norm_ffn1_post", "norm_mix_pre", "norm_mix_post",
             "norm_ffn2_pre", "norm_ffn2_post", "norm_ple_pre", "norm_ple_post"]
    for i, n in enumerate(names):
        cv[:, CV_G + i * 16:CV_G + (i + 1) * 16] = inp[n].reshape(16, 128).T
    cw = inp["conv_mod_w"].reshape(31, 8, 128)
    cv[:, CV_CW:CV_CW + 248] = cw.transpose(2, 1, 0).reshape(128, 248)
    cv[:, CV_CB:CV_CB + 8] = inp["conv_mod_b"].reshape(8, 128).T
    cv[:, CV_LG:CV_LG + 8] = inp["conv_mod_ln_g"].reshape(8, 128).T
    cv[:, CV_LB:CV_LB + 8] = inp["conv_mod_ln_b"].reshape(8, 128).T
    sw = inp["ssm_conv_w"].reshape(4, 40, 128)
    cv[:, CV_SW:CV_SW + 160] = sw.transpose(2, 1, 0).reshape(128, 160)
    cv[:, CV_SB:CV_SB + 40] = inp["ssm_conv_b"].reshape(40, 128).T
    cv[:, CV_NG:CV_NG + 24] = inp["ssm_norm_g"].reshape(24, 128).T
    cv[:, CV_BC:CV_BC + 48] = inp["dt_bias"].reshape(1, 48)
    cv[:, CV_BC + 48:CV_BC + 96] = inp["a_log"].reshape(1, 48)
    cv[:, CV_BC + 96:CV_BC + 144] = inp["d_skip"].reshape(1, 48)
    cv[:, CV_DF:CV_DF + 24] = np.repeat(inp["d_skip"].reshape(48), 64).reshape(24, 128).T
    return cv


def run(inputs, stage=99, NP=1, TS=16):
    inp = {k: np.ascontiguousarray(np.asarray(v, dtype=np.float32)) for k, v in inputs.items()}
    T = TP + TS
    xp = inp["x_prompt"]
    xs = inp["x_sample"].reshape(128, D)
    pp = inp["p_prompt"].reshape(4, 2048, 256)
    psm = inp["p_sample"].reshape(128, 256)
    shared = {"cvec": _cvec(inp), "ident": np.eye(128, dtype=np.float32),
              "tri": np.triu(np.ones((128, 128), np.float32))}
    for nm in ["w_ffn1_gate", "w_ffn1_up", "w_ffn1_down", "w_in", "w_out", "w_ffn2_gate", "w_ffn2_up",
               "w_ffn2_down", "w_ple_gate", "w_ple_proj"]:
        shared[nm] = inp[nm][0]
    in_maps = []
    for c in range(8):
        s = c // 2
        hf = c % 2
        m = dict(shared)
        m["flag"] = np.full((128, 1), float(hf), np.float32)
        m["negm"] = ((np.triu(np.ones((128, 128), np.float32)) - 1.0) * 30000.0).astype(np.float32)
        xTc = np.zeros((NP, KC, 128, T), np.float32)
        pTc = np.zeros((NP, 2, 128, T), np.float32)
        stc = np.zeros((NP, TS, 30, DCV), np.float32)
        stx = np.zeros((NP, TS * 3, DXBC), np.float32)
        sts = np.zeros((NP, TS, DSSM, DST), np.float32)
        for p in range(NP):
            b0 = (c * NP + p) * TS
            xt = np.concatenate([xp[s, hf * TP:(hf + 1) * TP], xs[b0:b0 + TS]], 0)
            xTc[p] = xt.T.reshape(KC, 128, T)
            pt = np.concatenate([pp[s, hf * TP:(hf + 1) * TP], psm[b0:b0 + TS]], 0)
            pTc[p] = pt.T.reshape(2, 128, T)
            stc[p] = inp["state_conv_mod"][0, b0:b0 + TS]
            stx[p] = inp["state_ssm_conv"][0, b0:b0 + TS].reshape(TS * 3, DXBC)
            sts[p] = inp["state_ssm"][0, b0:b0 + TS].reshape(TS, DSSM, DST)
        m.update({"xT": xTc, "pT": pTc, "st_conv": stc, "st_xbc": stx, "st_ssm": sts})
        in_maps.append(m)
    nc = build_program(NP, TS, stage)
    res = run_bass_kernel_spmd(nc, in_maps, core_ids=list(range(8)))
    R = res.results
    y_prompt = np.zeros((4, 2048, D), np.float32)
    y_sample = np.zeros((128, 1, D), np.float32)
    cmp_ = np.zeros((1, 4, 30, DCV), np.float32)
    xbp = np.zeros((1, 4, 3, DXBC), np.float32)
    ssp = np.zeros((1, 4, NH, 64, DST), np.float32)
    cms = np.zeros((1, 128, 30, DCV), np.float32)
    xbs = np.zeros((1, 128, 3, DXBC), np.float32)
    sss = np.zeros((1, 128, NH, 64, DST), np.float32)
    for c in range(8):
        r = R[c]
        s = c // 2
        for p in range(NP):
            b0 = (c * NP + p) * TS
            yt = r["yT"][p].reshape(D, T).T
            y_prompt[s, (c % 2) * TP:(c % 2 + 1) * TP] = yt[:TP]
            y_sample[b0:b0 + TS, 0] = yt[TP:]
            cms[0, b0:b0 + TS] = r["o_cms"][p]
            xbs[0, b0:b0 + TS] = r["o_xbs"][p]
            sss[0, b0:b0 + TS] = r["o_sss"][p].reshape(TS, NH, 64, DST)
        if c % 2 == 1:
            cmp_[0, s] = r["o_cmp"][NP - 1]
            xbp[0, s] = r["o_xbp"][NP - 1]
            ssp[0, s] = r["o_ssp"][NP - 1].reshape(NH, 64, DST)
    return (y_prompt, y_sample, cmp_, xbp, ssp, cms, xbs, sss)


def kernel(**inputs):
    return run(inputs)
```
